# Optimizing a Trainium2 kernel written in Bass

```python
import jax, jax.numpy as jnp
from jax import lax
import numpy as np

D_MODEL = 1024
BATCH = 8
SEQ = 8192
DEPTH = 4

P_DIM = 256
EPS = 1e-6
D_FF = 1408
MLA_HEADS = 4
MLA_Q_RANK = 256
MLA_KV_RANK = 128
MLA_NOPE = 128
MLA_ROPE = 64
MLA_V = 128
ROPE_THETA = 10000.0
GLA_HEADS = 4
GLA_DK = 64
GLA_DV = 128
GLA_GATE_RANK = 16
GLA_GATE_TAU = 16.0
GLA_CHUNK = 16
SB_HEADS = 8
SB_HEAD_DIM = D_MODEL // SB_HEADS
Q_BLOCK = 128

EVEN_SPLITS = (MLA_Q_RANK, MLA_KV_RANK, MLA_ROPE,
               GLA_HEADS * GLA_DK, GLA_HEADS * GLA_DK, GLA_HEADS * GLA_DV,
               GLA_GATE_RANK, GLA_HEADS * GLA_DV)
EVEN_IN = sum(EVEN_SPLITS)
EVEN_SPLIT_IDX = tuple(int(v) for v in np.cumsum(EVEN_SPLITS)[:-1])
EVEN_OUT = MLA_HEADS * MLA_V + GLA_HEADS * GLA_DV
N_EVEN = (DEPTH + 1) // 2
N_ODD = DEPTH // 2

kernel_name = "hybrid_mla_gla_stickbreaking_macaron_ple"


def rms_norm(x, w):
    xf = x.astype(jnp.float32)
    y = xf * lax.rsqrt(jnp.mean(xf * xf, axis=-1, keepdims=True) + EPS)
    return (y * w.astype(jnp.float32)).astype(x.dtype)


def swiglu_ffn(h, w_gu, w_down):
    a, b = jnp.split(h @ w_gu, 2, axis=-1)
    return (jax.nn.silu(a) * b) @ w_down


def rope_tables(positions, dim):
    half = dim // 2
    inv_freq = 1.0 / (ROPE_THETA ** (jnp.arange(half, dtype=jnp.float32) / half))
    ang = positions.astype(jnp.float32)[..., None] * inv_freq
    return jnp.cos(ang), jnp.sin(ang)


def apply_rope(t, cos, sin):
    t1, t2 = jnp.split(t.astype(jnp.float32), 2, axis=-1)
    return jnp.concatenate([t1 * cos - t2 * sin, t1 * sin + t2 * cos], axis=-1).astype(t.dtype)


def mla_attention(q_nope, q_rope, k_nope, k_rope, v):
    B, H, S, _ = q_nope.shape
    scale = (MLA_NOPE + MLA_ROPE) ** -0.5
    outs = []
    for blk in range(S // Q_BLOCK):
        lo, hi = blk * Q_BLOCK, (blk + 1) * Q_BLOCK
        s = (jnp.einsum("bhqd,bhkd->bhqk", q_nope[:, :, lo:hi], k_nope[:, :, :hi])
             + jnp.einsum("bhqr,bkr->bhqk", q_rope[:, :, lo:hi], k_rope[:, :hi])).astype(jnp.float32) * scale
        causal = jnp.arange(hi)[None, :] <= (lo + jnp.arange(Q_BLOCK))[:, None]
        w = jax.nn.softmax(jnp.where(causal, s, -jnp.inf), axis=-1)
        outs.append(jnp.einsum("bhqk,bhkd->bhqd", w.astype(v.dtype), v[:, :, :hi]))
    o = jnp.concatenate(outs, axis=2)
    return o.transpose(0, 2, 1, 3).reshape(B, S, H * v.shape[-1])


def gla_chunked(q, k, v, log_a):
    B, H, S, DK = q.shape
    DV = v.shape[-1]
    nc = S // GLA_CHUNK

    def to_chunks(t):
        return jnp.moveaxis(t.astype(jnp.float32).reshape(B, H, nc, GLA_CHUNK, t.shape[-1]), 2, 0)

    qc = to_chunks(q) * (DK ** -0.5)
    kc, vc, gc = to_chunks(k), to_chunks(v), to_chunks(log_a)
    mask = jnp.tril(jnp.ones((GLA_CHUNK, GLA_CHUNK), dtype=bool))

    def step(state, inp):
        qb, kb, vb, gb = inp
        b = jnp.cumsum(gb, axis=-2)
        diff = b[:, :, :, None, :] - b[:, :, None, :, :]
        decay = jnp.exp(jnp.where(mask[:, :, None], diff, -jnp.inf))
        att = jnp.einsum("bhijd,bhjd->bhij", qb[:, :, :, None, :] * decay, kb)
        o = att @ vb + jnp.einsum("bhid,bhde->bhie", qb * jnp.exp(b), state)
        b_last = b[:, :, -1:, :]
        state = (jnp.exp(b_last[:, :, 0, :])[..., None] * state
                 + jnp.einsum("bhjd,bhje->bhde", kb * jnp.exp(b_last - b), vb))
        return state, o

    state0 = jnp.zeros((B, H, DK, DV), jnp.float32)
    _, out = lax.scan(step, state0, (qc, kc, vc, gc))
    return jnp.moveaxis(out, 0, 2).reshape(B, H, S, DV)


def stick_breaking_attention(q, k, v):
    B, H, S, d = q.shape
    q = q * (SB_HEAD_DIM ** -0.5)
    r = jnp.arange(Q_BLOCK)
    upper_incl = (r[None, :] >= r[:, None]).astype(jnp.float32)
    outs = []
    for blk in range(S // Q_BLOCK):
        lo, hi = blk * Q_BLOCK, (blk + 1) * Q_BLOCK
        nck = blk + 1
        z = jnp.einsum("bhqd,bhkd->bhqk", q[:, :, lo:hi], k[:, :, :hi]).astype(jnp.float32)
        causal = jnp.arange(hi)[None, :] < (lo + r)[:, None]
        l = jnp.where(causal, jax.nn.log_sigmoid(-z), 0.0).reshape(B, H, Q_BLOCK, nck, Q_BLOCK)
        rc = jnp.einsum("bhqck,jk->bhqcj", l, upper_incl)
        c = jnp.arange(nck)
        strict_later = (c[None, :] > c[:, None]).astype(jnp.float32)
        suffix = jnp.einsum("bhqe,ce->bhqc", l.sum(axis=-1), strict_later)
        log_w = z + (rc + suffix[..., None]).reshape(B, H, Q_BLOCK, hi)
        a = jnp.where(causal, jnp.exp(log_w), 0.0)
        outs.append(jnp.einsum("bhqk,bhkd->bhqd", a.astype(v.dtype), v[:, :, :hi]))
    o = jnp.concatenate(outs, axis=2)
    return o.transpose(0, 2, 1, 3).reshape(B, S, H * d)


def mla_gla_mixer(h, cos, sin, w_in, q_norm, kv_norm, w_uq, w_ukv, w_gate2, b_gate, gla_norm, w_out):
    B, S, _ = h.shape
    c_q, c_kv, k_r, g_q, g_k, g_v, g_a, g_r = jnp.split(h @ w_in, EVEN_SPLIT_IDX, axis=-1)
    q = (rms_norm(c_q, q_norm) @ w_uq).reshape(B, S, MLA_HEADS, MLA_NOPE + MLA_ROPE)
    q_nope, q_rope = q[..., :MLA_NOPE], q[..., MLA_NOPE:]
    q_rope = apply_rope(q_rope, cos[:, :, None, :], sin[:, :, None, :])
    kv = (rms_norm(c_kv, kv_norm) @ w_ukv).reshape(B, S, MLA_HEADS, MLA_NOPE + MLA_V)
    k_nope, v_mla = kv[..., :MLA_NOPE], kv[..., MLA_NOPE:]
    k_rope = apply_rope(k_r, cos, sin)
    o_mla = mla_attention(q_nope.transpose(0, 2, 1, 3), q_rope.transpose(0, 2, 1, 3),
                          k_nope.transpose(0, 2, 1, 3), k_rope, v_mla.transpose(0, 2, 1, 3))
    log_a = jax.nn.log_sigmoid((g_a @ w_gate2 + b_gate).astype(jnp.float32)) / GLA_GATE_TAU
    heads_k = lambda t: t.reshape(B, S, GLA_HEADS, -1).transpose(0, 2, 1, 3)
    o_gla = gla_chunked(heads_k(g_q), heads_k(g_k), heads_k(g_v), heads_k(log_a))
    o_gla = rms_norm(o_gla.transpose(0, 2, 1, 3).astype(h.dtype), gla_norm)
    o_gla = (o_gla * jax.nn.silu(g_r.reshape(B, S, GLA_HEADS, GLA_DV))).reshape(B, S, GLA_HEADS * GLA_DV)
    return jnp.concatenate([o_mla.astype(h.dtype), o_gla], axis=-1) @ w_out


def stick_breaking_mixer(h, w_qkv, w_out):
    B, S, _ = h.shape
    qkv = (h @ w_qkv).reshape(B, S, 3, SB_HEADS, SB_HEAD_DIM).transpose(2, 0, 3, 1, 4)
    o = stick_breaking_attention(qkv[0], qkv[1], qkv[2])
    return o.astype(h.dtype) @ w_out


def setup_inputs(seed: int = 0) -> dict:
    key = jax.random.key(seed)
    ks = jax.random.split(key, 32)

    def dense(k, shape, fan_in):
        return jax.random.normal(k, shape, jnp.float32) * (fan_in ** -0.5)

    def gain(k, shape):
        return 1.0 + 0.02 * jax.random.normal(k, shape, jnp.float32)

    positions = (jnp.arange(SEQ, dtype=jnp.int32)[None, :]
                 + jax.random.randint(ks[2], (BATCH, 1), 0, 4096, dtype=jnp.int32))
    return {
        "x": jax.random.normal(ks[0], (BATCH, SEQ, D_MODEL), jnp.float32),
        "p": jax.random.normal(ks[1], (DEPTH, BATCH, SEQ, P_DIM), jnp.float32),
        "positions": positions,
        "ffn1_norm": gain(ks[3], (DEPTH, D_MODEL)),
        "ffn1_w_gu": dense(ks[4], (DEPTH, D_MODEL, 2 * D_FF), D_MODEL),
        "ffn1_w_down": dense(ks[5], (DEPTH, D_FF, D_MODEL), D_FF),
        "mix_norm": gain(ks[6], (DEPTH, D_MODEL)),
        "ffn2_norm": gain(ks[7], (DEPTH, D_MODEL)),
        "ffn2_w_gu": dense(ks[8], (DEPTH, D_MODEL, 2 * D_FF), D_MODEL),
        "ffn2_w_down": dense(ks[9], (DEPTH, D_FF, D_MODEL), D_FF),
        "ple_norm": gain(ks[10], (DEPTH, D_MODEL)),
        "ple_w_gate": dense(ks[11], (DEPTH, D_MODEL, D_MODEL), D_MODEL),
        "ple_w_proj": dense(ks[12], (DEPTH, P_DIM, D_MODEL), P_DIM),
        "ev_w_in": dense(ks[13], (N_EVEN, D_MODEL, EVEN_IN), D_MODEL),
        "ev_q_norm": gain(ks[14], (N_EVEN, MLA_Q_RANK)),
        "ev_kv_norm": gain(ks[15], (N_EVEN, MLA_KV_RANK)),
        "ev_w_uq": dense(ks[16], (N_EVEN, MLA_Q_RANK, MLA_HEADS * (MLA_NOPE + MLA_ROPE)), MLA_Q_RANK),
        "ev_w_ukv": dense(ks[17], (N_EVEN, MLA_KV_RANK, MLA_HEADS * (MLA_NOPE + MLA_V)), MLA_KV_RANK),
        "ev_w_gate2": dense(ks[18], (N_EVEN, GLA_GATE_RANK, GLA_HEADS * GLA_DK), GLA_GATE_RANK),
        "ev_b_gate": 0.01 * jax.random.normal(ks[19], (N_EVEN, GLA_HEADS * GLA_DK), jnp.float32),
        "ev_gla_norm": gain(ks[20], (N_EVEN, GLA_HEADS, GLA_DV)),
        "ev_w_out": dense(ks[21], (N_EVEN, EVEN_OUT, D_MODEL), EVEN_OUT),
        "od_w_qkv": dense(ks[22], (N_ODD, D_MODEL, 3 * D_MODEL), D_MODEL),
        "od_w_out": dense(ks[23], (N_ODD, D_MODEL, D_MODEL), D_MODEL),
        "final_norm": gain(ks[24], (D_MODEL,)),
    }


def reference(x, p, positions, ffn1_norm, ffn1_w_gu, ffn1_w_down, mix_norm, ffn2_norm, ffn2_w_gu,
              ffn2_w_down, ple_norm, ple_w_gate, ple_w_proj, ev_w_in, ev_q_norm, ev_kv_norm, ev_w_uq,
              ev_w_ukv, ev_w_gate2, ev_b_gate, ev_gla_norm, ev_w_out, od_w_qkv, od_w_out, final_norm):
    cos, sin = rope_tables(positions, MLA_ROPE)
    for i in range(DEPTH):
        x = x + 0.5 * swiglu_ffn(rms_norm(x, ffn1_norm[i]), ffn1_w_gu[i], ffn1_w_down[i])
        h = rms_norm(x, mix_norm[i])
        j = i // 2
        if i % 2 == 0:
            x = x + mla_gla_mixer(h, cos, sin, ev_w_in[j], ev_q_norm[j], ev_kv_norm[j], ev_w_uq[j],
                                  ev_w_ukv[j], ev_w_gate2[j], ev_b_gate[j], ev_gla_norm[j], ev_w_out[j])
        else:
            x = x + stick_breaking_mixer(h, od_w_qkv[j], od_w_out[j])
        x = x + 0.5 * swiglu_ffn(rms_norm(x, ffn2_norm[i]), ffn2_w_gu[i], ffn2_w_down[i])
        gate = jax.nn.sigmoid(rms_norm(x, ple_norm[i]) @ ple_w_gate[i])
        x = x + gate * (p[i] @ ple_w_proj[i])
    return rms_norm(x, final_norm)
```

```python
from contextlib import ExitStack
import numpy as np
import ml_dtypes
import concourse.bass as bass
import concourse.mybir as mybir
from concourse.bass_utils import run_bass_kernel_spmd

F32 = mybir.dt.float32
BF = mybir.dt.bfloat16
I32 = mybir.dt.int32
AF = mybir.ActivationFunctionType
ALU = mybir.AluOpType

D = 1024
DFF = 1408
PD = 256
EPS = 1e-6
NCORES = 8
ENGS = ["pe", "act", "dve", "pool", "sp"]


class DmaGroup:
    def __init__(self, sem):
        self.sem = sem
        self.cnt = 0


class Tile:
    def __init__(self, ap, name):
        self.ap = ap
        self.name = name
        self.w = None
        self.r = {}
        self.dg = None

    def v(self, dt, *shape):
        n = 1
        for s in shape:
            n *= s
        nb = n * (4 if dt in (F32, I32) else 2) // 2
        a = self.ap[:, 0:nb]
        if dt != BF:
            a = a.bitcast(dt)
        if len(shape) == 2:
            return a.rearrange("p (c n) -> p c n", c=shape[0])
        if len(shape) == 3:
            return a.rearrange("p (a c n) -> p a c n", a=shape[0], c=shape[1])
        return a


class Prog:
    def __init__(self, nc, stack):
        self.nc = nc
        self.stack = stack
        self.ops = {e: [] for e in ENGS}
        self.seen = {e: {} for e in ENGS}
        self.pending = {e: [] for e in ENGS}
        self.esem = {e: stack.enter_context(nc.semaphore("sem_" + e)) for e in ENGS}
        self.dgs = []
        self.nsem = 0
        self.top = 0
        self.big = None
        self.banks = []
        self.bank_i = 0
        self._start = {}
        self.free_dgs = []
        self.phase_dgs = []

    def init_mem(self, kb=206):
        nc = self.nc
        self.big = self.stack.enter_context(nc.sbuf_tensor("big", [128, kb * 512], BF))
        self.cap = kb * 512
        for i in range(8):
            t = self.stack.enter_context(nc.psum_tensor("ps%d" % i, [128, 512], F32))
            self.banks.append(Tile(t, "bank%d" % i))

    def alloc(self, name, nbytes):
        n = (nbytes + 63) // 64 * 32
        assert self.top + n <= self.cap, ("SBUF overflow", name, self.top, n, self.cap)
        t = Tile(self.big[:, self.top:self.top + n], name)
        self._start[id(t)] = self.top
        self.top += n
        return t

    def span(self, tiles, dt, *shape):
        n = sum(t.ap.shape[1] for t in tiles)
        st = self._start[id(tiles[0])]
        a = self.big[:, st:st + n]
        if dt != BF:
            a = a.bitcast(dt)
        if len(shape) == 2:
            return a.rearrange("p (c n) -> p c n", c=shape[0])
        return a

    def bank(self):
        b = self.banks[self.bank_i % len(self.banks)]
        self.bank_i += 1
        return b

    def new_dg(self, tiles):
        if self.free_dgs:
            dg = self.free_dgs.pop()
        else:
            sem = self.stack.enter_context(self.nc.semaphore("dsem%d" % self.nsem))
            self.nsem += 1
            dg = DmaGroup(sem)
            self.dgs.append(dg)
        self.phase_dgs.append(dg)
        for t in tiles:
            t.dg = dg
        return dg

    def recycle(self):
        self.free_dgs.extend(self.phase_dgs)
        self.phase_dgs = []

    def _add(self, eng, fn, deps, dma=None):
        waits = []
        deps = list(deps) + self.pending[eng]
        self.pending[eng] = []
        for d in deps:
            if d is None:
                continue
            if d[0] == "e":
                _, e2, i2 = d
                if e2 == eng and eng in ("pe", "sp"):
                    continue
                if self.seen[eng].get(e2, -1) >= i2:
                    continue
                self.seen[eng][e2] = i2
                waits.append(d)
            else:
                _, dg, cnt = d
                if self.seen[eng].get(id(dg), -1) >= cnt:
                    continue
                self.seen[eng][id(dg)] = cnt
                waits.append(d)
        idx = len(self.ops[eng])
        self.ops[eng].append({"fn": fn, "waits": waits, "inc": False, "dma": dma})
        return ("e", eng, idx)

    def op(self, eng, fn, reads=(), writes=()):
        deps = []
        for t in reads:
            deps.append(t.w)
        for t in writes:
            deps.append(t.w)
            deps.extend(t.r.values())
        me = self._add(eng, fn, deps)
        for t in reads:
            t.r[eng] = me
        for t in writes:
            t.w = me
            t.r = {}
        return me

    def load(self, q, tiles, out_ap, in_ap, **kw):
        dg = tiles[0].dg
        assert dg is not None and all(t.dg is dg for t in tiles)
        deps = []
        for t in tiles:
            if not (t.w is not None and t.w[0] == "d" and t.w[1] is dg):
                deps.append(t.w)
            deps.extend(t.r.values())
        dg.cnt += 16
        self._add(q, lambda e: e.dma_start(out=out_ap, in_=in_ap, **kw), deps, dma=dg)
        me = ("d", dg, dg.cnt)
        for t in tiles:
            t.w = me
            t.r = {}

    def store(self, q, tiles, out_ap, in_ap, **kw):
        dg = tiles[0].dg
        assert dg is not None and all(t.dg is dg for t in tiles)
        deps = [t.w for t in tiles]
        dg.cnt += 16
        self._add(q, lambda e: e.dma_start(out=out_ap, in_=in_ap, **kw), deps, dma=dg)
        me = ("d", dg, dg.cnt)
        for t in tiles:
            t.r["dma"] = me

    def barrier(self):
        deps = [("d", dg, dg.cnt) for dg in self.dgs if dg.cnt > 0]
        for e in ENGS:
            if e != "sp":
                for i in range(len(self.ops[e]) - 1, -1, -1):
                    if self.ops[e][i]["dma"] is None:
                        deps.append(("e", e, i))
                        break
        me = self._add("sp", lambda e: e.nop(), deps)
        for e in ENGS:
            if e != "sp":
                self.pending[e].append(me)

    def emit(self):
        nc = self.nc
        for e in ENGS:
            for o in self.ops[e]:
                for d in o["waits"]:
                    if d[0] == "e":
                        self.ops[d[1]][d[2]]["inc"] = True
        for e in ENGS:
            c = 0
            for o in self.ops[e]:
                if o["inc"]:
                    assert o["dma"] is None
                    c += 1
                o["cnt"] = c

        def replay(name, eng):
            sem = self.esem[name]
            for o in self.ops[name]:
                for d in o["waits"]:
                    if d[0] == "e":
                        eng.wait_ge(self.esem[d[1]], self.ops[d[1]][d[2]]["cnt"])
                    else:
                        eng.wait_ge(d[1].sem, d[2])
                ins = o["fn"](eng)
                if o["dma"] is not None:
                    ins.then_inc(o["dma"].sem, 16)
                elif o["inc"]:
                    ins.then_inc(sem, 1)

        with nc.Block() as block:
            @block.tensor
            def _(e):
                replay("pe", e)

            @block.scalar
            def _(e):
                replay("act", e)

            @block.vector
            def _(e):
                replay("dve", e)

            @block.gpsimd
            def _(e):
                replay("pool", e)

            @block.sync
            def _(e):
                replay("sp", e)

    def mm(self, out_t, out_ap, lt, l_ap, rt, r_ap, start, stop):
        self.op("pe", lambda e: e.matmul(out_ap, l_ap, r_ap, start=start, stop=stop), reads=[lt, rt], writes=[out_t])

    def act(self, out_t, out_ap, in_t, in_ap, func, scale=None, bias=None, extra_reads=()):
        kw = {}
        if scale is not None:
            kw["scale"] = scale
        if bias is not None:
            kw["bias"] = bias
        self.op("act", lambda e: e.activation(out=out_ap, in_=in_ap, func=func, **kw),
                reads=[in_t] + list(extra_reads), writes=[out_t])

    def tt(self, eng, out_t, out_ap, a_t, a_ap, b_t, b_ap, op):
        self.op(eng, lambda e: e.tensor_tensor(out=out_ap, in0=a_ap, in1=b_ap, op=op), reads=[a_t, b_t], writes=[out_t])

    def stt(self, out_t, out_ap, a_t, a_ap, scalar, b_t, b_ap, op0, op1, extra_reads=()):
        self.op("dve", lambda e: e.scalar_tensor_tensor(out=out_ap, in0=a_ap, scalar=scalar, in1=b_ap, op0=op0, op1=op1),
                reads=[a_t, b_t] + list(extra_reads), writes=[out_t])

    def ts(self, eng, out_t, out_ap, a_t, a_ap, s1, op0, s2=None, op1=None, extra_reads=()):
        if op1 is None:
            f = lambda e: e.tensor_scalar(out=out_ap, in0=a_ap, scalar1=s1, scalar2=None, op0=op0)
        else:
            f = lambda e: e.tensor_scalar(out=out_ap, in0=a_ap, scalar1=s1, scalar2=s2, op0=op0, op1=op1)
        self.op(eng, f, reads=[a_t] + list(extra_reads), writes=[out_t])

    def copy(self, eng, out_t, out_ap, in_t, in_ap):
        if eng == "act":
            self.op("act", lambda e: e.copy(out=out_ap, in_=in_ap), reads=[in_t], writes=[out_t])
        else:
            self.op(eng, lambda e: e.tensor_copy(out=out_ap, in_=in_ap), reads=[in_t], writes=[out_t])

    def memset(self, eng, t, ap, val):
        self.op(eng, lambda e: e.memset(ap, val), writes=[t])


C_ID, C_LE, C_LT, C_NGE, C_U, C_U2, C_BM, C_IF, C_SG, C_E127, CW = 0, 128, 256, 384, 512, 640, 768, 1024, 1025, 1026, 1027


def make_consts():
    c = np.zeros((128, CW), np.float32)
    k = np.arange(128)[:, None]
    q = np.arange(128)[None, :]
    c[:, C_ID:C_ID + 128] = (k == q)
    c[:, C_LE:C_LE + 128] = (k <= q)
    c[:, C_LT:C_LT + 128] = (k < q)
    c[:, C_NGE:C_NGE + 128] = -1.0 * (k >= q)
    c[:, C_U:C_U + 128] = (-1.0 / 16.0) * (k <= q)
    c[:, C_U2:C_U2 + 128] = (-1.0 / 16.0) * (k > q)
    bm = np.zeros((128, 256), np.float32)
    bm[:64, :128] = 1.0
    bm[64:, 128:] = 1.0
    c[:, C_BM:C_BM + 256] = bm
    inv_freq = (1.0 / (10000.0 ** (np.arange(32, dtype=np.float32) / 32.0))).astype(np.float32)
    c[:, C_IF] = np.tile(inv_freq, 4) / (2.0 * np.pi)
    c[:, C_SG] = np.tile(np.concatenate([-np.ones(32), np.ones(32)]), 2)
    c[127, C_E127] = -1.0
    return c


def gcol_layer(l, which):
    return l * 32 + which * 8


def build(S, NL, dbg=False, stop=None):
    NG = S // 512
    NT = S // 128
    NE = (NL + 1) // 2
    NO = NL // 2
    GC_FIN = NL * 32
    GC_EV = GC_FIN + 8
    GW = GC_EV + 7 * NE
    nc = bass.Bass("TRN2", target_bir_lowering=False)

    def din(name, shape, dt=F32):
        return nc.dram_tensor(name, list(shape), dt, kind="ExternalInput").ap()

    def dscr(name, shape, dt):
        return nc.dram_tensor(name, list(shape), dt, kind=("ExternalOutput" if dbg else "Internal")).ap()

    x_in = din("x", [S, D])
    p_in = din("p", [NL, S, PD])
    pos_in = din("pos", [1, S], I32)
    consts_in = din("consts", [128, CW])
    gains_in = din("gains", [128, GW])
    wgu_in = din("wgu", [NL * 2, 128, 8, 2 * DFF])
    wdn_in = din("wdn", [NL * 2, 128, 11, D])
    pleg_in = din("pleg", [NL, 128, 8, D])
    plep_in = din("plep", [NL, 128, 2, D])
    win_in = din("win", [NE, 128, 8, 2448])
    wuq_in = din("wuq", [NE, 128, 2, 1024])
    wukv_in = din("wukv", [NE, 128, 1024])
    wg2_in = din("wg2", [NE, 32, 256])
    woute_in = din("woute", [NE, 128, 8, D])
    if NO > 0:
        wqkv_in = din("wqkv", [NO, 128, 8, 3 * D])
        wouto_in = din("wouto", [NO, 128, 8, D])
    y_out = nc.dram_tensor("y", [S, D], F32, kind="ExternalOutput").ap()

    xT = dscr("xT", [D, S], F32)
    oT = dscr("oT", [D, S], BF)
    cosT = dscr("cosT", [128, S], F32)
    sinT = dscr("sinT", [128, S], F32)
    qnT = dscr("qnT", [512, S], BF)
    qrT = dscr("qrT", [256, S], BF)
    knT = dscr("knT", [512, S], BF)
    krT = dscr("krT", [128, S], BF)
    vm = dscr("vm", [S, 512], BF)
    gqT = dscr("gqT", [256, S], F32)
    gkT = dscr("gkT", [256, S], F32)
    gkt = dscr("gkt", [S, 256], F32)
    gv = dscr("gv", [S, 512], BF)
    gaT = dscr("gaT", [32, S], BF)
    sgrT = dscr("sgrT", [512, S], BF)
    sqT = dscr("sqT", [D, S], BF)
    skT = dscr("skT", [D, S], BF)
    sv = dscr("sv", [S, D], BF)

    stack = ExitStack()
    with stack:
        P = Prog(nc, stack)
        P.init_mem()

        cst = P.alloc("consts", CW * 4)
        gains = P.alloc("gains", GW * 4)
        cbf = P.alloc("cbf", 1280 * 2)
        P.new_dg([cst])
        P.new_dg([gains])
        cst_f = cst.v(F32, CW)
        gains_f = gains.v(F32, GW)
        cbf_b = cbf.v(BF, 1280)
        B_LE, B_LT, B_NGE, B_ONE, B_NEG, B_ZERO = 0, 128, 256, 384, 512, 640
        P.load("sp", [cst], cst_f, consts_in[:, :])
        P.load("sp", [gains], gains_f, gains_in[:, :])
        P.copy("dve", cbf, cbf_b[:, 0:384], cst, cst_f[:, C_LE:C_LE + 384])
        P.memset("dve", cbf, cbf_b[:, B_ONE:B_ONE + 128], 1.0)
        P.memset("dve", cbf, cbf_b[:, B_NEG:B_NEG + 128], -1.0)
        P.memset("dve", cbf, cbf_b[:, B_ZERO:B_ZERO + 512], 0.0)
        ident = cst_f[:, C_ID:C_ID + 128]
        base_top = P.top
        P.phase_dgs = []

        def gcol(c):
            return gains_f[:, c:c + 1]

        def grp(ap2d, g):
            return ap2d[:, g * 512:(g + 1) * 512]

        def prologue():
            P.top = base_top
            CH = min(S, 2048)
            pi_t = P.alloc("posi", CH * 4)
            t_u = P.alloc("u", CH * 4)
            t_t = P.alloc("t", CH * 4)
            t_k = P.alloc("k", CH * 4)
            t_f = P.alloc("f", CH * 4)
            t_m = P.alloc("m", CH * 4)
            t_o = P.alloc("o", CH * 4)
            P.new_dg([pi_t])
            P.new_dg([t_o])
            for c0 in range(0, S, CH):
                P.load("sp", [pi_t], pi_t.v(I32, CH), pos_in[0:1, c0:c0 + CH].partition_broadcast(128)[:, 0, :])
                P.copy("dve", t_u, t_u.v(F32, CH), pi_t, pi_t.v(I32, CH))
                P.ts("dve", t_u, t_u.v(F32, CH), t_u, t_u.v(F32, CH), cst_f[:, C_IF:C_IF + 1], ALU.mult, extra_reads=[cst])
                for shift, dst, signed in ((0.0, sinT, True), (0.25, cosT, False)):
                    P.ts("dve", t_t, t_t.v(F32, CH), t_u, t_u.v(F32, CH), shift, ALU.add)
                    P.copy("dve", t_k, t_k.v(I32, CH), t_t, t_t.v(F32, CH))
                    P.copy("dve", t_f, t_f.v(F32, CH), t_k, t_k.v(I32, CH))
                    P.tt("dve", t_f, t_f.v(F32, CH), t_t, t_t.v(F32, CH), t_f, t_f.v(F32, CH), ALU.subtract)
                    P.ts("dve", t_m, t_m.v(F32, CH), t_f, t_f.v(F32, CH), 0.5, ALU.is_gt)
                    P.tt("dve", t_f, t_f.v(F32, CH), t_f, t_f.v(F32, CH), t_m, t_m.v(F32, CH), ALU.subtract)
                    P.ts("dve", t_m, t_m.v(F32, CH), t_f, t_f.v(F32, CH), -0.5, ALU.is_lt)
                    P.tt("dve", t_f, t_f.v(F32, CH), t_f, t_f.v(F32, CH), t_m, t_m.v(F32, CH), ALU.add)
                    P.act(t_o, t_o.v(F32, CH), t_f, t_f.v(F32, CH), AF.Sin, scale=2.0 * np.pi * (1.0 - 1e-6))
                    if signed:
                        P.ts("dve", t_o, t_o.v(F32, CH), t_o, t_o.v(F32, CH), cst_f[:, C_SG:C_SG + 1], ALU.mult, extra_reads=[cst])
                    P.store("sp", [t_o], dst[:, c0:c0 + CH], t_o.v(F32, CH))
            P.barrier()

        def load_w(name, dram_ap, kc, cols):
            t = P.alloc(name, kc * cols * 2)
            P.new_dg([t])
            v = t.v(BF, kc, cols)
            nsp = 1
            while cols // nsp > 2048 or cols % nsp:
                nsp += 1
            w = cols // nsp
            for k in range(kc):
                for s_ in range(nsp):
                    P.load("pool", [t], v[:, k, s_ * w:(s_ + 1) * w], dram_ap[:, k, s_ * w:(s_ + 1) * w])
            return t, v

        def norm_stats(xc, xcv, nchunk, nfeat, sq, rs_bank, tmp, W=512):
            for c in range(nchunk):
                P.act(sq[c][0], sq[c][1], xc[c], xcv[c], AF.Square)
            ssum = P.bank()
            for c in range(nchunk):
                P.mm(ssum, ssum.ap[:, 0:W], cbf, cbf_b[:, B_ONE:B_ONE + 128], sq[c][0], sq[c][1], c == 0, c == nchunk - 1)
            P.act(tmp[0], tmp[1], ssum, ssum.ap[:, 0:W], AF.Ln, scale=1.0 / nfeat, bias=EPS)
            P.act(rs_bank, rs_bank.ap[:, 0:W], tmp[0], tmp[1], AF.Exp, scale=-0.5)

        GS = 256
        NGS = S // GS
        TT = GS // 128

        def alloc_lane(tag):
            L = {}
            L["xct"] = [P.alloc("xc%s%d" % (tag, c), GS * 4) for c in range(8)]
            P.new_dg(L["xct"])
            L["xcv"] = [t.v(F32, GS) for t in L["xct"]]
            L["xn"] = [P.alloc("xn%s%d" % (tag, c), GS * 2) for c in range(8)]
            L["xnv"] = [t.v(BF, GS) for t in L["xn"]]
            L["actt"] = [P.alloc("act%s%d" % (tag, c), GS * 2) for c in range(11)]
            L["actv"] = [t.v(BF, GS) for t in L["actt"]]
            L["sg"] = [P.alloc("sg%s%d" % (tag, c), GS * 4) for c in range(2)]
            t = P.alloc("ntmp" + tag, GS * 4)
            L["tmp"] = (t, t.v(F32, GS))
            return L

        def norm_xn(L, gc):
            xct, xcv, xn, xnv, actt, actv, rsb = L["xct"], L["xcv"], L["xn"], L["xnv"], L["actt"], L["actv"], L["rsb"]
            norm_stats(xct, xcv, 8, D, [(actt[c], actv[c]) for c in range(8)], rsb, L["tmp"], W=GS)
            for c in range(8):
                P.stt(xn[c], xnv[c], xct[c], xcv[c], gcol(gc + c), rsb, rsb.ap[:, 0:GS], ALU.mult, ALU.mult, extra_reads=[gains])

        def ffn_gen(L, W, gc):
            xct, xcv, xn, xnv, actt, actv, sg = L["xct"], L["xcv"], L["xn"], L["xnv"], L["actt"], L["actv"], L["sg"]
            wgu_t, wgu_v, wdn_t, wdn_v = W
            norm_xn(L, gc)
            yield
            for f in range(11):
                pg = P.bank()
                pu = P.bank()
                for k in range(8):
                    P.mm(pg, pg.ap[:, 0:GS], wgu_t, wgu_v[:, k, f * 128:(f + 1) * 128], xn[k], xnv[k], k == 0, k == 7)
                for k in range(8):
                    P.mm(pu, pu.ap[:, 0:GS], wgu_t, wgu_v[:, k, DFF + f * 128:DFF + (f + 1) * 128], xn[k], xnv[k], k == 0, k == 7)
                s = sg[f % 2]
                P.act(s, s.v(F32, GS), pg, pg.ap[:, 0:GS], AF.Silu)
                P.tt("dve", actt[f], actv[f], s, s.v(F32, GS), pu, pu.ap[:, 0:GS], ALU.mult)
            yield
            for m in range(8):
                pd = P.bank()
                for f in range(11):
                    P.mm(pd, pd.ap[:, 0:GS], wdn_t, wdn_v[:, f, m * 128:(m + 1) * 128], actt[f], actv[f], f == 0, f == 10)
                P.stt(xct[m], xcv[m], pd, pd.ap[:, 0:GS], 0.5, xct[m], xcv[m], ALU.mult, ALU.add)
            yield

        def run_lanes(g0, g1):
            a0 = next(g0, "END") != "END"
            a1 = True
            while a0 or a1:
                if a1:
                    a1 = next(g1, "END") != "END"
                if a0:
                    a0 = next(g0, "END") != "END"

        xcd = xT.rearrange("(c p) s -> p c s", p=128)

        def phase_A(l):
            P.top = base_top
            even = (l % 2 == 0)
            j = l // 2
            lanes = [alloc_lane("a"), alloc_lane("b")]
            W = load_w("wgu", wgu_in[2 * l], 8, 2 * DFF) + load_w("wdn", wdn_in[2 * l], 11, D)
            if even:
                win_t, win_v = load_w("win", win_in[j], 8, 2448)
                wuq_t, wuq_v = load_w("wuq", wuq_in[j], 2, 1024)
                wukv_t = P.alloc("wukv", 2048)
                P.new_dg([wukv_t])
                wukv_v = wukv_t.v(BF, 1024)
                P.load("pool", [wukv_t], wukv_v, wukv_in[j])
            else:
                wqkv_t, wqkv_v = load_w("wqkv", wqkv_in[j], 8, 3 * D)
            for li, L in enumerate(lanes):
                L["rsb"] = P.banks[li]
                L["stg"] = [P.alloc("stg%d_%d" % (li, i), 4096) for i in range(3)]
                for s_ in L["stg"]:
                    P.new_dg([s_])
                L["stg_i"] = 0
                if even:
                    L["cqf"] = [P.alloc("cqf%d_%d" % (li, c), GS * 4) for c in range(2)]
                    L["cqn"] = [P.alloc("cqn%d_%d" % (li, c), GS * 2) for c in range(2)]
                    L["ckvf"] = P.alloc("ckvf%d" % li, GS * 4)
                    L["ckvn"] = P.alloc("ckvn%d" % li, GS * 2)
                    L["cs"] = P.alloc("cos%d" % li, GS * 4)
                    L["sn"] = P.alloc("sin%d" % li, GS * 4)
                    P.new_dg([L["cs"], L["sn"]])
                    L["rt"] = [P.alloc("rt%d_%d" % (li, i), GS * 4) for i in range(2)]
                    L["gat"] = P.alloc("gat%d" % li, GS * 2)
                    P.new_dg([L["gat"]])
                    P.memset("dve", L["gat"], L["gat"].v(BF, GS)[0:32, :], 1.0)
            saved_banks = P.banks
            P.banks = saved_banks[2:]
            P.bank_i = 0

            def body(L, groups):
                xct, xcv, xn, xnv, actt, actv = L["xct"], L["xcv"], L["xn"], L["xnv"], L["actt"], L["actv"]
                rsb = L["rsb"]

                def nstg():
                    s_ = L["stg"][L["stg_i"] % 3]
                    L["stg_i"] += 1
                    return s_

                for g in groups:
                    gs = slice(g * GS, (g + 1) * GS)
                    if l == 0:
                        xts = []
                        for tt_ in range(TT):
                            s_ = nstg()
                            P.load("sp", [s_], s_.v(F32, 1024), x_in[g * GS + tt_ * 128: g * GS + (tt_ + 1) * 128, :])
                            xts.append(s_)
                        for c in range(8):
                            b = P.bank()
                            for tt_ in range(TT):
                                s_ = xts[tt_]
                                P.op("pe", lambda e, b=b, tt_=tt_, s_=s_, c=c: e.transpose(b.ap[:, tt_ * 128:(tt_ + 1) * 128], s_.v(F32, 1024)[:, c * 128:(c + 1) * 128], ident),
                                     reads=[s_, cst], writes=[b])
                            P.copy("act" if c % 2 else "dve", xct[c], xcv[c], b, b.ap[:, 0:GS])
                    else:
                        P.load("sp", xct, P.span(xct, F32, 8, GS), xcd[:, :, gs])
                    yield from ffn_gen(L, W, gcol_layer(l, 0))
                    P.store("sp", xct, xcd[:, :, gs], P.span(xct, F32, 8, GS))
                    norm_xn(L, gcol_layer(l, 1))
                    yield
                    if not even:
                        for kind, dst, sc in ((0, sqT, 128.0 ** -0.5), (1, skT, 1.0)):
                            for hh in range(0, 8, 4):
                                s_ = nstg()
                                sv_ = s_.v(BF, 4, GS)
                                for h4 in range(4):
                                    h = hh + h4
                                    b = P.bank()
                                    for k in range(8):
                                        P.mm(b, b.ap[:, 0:GS], wqkv_t, wqkv_v[:, k, kind * D + h * 128: kind * D + (h + 1) * 128], xn[k], xnv[k], k == 0, k == 7)
                                    P.act(s_, sv_[:, h4, :], b, b.ap[:, 0:GS], AF.Identity, scale=sc)
                                P.store("sp", [s_], dst.rearrange("(c p) s -> p c s", p=128)[:, hh:hh + 4, gs], sv_)
                            yield
                        for tt_ in range(TT):
                            s_ = nstg()
                            sv_ = s_.v(BF, 1024)
                            for half in range(2):
                                b = P.bank()
                                for k in range(8):
                                    P.mm(b, b.ap[:, :], xn[k], xnv[k][:, tt_ * 128:(tt_ + 1) * 128], wqkv_t, wqkv_v[:, k, 2 * D + half * 512: 2 * D + (half + 1) * 512], k == 0, k == 7)
                                P.copy("dve" if half else "act", s_, sv_[:, half * 512:(half + 1) * 512], b, b.ap[:, :])
                            P.store("sp", [s_], sv[g * GS + tt_ * 128: g * GS + (tt_ + 1) * 128, :], sv_)
                        yield
                    else:
                        GE = GC_EV + 7 * j
                        cqf, cqn, ckvf, ckvn, cs_t, sn_t, rt, gat = L["cqf"], L["cqn"], L["ckvf"], L["ckvn"], L["cs"], L["sn"], L["rt"], L["gat"]
                        P.load("sp", [cs_t, sn_t], cs_t.v(F32, GS), cosT[:, gs])
                        P.load("sp", [cs_t, sn_t], sn_t.v(F32, GS), sinT[:, gs])

                        def fm(chunk, ncols=128):
                            b = P.bank()
                            for k in range(8):
                                P.mm(b, b.ap[0:ncols, 0:GS], win_t, win_v[:, k, chunk * 128: chunk * 128 + ncols], xn[k], xnv[k], k == 0, k == 7)
                            return b

                        for c in range(2):
                            b = fm(c)
                            P.copy("dve", cqf[c], cqf[c].v(F32, GS), b, b.ap[:, 0:GS])
                        b = fm(2)
                        P.copy("dve", ckvf, ckvf.v(F32, GS), b, b.ap[:, 0:GS])
                        rs2 = P.bank()
                        norm_stats(cqf, [t.v(F32, GS) for t in cqf], 2, 256, [(actt[c], actv[c]) for c in range(2)], rs2, L["tmp"], W=GS)
                        for c in range(2):
                            P.stt(cqn[c], cqn[c].v(BF, GS), cqf[c], cqf[c].v(F32, GS), gcol(GE + c), rs2, rs2.ap[:, 0:GS], ALU.mult, ALU.mult, extra_reads=[gains])
                        rs3 = P.bank()
                        norm_stats([ckvf], [ckvf.v(F32, GS)], 1, 128, [(actt[2], actv[2])], rs3, L["tmp"], W=GS)
                        P.stt(ckvn, ckvn.v(BF, GS), ckvf, ckvf.v(F32, GS), gcol(GE + 2), rs3, rs3.ap[:, 0:GS], ALU.mult, ALU.mult, extra_reads=[gains])
                        yield

                        def rope(b_a, b_s, out_t, out_ap):
                            P.tt("dve", rt[0], rt[0].v(F32, GS), b_a, b_a.ap[:, 0:GS], cs_t, cs_t.v(F32, GS), ALU.mult)
                            P.tt("dve", rt[1], rt[1].v(F32, GS), b_s, b_s.ap[:, 0:GS], sn_t, sn_t.v(F32, GS), ALU.mult)
                            P.tt("pool", out_t, out_ap, rt[0], rt[0].v(F32, GS), rt[1], rt[1].v(F32, GS), ALU.add)

                        b3 = fm(3)
                        b4 = fm(4)
                        s_ = nstg()
                        rope(b3, b4, s_, s_.v(BF, GS))
                        P.store("sp", [s_], krT[:, gs], s_.v(BF, GS))
                        for c0, dst, sc in ((5, gqT, 64.0 ** -0.5), (7, gkT, 1.0)):
                            s_ = nstg()
                            sv_ = s_.v(F32, 2, GS)
                            for c in range(2):
                                b = fm(c0 + c)
                                P.act(s_, sv_[:, c, :], b, b.ap[:, 0:GS], AF.Identity, scale=sc)
                            P.store("sp", [s_], dst.rearrange("(c p) s -> p c s", p=128)[:, :, gs], sv_)
                        yield
                        s_ = nstg()
                        sv_ = s_.v(BF, 4, GS)
                        for c in range(4):
                            b = fm(9 + c)
                            P.act(s_, sv_[:, c, :], b, b.ap[:, 0:GS], AF.Silu)
                        P.store("sp", [s_], sgrT.rearrange("(c p) s -> p c s", p=128)[:, :, gs], sv_)
                        b = fm(13, 16)
                        P.copy("dve", gat, gat.v(BF, GS)[0:16, :], b, b.ap[0:16, 0:GS])
                        P.store("sp", [gat], gaT[:, gs], gat.v(BF, GS)[0:32, :])
                        yield
                        sk_ = nstg()
                        svv_ = nstg()
                        for tt_ in range(TT):
                            b1 = P.bank()
                            b2 = P.bank()
                            for k in range(8):
                                P.mm(b1, b1.ap[:, 0:256], xn[k], xnv[k][:, tt_ * 128:(tt_ + 1) * 128], win_t, win_v[:, k, 1680:1936], k == 0, k == 7)
                            for k in range(8):
                                P.mm(b2, b2.ap[:, :], xn[k], xnv[k][:, tt_ * 128:(tt_ + 1) * 128], win_t, win_v[:, k, 1936:2448], k == 0, k == 7)
                            P.copy("act", sk_, sk_.v(F32, TT, 256)[:, tt_, :], b1, b1.ap[:, 0:256])
                            P.copy("dve", svv_, svv_.v(BF, TT, 512)[:, tt_, :], b2, b2.ap[:, :])
                        P.store("sp", [sk_], gkt[gs, :].rearrange("(t p) c -> p t c", p=128), sk_.v(F32, TT, 256))
                        P.store("sp", [svv_], gv[gs, :].rearrange("(t p) c -> p t c", p=128), svv_.v(BF, TT, 512))
                        yield

                        def uq(chunk):
                            b = P.bank()
                            for k in range(2):
                                P.mm(b, b.ap[:, 0:GS], wuq_t, wuq_v[:, k, chunk * 128:(chunk + 1) * 128], cqn[k], cqn[k].v(BF, GS), k == 0, k == 1)
                            return b
                        s_ = nstg()
                        sv_ = s_.v(BF, 4, GS)
                        for h in range(4):
                            b = uq(h)
                            P.copy("act", s_, sv_[:, h, :], b, b.ap[:, 0:GS])
                        P.store("sp", [s_], qnT.rearrange("(c p) s -> p c s", p=128)[:, :, gs], sv_)
                        s_ = nstg()
                        sv_ = s_.v(BF, 2, GS)
                        for pr in range(2):
                            ba = uq(4 + pr)
                            bs = uq(6 + pr)
                            rope(ba, bs, s_, sv_[:, pr, :])
                        P.store("sp", [s_], qrT.rearrange("(c p) s -> p c s", p=128)[:, :, gs], sv_)
                        s_ = nstg()
                        sv_ = s_.v(BF, 4, GS)
                        for h in range(4):
                            b = P.bank()
                            P.mm(b, b.ap[:, 0:GS], wukv_t, wukv_v[:, h * 128:(h + 1) * 128], ckvn, ckvn.v(BF, GS), True, True)
                            P.copy("act", s_, sv_[:, h, :], b, b.ap[:, 0:GS])
                        P.store("sp", [s_], knT.rearrange("(c p) s -> p c s", p=128)[:, :, gs], sv_)
                        s_ = nstg()
                        sv_ = s_.v(BF, TT, 512)
                        for tt_ in range(TT):
                            b = P.bank()
                            P.mm(b, b.ap[:, :], ckvn, ckvn.v(BF, GS)[:, tt_ * 128:(tt_ + 1) * 128], wukv_t, wukv_v[:, 512:1024], True, True)
                            P.copy("dve", s_, sv_[:, tt_, :], b, b.ap[:, :])
                        P.store("sp", [s_], vm[gs, :].rearrange("(t p) c -> p t c", p=128), sv_)
                        yield

            run_lanes(body(lanes[0], range(0, NGS, 2)), body(lanes[1], range(1, NGS, 2)))
            P.banks = saved_banks
            P.barrier()

        def phase_B(l):
            P.top = base_top
            even = (l % 2 == 0)
            j = l // 2
            last = (l == NL - 1)
            lanes = [alloc_lane("a"), alloc_lane("b")]
            W = load_w("wgu", wgu_in[2 * l + 1], 8, 2 * DFF) + load_w("wdn", wdn_in[2 * l + 1], 11, D)
            wo_t, wo_v = load_w("wo", (woute_in if even else wouto_in)[j], 8, D)
            wg_t, wg_v = load_w("pleg", pleg_in[l], 8, D)
            wp_t, wp_v = load_w("plep", plep_in[l], 2, D)
            for li, L in enumerate(lanes):
                L["rsb"] = P.banks[li]
                L["ot"] = [P.alloc("ot%d_%d" % (li, c), GS * 2) for c in range(8)]
                P.new_dg(L["ot"])
                L["pin"] = P.alloc("pin%d" % li, TT * 256 * 4)
                P.new_dg([L["pin"]])
                L["pT"] = [P.alloc("pT%d_%d" % (li, c), GS * 2) for c in range(2)]
                L["sig"] = [P.alloc("sig%d_%d" % (li, c), GS * 4) for c in range(2)]
                L["pt2"] = [P.alloc("pt2%d_%d" % (li, c), GS * 4) for c in range(2)]
                if last:
                    L["yt"] = [P.alloc("yt%d_%d" % (li, c), 4096) for c in range(2)]
                    for t in L["yt"]:
                        P.new_dg([t])
            saved_banks = P.banks
            P.banks = saved_banks[2:]
            P.bank_i = 0
            ocd = oT.rearrange("(c p) s -> p c s", p=128)

            def body(L, groups):
                xct, xcv, xn, xnv, actt, actv = L["xct"], L["xcv"], L["xn"], L["xnv"], L["actt"], L["actv"]
                rsb, ot, pin, pT, sig, pt2 = L["rsb"], L["ot"], L["pin"], L["pT"], L["sig"], L["pt2"]
                for g in groups:
                    gs = slice(g * GS, (g + 1) * GS)
                    P.load("sp", xct, P.span(xct, F32, 8, GS), xcd[:, :, gs])
                    P.load("sp", ot, P.span(ot, BF, 8, GS), ocd[:, :, gs])
                    P.load("sp", [pin], pin.v(F32, TT, 256), p_in[l, gs, :].rearrange("(t p) c -> p t c", p=128))
                    for m in range(8):
                        b = P.bank()
                        for k in range(8):
                            P.mm(b, b.ap[:, 0:GS], wo_t, wo_v[:, k, m * 128:(m + 1) * 128], ot[k], ot[k].v(BF, GS), k == 0, k == 7)
                        P.tt("dve", xct[m], xcv[m], b, b.ap[:, 0:GS], xct[m], xcv[m], ALU.add)
                    yield
                    yield from ffn_gen(L, W, gcol_layer(l, 2))
                    norm_xn(L, gcol_layer(l, 3))
                    for c in range(2):
                        b = P.bank()
                        for tt_ in range(TT):
                            P.op("pe", lambda e, b=b, tt_=tt_, c=c: e.transpose(b.ap[:, tt_ * 128:(tt_ + 1) * 128], pin.v(F32, TT, 256)[:, tt_, c * 128:(c + 1) * 128], ident),
                                 reads=[pin, cst], writes=[b])
                        P.copy("act", pT[c], pT[c].v(BF, GS), b, b.ap[:, 0:GS])
                    yield
                    for m in range(8):
                        bg = P.bank()
                        bp = P.bank()
                        for k in range(8):
                            P.mm(bg, bg.ap[:, 0:GS], wg_t, wg_v[:, k, m * 128:(m + 1) * 128], xn[k], xnv[k], k == 0, k == 7)
                        for k in range(2):
                            P.mm(bp, bp.ap[:, 0:GS], wp_t, wp_v[:, k, m * 128:(m + 1) * 128], pT[k], pT[k].v(BF, GS), k == 0, k == 1)
                        s_ = sig[m % 2]
                        t2 = pt2[m % 2]
                        P.act(s_, s_.v(F32, GS), bg, bg.ap[:, 0:GS], AF.Sigmoid)
                        P.tt("dve", t2, t2.v(F32, GS), s_, s_.v(F32, GS), bp, bp.ap[:, 0:GS], ALU.mult)
                        P.tt("pool", xct[m], xcv[m], xct[m], xcv[m], t2, t2.v(F32, GS), ALU.add)
                    if not last:
                        P.store("sp", xct, xcd[:, :, gs], P.span(xct, F32, 8, GS))
                        yield
                    else:
                        yield
                        norm_stats(xct, xcv, 8, D, [(actt[c], actv[c]) for c in range(8)], rsb, L["tmp"], W=GS)
                        for c in range(8):
                            P.stt(xct[c], xcv[c], xct[c], xcv[c], gcol(GC_FIN + c), rsb, rsb.ap[:, 0:GS], ALU.mult, ALU.mult, extra_reads=[gains])
                        yield
                        for tt_ in range(TT):
                            y_ = L["yt"][tt_ % 2]
                            yv = y_.v(F32, 1024)
                            for half in range(2):
                                b = P.bank()
                                for c4 in range(4):
                                    c = half * 4 + c4
                                    P.op("pe", lambda e, b=b, c4=c4, c=c, tt_=tt_: e.transpose(b.ap[:, c4 * 128:(c4 + 1) * 128], xcv[c][:, tt_ * 128:(tt_ + 1) * 128], ident),
                                         reads=[xct[c], cst], writes=[b])
                                P.copy("act" if half else "dve", y_, yv[:, half * 512:(half + 1) * 512], b, b.ap[:, :])
                            P.store("sp", [y_], y_out[g * GS + tt_ * 128: g * GS + (tt_ + 1) * 128, :], yv)
                        yield

            run_lanes(body(lanes[0], range(0, NGS, 2)), body(lanes[1], range(1, NGS, 2)))
            P.banks = saved_banks
            P.barrier()

        def phase_SB():
            P.top = base_top
            KT = [P.alloc("KT%d" % i, S * 2) for i in range(2)]
            VV = [P.alloc("VV%d" % i, S * 2) for i in range(2)]
            for t in KT + VV:
                P.new_dg([t])
            qt = [P.alloc("qt%d" % i, 1024) for i in range(3)]
            for t in qt:
                P.new_dg([t])
            et = [P.alloc("et%d" % i, 2048) for i in range(3)]
            lpt = [P.alloc("lp%d" % i, 1024) for i in range(3)]
            At = [P.alloc("At%d" % i, 1024) for i in range(3)]
            lp2 = [P.alloc("lp2%d" % i, 1024) for i in range(2)]
            ost = [P.alloc("ost%d" % i, 1024) for i in range(2)]
            for t in ost:
                P.new_dg([t])
            Bb = P.banks[0:3]
            Db = P.banks[3:5]
            Ob = P.banks[5:7]
            items = []
            gi = 0
            for h in range(8):
                for g in range(NG):
                    n = 4 * g + 4
                    for ci, c in enumerate(range(n - 1, -1, -1)):
                        r = c - 4 * g
                        q0 = r * 128 if r >= 0 else 0
                        items.append(dict(h=h, g=g, c=c, q0=q0, diag=(r >= 0), first=(ci == 0), last=(c == 0), gi=gi))
                    gi += 1
            state = {}

            def s1(i, it):
                h, g, c, q0 = it["h"], it["g"], it["c"], it["q0"]
                if it["first"]:
                    def ld_kv(hh):
                        P.load("sp", [KT[hh % 2]], KT[hh % 2].v(BF, S), skT[hh * 128:(hh + 1) * 128, :])
                        P.load("sp", [VV[hh % 2]], VV[hh % 2].v(BF, NT, 128), sv[:, hh * 128:(hh + 1) * 128].rearrange("(n p) d -> p n d", p=128))

                    def ld_q(gg):
                        h_, g_ = gg // NG, gg % NG
                        P.load("sp", [qt[gg % 3]], qt[gg % 3].v(BF, 512), sqT[h_ * 128:(h_ + 1) * 128, g_ * 512:(g_ + 1) * 512])
                    if it["gi"] == 0:
                        ld_kv(0)
                        ld_q(0)
                    if it["gi"] + 1 < 8 * NG:
                        ld_q(it["gi"] + 1)
                if g == 0 and c == 4 * g + 1 and h + 1 < 8:
                    P.load("sp", [KT[(h + 1) % 2]], KT[(h + 1) % 2].v(BF, S), skT[(h + 1) * 128:(h + 2) * 128, :])
                    P.load("sp", [VV[(h + 1) % 2]], VV[(h + 1) % 2].v(BF, NT, 128), sv[:, (h + 1) * 128:(h + 2) * 128].rearrange("(n p) d -> p n d", p=128))
                kt = KT[h % 2]
                q_ = qt[it["gi"] % 3]
                B = Bb[i % 3]
                P.mm(B, B.ap[:, q0:512], kt, kt.v(BF, S)[:, c * 128:(c + 1) * 128], q_, q_.v(BF, 512)[:, q0:512], True, False)
                e_ = et[i % 3]
                P.act(e_, e_.v(F32, 512)[:, q0:512], B, B.ap[:, q0:512], AF.Exp)

            def s1b(i, it):
                q0 = it["q0"]
                e_ = et[i % 3]
                l_ = lpt[i % 3]
                P.act(l_, l_.v(BF, 512)[:, q0:512], e_, e_.v(F32, 512)[:, q0:512], AF.Ln, bias=1.0)
                if it["diag"]:
                    P.tt("pool", l_, l_.v(BF, 512)[:, q0:q0 + 128], l_, l_.v(BF, 512)[:, q0:q0 + 128], cbf, cbf_b[:, B_LT:B_LT + 128], ALU.mult)

            def s2(i, it):
                h, g, c, q0 = it["h"], it["g"], it["c"], it["q0"]
                B = Bb[i % 3]
                l_ = lpt[i % 3]
                Dk = Db[it["gi"] % 2]
                if it["first"]:
                    P.mm(B, B.ap[:, q0:512], cbf, cbf_b[:, B_NGE:B_NGE + 128], l_, l_.v(BF, 512)[:, q0:512], False, True)
                    P.mm(Dk, Dk.ap[:, :], cbf, cbf_b[:, B_NEG:B_NEG + 128], cbf, cbf_b[:, B_ZERO:B_ZERO + 512], True, False)
                else:
                    l2 = lp2[i % 2]
                    P.stt(l2, l2.v(BF, 512)[:, q0:512], Dk, Dk.ap[:, q0:512], cst_f[:, C_E127:C_E127 + 1], l_, l_.v(BF, 512)[:, q0:512],
                          ALU.mult, ALU.add, extra_reads=[cst])
                    P.mm(B, B.ap[:, q0:512], cbf, cbf_b[:, B_NGE:B_NGE + 128], l2, l2.v(BF, 512)[:, q0:512], False, True)
                if not it["last"]:
                    P.mm(Dk, Dk.ap[:, q0:512], cbf, cbf_b[:, B_NEG:B_NEG + 128], l_, l_.v(BF, 512)[:, q0:512], False, False)
                a_ = At[i % 3]
                P.act(a_, a_.v(BF, 512)[:, q0:512], B, B.ap[:, q0:512], AF.Exp)
                if it["diag"]:
                    P.tt("pool", a_, a_.v(BF, 512)[:, q0:q0 + 128], a_, a_.v(BF, 512)[:, q0:q0 + 128], cbf, cbf_b[:, B_LT:B_LT + 128], ALU.mult)

            def s3(i, it):
                h, g, c, q0 = it["h"], it["g"], it["c"], it["q0"]
                a_ = At[i % 3]
                vt = VV[h % 2]
                O = Ob[it["gi"] % 2]
                if it["first"]:
                    P.mm(O, O.ap[:, :], cbf, cbf_b[:, B_NEG:B_NEG + 128], cbf, cbf_b[:, B_ZERO:B_ZERO + 512], True, False)
                P.mm(O, O.ap[:, q0:512], vt, vt.v(BF, NT, 128)[:, c, :], a_, a_.v(BF, 512)[:, q0:512], False, it["last"])
                if it["last"]:
                    o_ = ost[it["gi"] % 2]
                    P.copy("dve", o_, o_.v(BF, 512), O, O.ap[:, :])
                    P.store("sp", [o_], oT[h * 128:(h + 1) * 128, g * 512:(g + 1) * 512], o_.v(BF, 512))

            n = len(items)
            for t in range(n + 2):
                if t < n:
                    s1(t, items[t])
                if 0 <= t - 1 < n:
                    s2(t - 1, items[t - 1])
                if t < n:
                    s1b(t, items[t])
                if 0 <= t - 2 < n:
                    s3(t - 2, items[t - 2])
            P.barrier()

        def phase_MLA():
            P.top = base_top
            KN = [P.alloc("KN%d" % i, S * 2) for i in range(2)]
            VV = [P.alloc("VM%d" % i, S * 2) for i in range(2)]
            KR = P.alloc("KR", S * 2)
            for t in KN + VV + [KR]:
                P.new_dg([t])
            qn = [P.alloc("qn%d" % i, 1024) for i in range(3)]
            qr = [P.alloc("qr%d" % i, 1024) for i in range(3)]
            for i in range(3):
                P.new_dg([qn[i], qr[i]])
            Pt = [P.alloc("Pt%d" % i, 1024) for i in range(3)]
            rec = P.alloc("rec", 2048)
            ost = [P.alloc("ost%d" % i, 1024) for i in range(2)]
            for t in ost:
                P.new_dg([t])
            Bb = P.banks[0:3]
            Db = P.banks[3:5]
            Ob = P.banks[5:7]
            scale = float((128 + 64) ** -0.5)
            items = []
            gi = 0
            for h in range(4):
                for g in range(NG):
                    n = 4 * g + 4
                    for c in range(n):
                        r = c - 4 * g
                        q0 = r * 128 if r >= 0 else 0
                        items.append(dict(h=h, g=g, c=c, q0=q0, diag=(r >= 0), first=(c == 0), last=(c == n - 1), gi=gi))
                    gi += 1
            P.load("sp", [KR], KR.v(BF, S), krT[:, :])

            def s1(i, it):
                h, g, c, q0 = it["h"], it["g"], it["c"], it["q0"]
                hb = 64 * (h % 2)
                if it["first"]:
                    def ld_kv(hh):
                        P.load("sp", [KN[hh % 2]], KN[hh % 2].v(BF, S), knT[hh * 128:(hh + 1) * 128, :])
                        P.load("sp", [VV[hh % 2]], VV[hh % 2].v(BF, NT, 128), vm[:, hh * 128:(hh + 1) * 128].rearrange("(n p) d -> p n d", p=128))

                    def ld_q(gg):
                        h_, g_ = gg // NG, gg % NG
                        k_ = gg % 3
                        P.load("sp", [qn[k_], qr[k_]], qn[k_].v(BF, 512), qnT[h_ * 128:(h_ + 1) * 128, g_ * 512:(g_ + 1) * 512])
                        P.load("sp", [qn[k_], qr[k_]], qr[k_].v(BF, 512), qrT[(h_ // 2) * 128:(h_ // 2 + 1) * 128, g_ * 512:(g_ + 1) * 512])
                    if it["gi"] == 0:
                        ld_kv(0)
                        ld_q(0)
                    if it["gi"] + 1 < 4 * NG:
                        ld_q(it["gi"] + 1)
                if g == 0 and c == 2 and h + 1 < 4:
                    P.load("sp", [KN[(h + 1) % 2]], KN[(h + 1) % 2].v(BF, S), knT[(h + 1) * 128:(h + 2) * 128, :])
                    P.load("sp", [VV[(h + 1) % 2]], VV[(h + 1) % 2].v(BF, NT, 128), vm[:, (h + 1) * 128:(h + 2) * 128].rearrange("(n p) d -> p n d", p=128))
                k3 = it["gi"] % 3
                kt = KN[h % 2]
                B = Bb[i % 3]
                P.mm(B, B.ap[:, q0:512], kt, kt.v(BF, S)[:, c * 128:(c + 1) * 128], qn[k3], qn[k3].v(BF, 512)[:, q0:512], True, False)
                P.mm(B, B.ap[:, q0:512], KR, KR.v(BF, S)[hb:hb + 64, c * 128:(c + 1) * 128], qr[k3], qr[k3].v(BF, 512)[hb:hb + 64, q0:512], False, True)
                p_ = Pt[i % 3]
                P.act(p_, p_.v(BF, 512)[:, q0:512], B, B.ap[:, q0:512], AF.Exp, scale=scale)
                if it["diag"]:
                    P.tt("pool", p_, p_.v(BF, 512)[:, q0:q0 + 128], p_, p_.v(BF, 512)[:, q0:q0 + 128], cbf, cbf_b[:, B_LE:B_LE + 128], ALU.mult)

            def s2(i, it):
                h, g, c, q0 = it["h"], it["g"], it["c"], it["q0"]
                p_ = Pt[i % 3]
                vt = VV[h % 2]
                O = Ob[it["gi"] % 2]
                Dn = Db[it["gi"] % 2]
                P.mm(O, O.ap[:, q0:512], vt, vt.v(BF, NT, 128)[:, c, :], p_, p_.v(BF, 512)[:, q0:512], it["first"], it["last"])
                P.mm(Dn, Dn.ap[:, q0:512], cbf, cbf_b[:, B_ONE:B_ONE + 128], p_, p_.v(BF, 512)[:, q0:512], it["first"], it["last"])
                if it["last"]:
                    P.op("dve", lambda e: e.reciprocal(out=rec.v(F32, 512), in_=Dn.ap[:, :]), reads=[Dn], writes=[rec])
                    o_ = ost[it["gi"] % 2]
                    P.tt("dve", o_, o_.v(BF, 512), O, O.ap[:, :], rec, rec.v(F32, 512), ALU.mult)
                    P.store("sp", [o_], oT[h * 128:(h + 1) * 128, g * 512:(g + 1) * 512], o_.v(BF, 512))

            n = len(items)
            for t in range(n + 1):
                if t < n:
                    s1(t, items[t])
                if 0 <= t - 1 < n:
                    s2(t - 1, items[t - 1])
            P.barrier()

        def phase_GLA(j):
            P.top = base_top
            GE = GC_EV + 7 * j
            wg2 = P.alloc("wg2", 512)
            P.new_dg([wg2])
            P.load("pool", [wg2], wg2.v(BF, 256)[0:32, :], wg2_in[j])
            NB = 2
            ga = [P.alloc("ga%d" % i, 256) for i in range(NB)]
            gq = [P.alloc("gq%d" % i, 1024) for i in range(NB)]
            gk = [P.alloc("gk%d" % i, 1024) for i in range(NB)]
            gkt_ = [P.alloc("gkt%d" % i, 1024) for i in range(NB)]
            gv_ = [P.alloc("gv%d" % i, 1024) for i in range(NB)]
            sgr = [P.alloc("sgr%d" % i, 1024) for i in range(NB)]
            for i in range(NB):
                P.new_dg([ga[i], gq[i], gk[i], gkt_[i], gv_[i], sgr[i]])
            e1 = P.alloc("e1", 1024)
            lsp = P.alloc("lsp", 1024)
            ebt = P.alloc("ebt", 1024)
            enbt = P.alloc("enbt", 1024)
            edt = P.alloc("edt", 1024)
            qtl = P.alloc("qtl", 512)
            ktl = P.alloc("ktl", 512)
            kb = P.alloc("kb", 512)
            attm = P.alloc("attm", 1024)
            Sst = [P.alloc("Sst%d" % i, 1024) for i in range(2)]
            Sz = [P.alloc("Sz%d" % i, 512) for i in range(2)]
            of = P.alloc("of", 2048)
            osq = P.alloc("osq", 1024)
            ntmp = P.alloc("gtmp", 2048)
            on = P.alloc("on", 2048)
            ost = [P.alloc("gost%d" % i, 1024) for i in range(2)]
            for t in ost:
                P.new_dg([t])
            for i in range(2):
                P.memset("dve", Sst[i], Sst[i].v(F32, 256), 0.0)
                P.memset("dve", Sz[i], Sz[i].v(BF, 256), 0.0)
            P.bank_i = 0
            U_f = cst_f[:, C_U:C_U + 128]
            U2_f = cst_f[:, C_U2:C_U2 + 128]
            bm_f = cst_f[:, C_BM:C_BM + 256]

            def loads(n):
                i = n % NB
                sl = slice(n * 128, (n + 1) * 128)
                grp_t = [ga[i], gq[i], gk[i], gkt_[i], gv_[i], sgr[i]]
                P.load("sp", grp_t, ga[i].v(BF, 128)[0:32, :], gaT[:, sl])
                P.load("sp", grp_t, gq[i].v(F32, 2, 128), gqT.rearrange("(c p) s -> p c s", p=128)[:, :, sl])
                P.load("sp", grp_t, gk[i].v(F32, 2, 128), gkT.rearrange("(c p) s -> p c s", p=128)[:, :, sl])
                P.load("sp", grp_t, gkt_[i].v(F32, 256), gkt[sl, :])
                P.load("sp", grp_t, gv_[i].v(BF, 512), gv[sl, :])
                P.load("sp", grp_t, sgr[i].v(BF, 4, 128), sgrT.rearrange("(c p) s -> p c s", p=128)[:, :, sl])

            loads(0)
            for n in range(NT):
                i = n % NB
                if n + 1 < NT:
                    loads(n + 1)
                bG = P.bank()
                P.mm(bG, bG.ap[:, 0:256], ga[i], ga[i].v(BF, 128)[0:32, :], wg2, wg2.v(BF, 256)[0:32, :], True, True)
                P.act(e1, e1.v(F32, 256), bG, bG.ap[:, 0:256], AF.Exp, scale=-1.0)
                P.act(lsp, lsp.v(F32, 256), e1, e1.v(F32, 256), AF.Ln, bias=1.0)
                bD = P.bank()
                P.mm(bD, bD.ap[:, 0:256], cst, U2_f, lsp, lsp.v(F32, 256), True, True)
                bT = P.bank()
                for c in range(2):
                    P.mm(bT, bT.ap[:, c * 128:(c + 1) * 128], lsp, lsp.v(F32, 256)[:, c * 128:(c + 1) * 128], cst, U_f, True, True)
                P.act(ebt, ebt.v(F32, 256), bT, bT.ap[:, 0:256], AF.Exp)
                P.act(enbt, enbt.v(F32, 256), bT, bT.ap[:, 0:256], AF.Exp, scale=-1.0)
                P.act(edt, edt.v(F32, 256), bD, bD.ap[:, 0:256], AF.Exp)
                P.tt("dve", qtl, qtl.v(BF, 256), gq[i], gq[i].v(F32, 256), ebt, ebt.v(F32, 256), ALU.mult)
                P.tt("dve", ktl, ktl.v(BF, 256), gk[i], gk[i].v(F32, 256), enbt, enbt.v(F32, 256), ALU.mult)
                P.tt("dve", kb, kb.v(BF, 256), gkt_[i], gkt_[i].v(F32, 256), edt, edt.v(F32, 256), ALU.mult)
                ba = P.bank()
                bb = P.bank()
                for h in range(4):
                    pr, hb = h // 2, 64 * (h % 2)
                    bk = ba if h % 2 == 0 else bb
                    P.mm(bk, bk.ap[:, pr * 128:(pr + 1) * 128], ktl, ktl.v(BF, 2, 128)[hb:hb + 64, pr, :], qtl, qtl.v(BF, 2, 128)[hb:hb + 64, pr, :], True, True)
                for h in range(4):
                    pr = h // 2
                    bk = ba if h % 2 == 0 else bb
                    P.tt("dve", attm, attm.v(BF, 4, 128)[:, h, :], bk, bk.ap[:, pr * 128:(pr + 1) * 128], cbf, cbf_b[:, B_LE:B_LE + 128], ALU.mult)
                bO = P.bank()
                for h in range(4):
                    pr = h // 2
                    P.mm(bO, bO.ap[:, h * 128:(h + 1) * 128], gv_[i], gv_[i].v(BF, 512)[:, h * 128:(h + 1) * 128], attm, attm.v(BF, 4, 128)[:, h, :], True, False)
                    P.mm(bO, bO.ap[:, h * 128:(h + 1) * 128], Sz[pr], Sz[pr].v(BF, 256)[:, (h % 2) * 128:(h % 2 + 1) * 128], qtl, qtl.v(BF, 2, 128)[:, pr, :], False, True)
                for pr in range(2):
                    bS = P.bank()
                    P.mm(bS, bS.ap[:, 0:256], kb, kb.v(BF, 256)[:, pr * 128:(pr + 1) * 128], gv_[i], gv_[i].v(BF, 512)[:, pr * 256:(pr + 1) * 256], True, True)
                    P.stt(Sst[pr], Sst[pr].v(F32, 256), Sst[pr], Sst[pr].v(F32, 256), ebt.v(F32, 2, 128)[:, pr, 127:128], bS, bS.ap[:, 0:256], ALU.mult, ALU.add, extra_reads=[ebt])
                    P.tt("dve", Sz[pr], Sz[pr].v(BF, 256), Sst[pr], Sst[pr].v(F32, 256), cst, bm_f, ALU.mult)
                P.copy("act", of, of.v(F32, 512), bO, bO.ap[:, :])
                brs = P.bank()
                norm_stats([of], [of.v(F32, 512)], 1, 128, [(osq, osq.v(BF, 512))], brs, (ntmp, ntmp.v(F32, 512)))
                o_ = ost[n % 2]
                for h in range(4):
                    P.stt(on, on.v(F32, 4, 128)[:, h, :], of, of.v(F32, 4, 128)[:, h, :], gcol(GE + 3 + h), brs, brs.ap[:, h * 128:(h + 1) * 128], ALU.mult, ALU.mult, extra_reads=[gains])
                P.tt("dve", o_, o_.v(BF, 512), on, on.v(F32, 512), sgr[i], sgr[i].v(BF, 512), ALU.mult)
                P.store("sp", [o_], oT.rearrange("(c p) s -> p c s", p=128)[:, 4:8, n * 128:(n + 1) * 128], o_.v(BF, 4, 128))
            P.barrier()

        seq = []
        seq.append(("PRO", prologue))
        for l in range(NL):
            seq.append(("A%d" % l, lambda l=l: phase_A(l)))
            if l % 2 == 0:
                seq.append(("MLA%d" % l, phase_MLA))
                seq.append(("GLA%d" % l, lambda l=l: phase_GLA(l // 2)))
            else:
                seq.append(("SB%d" % l, phase_SB))
            seq.append(("B%d" % l, lambda l=l: phase_B(l)))
        for name, fn in seq:
            fn()
            P.recycle()
            if stop is not None and name == stop:
                break
        P.barrier()
        P.emit()
    return nc


def kc_layout(w):
    K, N = w.shape
    return np.ascontiguousarray(w.reshape(K // 128, 128, N).transpose(1, 0, 2))


def col_layout(v):
    return np.ascontiguousarray(v.reshape(-1, 128).T)


def prep_shared(inp, NL):
    NE = (NL + 1) // 2
    NO = NL // 2
    f = lambda a: np.asarray(a, np.float32)
    sh = {}
    sh["consts"] = make_consts()
    gl = []
    for l in range(NL):
        for nm in ("ffn1_norm", "mix_norm", "ffn2_norm", "ple_norm"):
            gl.append(col_layout(f(inp[nm][l])))
    gl.append(col_layout(f(inp["final_norm"])))
    for j in range(NE):
        gl.append(col_layout(f(inp["ev_q_norm"][j])))
        gl.append(col_layout(f(inp["ev_kv_norm"][j])))
        gl.append(np.ascontiguousarray(f(inp["ev_gla_norm"][j]).T))
    sh["gains"] = np.ascontiguousarray(np.concatenate(gl, axis=1))
    wgu, wdn = [], []
    for l in range(NL):
        for a, b in (("ffn1_w_gu", "ffn1_w_down"), ("ffn2_w_gu", "ffn2_w_down")):
            wgu.append(kc_layout(f(inp[a][l])))
            wdn.append(kc_layout(f(inp[b][l])))
    sh["wgu"] = np.stack(wgu)
    sh["wdn"] = np.stack(wdn)
    sh["pleg"] = np.stack([kc_layout(f(inp["ple_w_gate"][l])) for l in range(NL)])
    sh["plep"] = np.stack([kc_layout(f(inp["ple_w_proj"][l])) for l in range(NL)])
    win, wuq, wukv, wg2, woute = [], [], [], [], []
    for j in range(NE):
        w = f(inp["ev_w_in"][j])
        cq, ckv, kr = w[:, 0:256], w[:, 256:384], w[:, 384:448]
        gq, gk, gvv, ga, gr = w[:, 448:704], w[:, 704:960], w[:, 960:1472], w[:, 1472:1488], w[:, 1488:2000]
        krs = np.concatenate([kr[:, 32:64], kr[:, 0:32]], axis=1)
        ext = np.concatenate([cq, ckv, kr, kr, krs, krs, gq, gk, gr, ga, gk, gvv], axis=1)
        assert ext.shape[1] == 2448
        win.append(kc_layout(ext))
        u = f(inp["ev_w_uq"][j]).reshape(256, 4, 192)
        nope = [u[:, h, 0:128] for h in range(4)]
        rope = [u[:, h, 128:192] for h in range(4)]
        rsw = [np.concatenate([r[:, 32:64], r[:, 0:32]], axis=1) for r in rope]
        wuq.append(kc_layout(np.concatenate(nope + rope + rsw, axis=1)))
        kv = f(inp["ev_w_ukv"][j]).reshape(128, 4, 256)
        wukv.append(np.ascontiguousarray(np.concatenate([kv[:, h, 0:128] for h in range(4)] + [kv[:, h, 128:256] for h in range(4)], axis=1)))
        g2 = np.zeros((32, 256), np.float32)
        g2[0:16] = f(inp["ev_w_gate2"][j])
        g2[16] = f(inp["ev_b_gate"][j])
        wg2.append(g2)
        woute.append(kc_layout(f(inp["ev_w_out"][j])))
    sh["win"] = np.stack(win)
    sh["wuq"] = np.stack(wuq)
    sh["wukv"] = np.stack(wukv)
    sh["wg2"] = np.stack(wg2)
    sh["woute"] = np.stack(woute)
    if NO > 0:
        sh["wqkv"] = np.stack([kc_layout(f(inp["od_w_qkv"][j])) for j in range(NO)])
        sh["wouto"] = np.stack([kc_layout(f(inp["od_w_out"][j])) for j in range(NO)])
    return sh


def run(inp, S, NL, dbg=False, stop=None, ncores=NCORES):
    nc = build(S, NL, dbg=dbg, stop=stop)
    sh = prep_shared(inp, NL)
    x = np.asarray(inp["x"], np.float32)
    p = np.asarray(inp["p"], np.float32)
    pos = np.asarray(inp["positions"], np.int32)
    in_maps = []
    for c in range(ncores):
        m = dict(sh)
        m["x"] = np.ascontiguousarray(x[c, :S])
        m["p"] = np.ascontiguousarray(p[:NL, c, :S])
        m["pos"] = np.ascontiguousarray(pos[c:c + 1, :S])
        in_maps.append(m)
    res = run_bass_kernel_spmd(nc, in_maps, core_ids=list(range(ncores)))
    return res.results


def kernel(**inputs):
    x = np.asarray(inputs["x"])
    B, S, _ = x.shape
    NL = int(np.asarray(inputs["ffn1_norm"]).shape[0])
    res = run(inputs, S, NL)
    return np.stack([np.asarray(r["y"], np.float32) for r in res], axis=0)
```

```python
from contextlib import ExitStack
import numpy as np
import ml_dtypes
import concourse.bass as bass
import concourse.mybir as mybir
from concourse.bass_utils import run_bass_kernel_spmd

F32 = mybir.dt.float32
BF = mybir.dt.bfloat16
I32 = mybir.dt.int32
AF = mybir.ActivationFunctionType
ALU = mybir.AluOpType

D = 1024
DFF = 1408
PD = 256
EPS = 1e-6
NCORES = 8
ENGS = ["pe", "act", "dve", "pool", "sp"]


class DmaGroup:
    def __init__(self, sem):
        self.sem = sem
        self.cnt = 0


class Tile:
    def __init__(self, ap, name):
        self.ap = ap
        self.name = name
        self.w = None
        self.r = {}
        self.dg = None

    def v(self, dt, *shape):
        n = 1
        for s in shape:
            n *= s
        nb = n * (4 if dt in (F32, I32) else 2) // 2
        a = self.ap[:, 0:nb]
        if dt != BF:
            a = a.bitcast(dt)
        if len(shape) == 2:
            return a.rearrange("p (c n) -> p c n", c=shape[0])
        if len(shape) == 3:
            return a.rearrange("p (a c n) -> p a c n", a=shape[0], c=shape[1])
        return a


class Prog:
    def __init__(self, nc, stack):
        self.nc = nc
        self.stack = stack
        self.ops = {e: [] for e in ENGS}
        self.seen = {e: {} for e in ENGS}
        self.pending = {e: [] for e in ENGS}
        self.esem = {e: stack.enter_context(nc.semaphore("sem_" + e)) for e in ENGS}
        self.dgs = []
        self.nsem = 0
        self.top = 0
        self.big = None
        self.banks = []
        self.bank_i = 0
        self._start = {}
        self.free_dgs = []
        self.phase_dgs = []

    def init_mem(self, kb=206):
        nc = self.nc
        self.big = self.stack.enter_context(nc.sbuf_tensor("big", [128, kb * 512], BF))
        self.cap = kb * 512
        for i in range(8):
            t = self.stack.enter_context(nc.psum_tensor("ps%d" % i, [128, 512], F32))
            self.banks.append(Tile(t, "bank%d" % i))

    def alloc(self, name, nbytes):
        n = (nbytes + 63) // 64 * 32
        assert self.top + n <= self.cap, ("SBUF overflow", name, self.top, n, self.cap)
        t = Tile(self.big[:, self.top:self.top + n], name)
        self._start[id(t)] = self.top
        self.top += n
        return t

    def span(self, tiles, dt, *shape):
        n = sum(t.ap.shape[1] for t in tiles)
        st = self._start[id(tiles[0])]
        a = self.big[:, st:st + n]
        if dt != BF:
            a = a.bitcast(dt)
        if len(shape) == 2:
            return a.rearrange("p (c n) -> p c n", c=shape[0])
        return a

    def bank(self):
        b = self.banks[self.bank_i % len(self.banks)]
        self.bank_i += 1
        return b

    def new_dg(self, tiles):
        if self.free_dgs:
            dg = self.free_dgs.pop()
        else:
            sem = self.stack.enter_context(self.nc.semaphore("dsem%d" % self.nsem))
            self.nsem += 1
            dg = DmaGroup(sem)
            self.dgs.append(dg)
        self.phase_dgs.append(dg)
        for t in tiles:
            t.dg = dg
        return dg

    def recycle(self):
        self.free_dgs.extend(self.phase_dgs)
        self.phase_dgs = []

    def _add(self, eng, fn, deps, dma=None):
        waits = []
        deps = list(deps) + self.pending[eng]
        self.pending[eng] = []
        for d in deps:
            if d is None:
                continue
            if d[0] == "e":
                _, e2, i2 = d
                if e2 == eng and eng in ("pe", "sp"):
                    continue
                if self.seen[eng].get(e2, -1) >= i2:
                    continue
                self.seen[eng][e2] = i2
                waits.append(d)
            else:
                _, dg, cnt = d
                if self.seen[eng].get(id(dg), -1) >= cnt:
                    continue
                self.seen[eng][id(dg)] = cnt
                waits.append(d)
        idx = len(self.ops[eng])
        self.ops[eng].append({"fn": fn, "waits": waits, "inc": False, "dma": dma})
        return ("e", eng, idx)

    def op(self, eng, fn, reads=(), writes=()):
        deps = []
        for t in reads:
            deps.append(t.w)
        for t in writes:
            deps.append(t.w)
            deps.extend(t.r.values())
        me = self._add(eng, fn, deps)
        for t in reads:
            t.r[eng] = me
        for t in writes:
            t.w = me
            t.r = {}
        return me

    def load(self, q, tiles, out_ap, in_ap, **kw):
        dg = tiles[0].dg
        assert dg is not None and all(t.dg is dg for t in tiles)
        deps = []
        for t in tiles:
            if not (t.w is not None and t.w[0] == "d" and t.w[1] is dg):
                deps.append(t.w)
            deps.extend(t.r.values())
        dg.cnt += 16
        self._add(q, lambda e: e.dma_start(out=out_ap, in_=in_ap, **kw), deps, dma=dg)
        me = ("d", dg, dg.cnt)
        for t in tiles:
            t.w = me
            t.r = {}

    def store(self, q, tiles, out_ap, in_ap, **kw):
        dg = tiles[0].dg
        assert dg is not None and all(t.dg is dg for t in tiles)
        deps = [t.w for t in tiles]
        dg.cnt += 16
        self._add(q, lambda e: e.dma_start(out=out_ap, in_=in_ap, **kw), deps, dma=dg)
        me = ("d", dg, dg.cnt)
        for t in tiles:
            t.r["dma"] = me

    def barrier(self):
        deps = [("d", dg, dg.cnt) for dg in self.dgs if dg.cnt > 0]
        for e in ENGS:
            if e != "sp":
                for i in range(len(self.ops[e]) - 1, -1, -1):
                    if self.ops[e][i]["dma"] is None:
                        deps.append(("e", e, i))
                        break
        me = self._add("sp", lambda e: e.nop(), deps)
        for e in ENGS:
            if e != "sp":
                self.pending[e].append(me)

    def emit(self):
        nc = self.nc
        for e in ENGS:
            for o in self.ops[e]:
                for d in o["waits"]:
                    if d[0] == "e":
                        self.ops[d[1]][d[2]]["inc"] = True
        for e in ENGS:
            c = 0
            for o in self.ops[e]:
                if o["inc"]:
                    assert o["dma"] is None
                    c += 1
                o["cnt"] = c

        def replay(name, eng):
            sem = self.esem[name]
            for o in self.ops[name]:
                for d in o["waits"]:
                    if d[0] == "e":
                        eng.wait_ge(self.esem[d[1]], self.ops[d[1]][d[2]]["cnt"])
                    else:
                        eng.wait_ge(d[1].sem, d[2])
                ins = o["fn"](eng)
                if o["dma"] is not None:
                    ins.then_inc(o["dma"].sem, 16)
                elif o["inc"]:
                    ins.then_inc(sem, 1)

        with nc.Block() as block:
            @block.tensor
            def _(e):
                replay("pe", e)

            @block.scalar
            def _(e):
                replay("act", e)

            @block.vector
            def _(e):
                replay("dve", e)

            @block.gpsimd
            def _(e):
                replay("pool", e)

            @block.sync
            def _(e):
                replay("sp", e)

    def mm(self, out_t, out_ap, lt, l_ap, rt, r_ap, start, stop):
        self.op("pe", lambda e: e.matmul(out_ap, l_ap, r_ap, start=start, stop=stop), reads=[lt, rt], writes=[out_t])

    def act(self, out_t, out_ap, in_t, in_ap, func, scale=None, bias=None, extra_reads=()):
        kw = {}
        if scale is not None:
            kw["scale"] = scale
        if bias is not None:
            kw["bias"] = bias
        self.op("act", lambda e: e.activation(out=out_ap, in_=in_ap, func=func, **kw),
                reads=[in_t] + list(extra_reads), writes=[out_t])

    def tt(self, eng, out_t, out_ap, a_t, a_ap, b_t, b_ap, op):
        self.op(eng, lambda e: e.tensor_tensor(out=out_ap, in0=a_ap, in1=b_ap, op=op), reads=[a_t, b_t], writes=[out_t])

    def stt(self, out_t, out_ap, a_t, a_ap, scalar, b_t, b_ap, op0, op1, extra_reads=()):
        self.op("dve", lambda e: e.scalar_tensor_tensor(out=out_ap, in0=a_ap, scalar=scalar, in1=b_ap, op0=op0, op1=op1),
                reads=[a_t, b_t] + list(extra_reads), writes=[out_t])

    def ts(self, eng, out_t, out_ap, a_t, a_ap, s1, op0, s2=None, op1=None, extra_reads=()):
        if op1 is None:
            f = lambda e: e.tensor_scalar(out=out_ap, in0=a_ap, scalar1=s1, scalar2=None, op0=op0)
        else:
            f = lambda e: e.tensor_scalar(out=out_ap, in0=a_ap, scalar1=s1, scalar2=s2, op0=op0, op1=op1)
        self.op(eng, f, reads=[a_t] + list(extra_reads), writes=[out_t])

    def copy(self, eng, out_t, out_ap, in_t, in_ap):
        if eng == "act":
            self.op("act", lambda e: e.copy(out=out_ap, in_=in_ap), reads=[in_t], writes=[out_t])
        else:
            self.op(eng, lambda e: e.tensor_copy(out=out_ap, in_=in_ap), reads=[in_t], writes=[out_t])

    def memset(self, eng, t, ap, val):
        self.op(eng, lambda e: e.memset(ap, val), writes=[t])


C_ID, C_LE, C_LT, C_NGE, C_U, C_U2, C_BM, C_IF, C_SG, C_E127, CW = 0, 128, 256, 384, 512, 640, 768, 1024, 1025, 1026, 1027


def make_consts():
    c = np.zeros((128, CW), np.float32)
    k = np.arange(128)[:, None]
    q = np.arange(128)[None, :]
    c[:, C_ID:C_ID + 128] = (k == q)
    c[:, C_LE:C_LE + 128] = (k <= q)
    c[:, C_LT:C_LT + 128] = (k < q)
    c[:, C_NGE:C_NGE + 128] = -1.0 * (k >= q)
    c[:, C_U:C_U + 128] = (-1.0 / 16.0) * (k <= q)
    c[:, C_U2:C_U2 + 128] = (-1.0 / 16.0) * (k > q)
    bm = np.zeros((128, 256), np.float32)
    bm[:64, :128] = 1.0
    bm[64:, 128:] = 1.0
    c[:, C_BM:C_BM + 256] = bm
    inv_freq = (1.0 / (10000.0 ** (np.arange(32, dtype=np.float32) / 32.0))).astype(np.float32)
    c[:, C_IF] = np.tile(inv_freq, 4) / (2.0 * np.pi)
    c[:, C_SG] = np.tile(np.concatenate([-np.ones(32), np.ones(32)]), 2)
    c[127, C_E127] = -1.0
    return c


def gcol_layer(l, which):
    return l * 32 + which * 8


def build(S, NL, dbg=False, stop=None):
    NG = S // 512
    NT = S // 128
    NE = (NL + 1) // 2
    NO = NL // 2
    GC_FIN = NL * 32
    GC_EV = GC_FIN + 8
    GW = GC_EV + 7 * NE
    nc = bass.Bass("TRN2", target_bir_lowering=False)

    def din(name, shape, dt=F32):
        return nc.dram_tensor(name, list(shape), dt, kind="ExternalInput").ap()

    def dscr(name, shape, dt):
        return nc.dram_tensor(name, list(shape), dt, kind=("ExternalOutput" if dbg else "Internal")).ap()

    x_in = din("x", [S, D])
    p_in = din("p", [NL, S, PD])
    pos_in = din("pos", [1, S], I32)
    consts_in = din("consts", [128, CW])
    gains_in = din("gains", [128, GW])
    wgu_in = din("wgu", [NL * 2, 128, 8, 2 * DFF])
    wdn_in = din("wdn", [NL * 2, 128, 11, D])
    pleg_in = din("pleg", [NL, 128, 8, D])
    plep_in = din("plep", [NL, 128, 2, D])
    win_in = din("win", [NE, 128, 8, 2448])
    wuq_in = din("wuq", [NE, 128, 2, 1024])
    wukv_in = din("wukv", [NE, 128, 1024])
    wg2_in = din("wg2", [NE, 32, 256])
    woute_in = din("woute", [NE, 128, 8, D])
    if NO > 0:
        wqkv_in = din("wqkv", [NO, 128, 8, 3 * D])
        wouto_in = din("wouto", [NO, 128, 8, D])
    y_out = nc.dram_tensor("y", [S, D], F32, kind="ExternalOutput").ap()

    xT = dscr("xT", [D, S], F32)
    oT = dscr("oT", [D, S], BF)
    cosT = dscr("cosT", [128, S], F32)
    sinT = dscr("sinT", [128, S], F32)
    qnT = dscr("qnT", [512, S], BF)
    qrT = dscr("qrT", [256, S], BF)
    knT = dscr("knT", [512, S], BF)
    krT = dscr("krT", [128, S], BF)
    vm = dscr("vm", [S, 512], BF)
    gqT = dscr("gqT", [256, S], F32)
    gkT = dscr("gkT", [256, S], F32)
    gkt = dscr("gkt", [S, 256], F32)
    gv = dscr("gv", [S, 512], BF)
    gaT = dscr("gaT", [32, S], BF)
    sgrT = dscr("sgrT", [512, S], BF)
    sqT = dscr("sqT", [D, S], BF)
    skT = dscr("skT", [D, S], BF)
    sv = dscr("sv", [S, D], BF)

    stack = ExitStack()
    with stack:
        P = Prog(nc, stack)
        P.init_mem()

        cst = P.alloc("consts", CW * 4)
        gains = P.alloc("gains", GW * 4)
        cbf = P.alloc("cbf", 1280 * 2)
        P.new_dg([cst])
        P.new_dg([gains])
        cst_f = cst.v(F32, CW)
        gains_f = gains.v(F32, GW)
        cbf_b = cbf.v(BF, 1280)
        B_LE, B_LT, B_NGE, B_ONE, B_NEG, B_ZERO = 0, 128, 256, 384, 512, 640
        P.load("sp", [cst], cst_f, consts_in[:, :])
        P.load("sp", [gains], gains_f, gains_in[:, :])
        P.copy("dve", cbf, cbf_b[:, 0:384], cst, cst_f[:, C_LE:C_LE + 384])
        P.memset("dve", cbf, cbf_b[:, B_ONE:B_ONE + 128], 1.0)
        P.memset("dve", cbf, cbf_b[:, B_NEG:B_NEG + 128], -1.0)
        P.memset("dve", cbf, cbf_b[:, B_ZERO:B_ZERO + 512], 0.0)
        ident = cst_f[:, C_ID:C_ID + 128]
        base_top = P.top
        P.phase_dgs = []

        def gcol(c):
            return gains_f[:, c:c + 1]

        def grp(ap2d, g):
            return ap2d[:, g * 512:(g + 1) * 512]

        def prologue():
            P.top = base_top
            CH = min(S, 2048)
            pi_t = P.alloc("posi", CH * 4)
            t_u = P.alloc("u", CH * 4)
            t_t = P.alloc("t", CH * 4)
            t_k = P.alloc("k", CH * 4)
            t_f = P.alloc("f", CH * 4)
            t_m = P.alloc("m", CH * 4)
            t_o = P.alloc("o", CH * 4)
            P.new_dg([pi_t])
            P.new_dg([t_o])
            for c0 in range(0, S, CH):
                P.load("sp", [pi_t], pi_t.v(I32, CH), pos_in[0:1, c0:c0 + CH].partition_broadcast(128)[:, 0, :])
                P.copy("dve", t_u, t_u.v(F32, CH), pi_t, pi_t.v(I32, CH))
                P.ts("dve", t_u, t_u.v(F32, CH), t_u, t_u.v(F32, CH), cst_f[:, C_IF:C_IF + 1], ALU.mult, extra_reads=[cst])
                for shift, dst, signed in ((0.0, sinT, True), (0.25, cosT, False)):
                    P.ts("dve", t_t, t_t.v(F32, CH), t_u, t_u.v(F32, CH), shift, ALU.add)
                    P.copy("dve", t_k, t_k.v(I32, CH), t_t, t_t.v(F32, CH))
                    P.copy("dve", t_f, t_f.v(F32, CH), t_k, t_k.v(I32, CH))
                    P.tt("dve", t_f, t_f.v(F32, CH), t_t, t_t.v(F32, CH), t_f, t_f.v(F32, CH), ALU.subtract)
                    P.ts("dve", t_m, t_m.v(F32, CH), t_f, t_f.v(F32, CH), 0.5, ALU.is_gt)
                    P.tt("dve", t_f, t_f.v(F32, CH), t_f, t_f.v(F32, CH), t_m, t_m.v(F32, CH), ALU.subtract)
                    P.ts("dve", t_m, t_m.v(F32, CH), t_f, t_f.v(F32, CH), -0.5, ALU.is_lt)
                    P.tt("dve", t_f, t_f.v(F32, CH), t_f, t_f.v(F32, CH), t_m, t_m.v(F32, CH), ALU.add)
                    P.act(t_o, t_o.v(F32, CH), t_f, t_f.v(F32, CH), AF.Sin, scale=2.0 * np.pi * (1.0 - 1e-6))
                    if signed:
                        P.ts("dve", t_o, t_o.v(F32, CH), t_o, t_o.v(F32, CH), cst_f[:, C_SG:C_SG + 1], ALU.mult, extra_reads=[cst])
                    P.store("sp", [t_o], dst[:, c0:c0 + CH], t_o.v(F32, CH))
            P.barrier()

        def load_w(name, dram_ap, kc, cols):
            t = P.alloc(name, kc * cols * 2)
            P.new_dg([t])
            v = t.v(BF, kc, cols)
            nsp = 1
            while cols // nsp > 2048 or cols % nsp:
                nsp += 1
            w = cols // nsp
            for k in range(kc):
                for s_ in range(nsp):
                    P.load("pool", [t], v[:, k, s_ * w:(s_ + 1) * w], dram_ap[:, k, s_ * w:(s_ + 1) * w])
            return t, v

        def norm_stats(xc, xcv, nchunk, nfeat, sq, rs_bank, tmp, W=512):
            for c in range(nchunk):
                P.act(sq[c][0], sq[c][1], xc[c], xcv[c], AF.Square)
            ssum = P.bank()
            for c in range(nchunk):
                P.mm(ssum, ssum.ap[:, 0:W], cbf, cbf_b[:, B_ONE:B_ONE + 128], sq[c][0], sq[c][1], c == 0, c == nchunk - 1)
            P.act(tmp[0], tmp[1], ssum, ssum.ap[:, 0:W], AF.Ln, scale=1.0 / nfeat, bias=EPS)
            P.act(rs_bank, rs_bank.ap[:, 0:W], tmp[0], tmp[1], AF.Exp, scale=-0.5)

        GS = 256
        NGS = S // GS
        TT = GS // 128

        def alloc_lane(tag):
            L = {}
            L["xct"] = [P.alloc("xc%s%d" % (tag, c), GS * 4) for c in range(8)]
            P.new_dg(L["xct"])
            L["xcv"] = [t.v(F32, GS) for t in L["xct"]]
            L["xn"] = [P.alloc("xn%s%d" % (tag, c), GS * 2) for c in range(8)]
            L["xnv"] = [t.v(BF, GS) for t in L["xn"]]
            L["actt"] = [P.alloc("act%s%d" % (tag, c), GS * 2) for c in range(11)]
            L["actv"] = [t.v(BF, GS) for t in L["actt"]]
            L["sg"] = [P.alloc("sg%s%d" % (tag, c), GS * 4) for c in range(2)]
            t = P.alloc("ntmp" + tag, GS * 4)
            L["tmp"] = (t, t.v(F32, GS))
            return L

        def norm_xn(L, gc):
            xct, xcv, xn, xnv, actt, actv, rsb = L["xct"], L["xcv"], L["xn"], L["xnv"], L["actt"], L["actv"], L["rsb"]
            norm_stats(xct, xcv, 8, D, [(actt[c], actv[c]) for c in range(8)], rsb, L["tmp"], W=GS)
            for c in range(8):
                P.stt(xn[c], xnv[c], xct[c], xcv[c], gcol(gc + c), rsb, rsb.ap[:, 0:GS], ALU.mult, ALU.mult, extra_reads=[gains])

        def ffn_gen(L, W, gc):
            xct, xcv, xn, xnv, actt, actv, sg = L["xct"], L["xcv"], L["xn"], L["xnv"], L["actt"], L["actv"], L["sg"]
            wgu_t, wgu_v, wdn_t, wdn_v = W
            norm_xn(L, gc)
            yield
            for f in range(11):
                pg = P.bank()
                pu = P.bank()
                for k in range(8):
                    P.mm(pg, pg.ap[:, 0:GS], wgu_t, wgu_v[:, k, f * 128:(f + 1) * 128], xn[k], xnv[k], k == 0, k == 7)
                for k in range(8):
                    P.mm(pu, pu.ap[:, 0:GS], wgu_t, wgu_v[:, k, DFF + f * 128:DFF + (f + 1) * 128], xn[k], xnv[k], k == 0, k == 7)
                s = sg[f % 2]
                P.act(s, s.v(F32, GS), pg, pg.ap[:, 0:GS], AF.Silu)
                P.tt("dve", actt[f], actv[f], s, s.v(F32, GS), pu, pu.ap[:, 0:GS], ALU.mult)
            yield
            for m in range(8):
                pd = P.bank()
                for f in range(11):
                    P.mm(pd, pd.ap[:, 0:GS], wdn_t, wdn_v[:, f, m * 128:(m + 1) * 128], actt[f], actv[f], f == 0, f == 10)
                P.stt(xct[m], xcv[m], pd, pd.ap[:, 0:GS], 0.5, xct[m], xcv[m], ALU.mult, ALU.add)
            yield

        def run_lanes(g0, g1):
            a0 = next(g0, "END") != "END"
            a1 = True
            while a0 or a1:
                if a1:
                    a1 = next(g1, "END") != "END"
                if a0:
                    a0 = next(g0, "END") != "END"

        xcd = xT.rearrange("(c p) s -> p c s", p=128)

        def phase_A(l):
            P.top = base_top
            even = (l % 2 == 0)
            j = l // 2
            lanes = [alloc_lane("a"), alloc_lane("b")]
            W = load_w("wgu", wgu_in[2 * l], 8, 2 * DFF) + load_w("wdn", wdn_in[2 * l], 11, D)
            if even:
                win_t, win_v = load_w("win", win_in[j], 8, 2448)
                wuq_t, wuq_v = load_w("wuq", wuq_in[j], 2, 1024)
                wukv_t = P.alloc("wukv", 2048)
                P.new_dg([wukv_t])
                wukv_v = wukv_t.v(BF, 1024)
                P.load("pool", [wukv_t], wukv_v, wukv_in[j])
            else:
                wqkv_t, wqkv_v = load_w("wqkv", wqkv_in[j], 8, 3 * D)
            for li, L in enumerate(lanes):
                L["rsb"] = P.banks[li]
                L["stg"] = [P.alloc("stg%d_%d" % (li, i), 4096) for i in range(3)]
                for s_ in L["stg"]:
                    P.new_dg([s_])
                L["stg_i"] = 0
                if even:
                    L["cqf"] = [P.alloc("cqf%d_%d" % (li, c), GS * 4) for c in range(2)]
                    L["cqn"] = [P.alloc("cqn%d_%d" % (li, c), GS * 2) for c in range(2)]
                    L["ckvf"] = P.alloc("ckvf%d" % li, GS * 4)
                    L["ckvn"] = P.alloc("ckvn%d" % li, GS * 2)
                    L["cs"] = P.alloc("cos%d" % li, GS * 4)
                    L["sn"] = P.alloc("sin%d" % li, GS * 4)
                    P.new_dg([L["cs"], L["sn"]])
                    L["rt"] = [P.alloc("rt%d_%d" % (li, i), GS * 4) for i in range(2)]
                    L["gat"] = P.alloc("gat%d" % li, GS * 2)
                    P.new_dg([L["gat"]])
                    P.memset("dve", L["gat"], L["gat"].v(BF, GS)[0:32, :], 1.0)
            saved_banks = P.banks
            P.banks = saved_banks[2:]
            P.bank_i = 0

            def body(L, groups):
                xct, xcv, xn, xnv, actt, actv = L["xct"], L["xcv"], L["xn"], L["xnv"], L["actt"], L["actv"]
                rsb = L["rsb"]

                def nstg():
                    s_ = L["stg"][L["stg_i"] % 3]
                    L["stg_i"] += 1
                    return s_

                for g in groups:
                    gs = slice(g * GS, (g + 1) * GS)
                    if l == 0:
                        xts = []
                        for tt_ in range(TT):
                            s_ = nstg()
                            P.load("sp", [s_], s_.v(F32, 1024), x_in[g * GS + tt_ * 128: g * GS + (tt_ + 1) * 128, :])
                            xts.append(s_)
                        for c in range(8):
                            b = P.bank()
                            for tt_ in range(TT):
                                s_ = xts[tt_]
                                P.op("pe", lambda e, b=b, tt_=tt_, s_=s_, c=c: e.transpose(b.ap[:, tt_ * 128:(tt_ + 1) * 128], s_.v(F32, 1024)[:, c * 128:(c + 1) * 128], ident),
                                     reads=[s_, cst], writes=[b])
                            P.copy("act" if c % 2 else "dve", xct[c], xcv[c], b, b.ap[:, 0:GS])
                    else:
                        P.load("sp", xct, P.span(xct, F32, 8, GS), xcd[:, :, gs])
                    yield from ffn_gen(L, W, gcol_layer(l, 0))
                    P.store("sp", xct, xcd[:, :, gs], P.span(xct, F32, 8, GS))
                    norm_xn(L, gcol_layer(l, 1))
                    yield
                    if not even:
                        for kind, dst, sc in ((0, sqT, 128.0 ** -0.5), (1, skT, 1.0)):
                            for hh in range(0, 8, 4):
                                s_ = nstg()
                                sv_ = s_.v(BF, 4, GS)
                                for h4 in range(4):
                                    h = hh + h4
                                    b = P.bank()
                                    for k in range(8):
                                        P.mm(b, b.ap[:, 0:GS], wqkv_t, wqkv_v[:, k, kind * D + h * 128: kind * D + (h + 1) * 128], xn[k], xnv[k], k == 0, k == 7)
                                    P.act(s_, sv_[:, h4, :], b, b.ap[:, 0:GS], AF.Identity, scale=sc)
                                P.store("sp", [s_], dst.rearrange("(c p) s -> p c s", p=128)[:, hh:hh + 4, gs], sv_)
                            yield
                        for tt_ in range(TT):
                            s_ = nstg()
                            sv_ = s_.v(BF, 1024)
                            for half in range(2):
                                b = P.bank()
                                for k in range(8):
                                    P.mm(b, b.ap[:, :], xn[k], xnv[k][:, tt_ * 128:(tt_ + 1) * 128], wqkv_t, wqkv_v[:, k, 2 * D + half * 512: 2 * D + (half + 1) * 512], k == 0, k == 7)
                                P.copy("dve" if half else "act", s_, sv_[:, half * 512:(half + 1) * 512], b, b.ap[:, :])
                            P.store("sp", [s_], sv[g * GS + tt_ * 128: g * GS + (tt_ + 1) * 128, :], sv_)
                        yield
                    else:
                        GE = GC_EV + 7 * j
                        cqf, cqn, ckvf, ckvn, cs_t, sn_t, rt, gat = L["cqf"], L["cqn"], L["ckvf"], L["ckvn"], L["cs"], L["sn"], L["rt"], L["gat"]
                        P.load("sp", [cs_t, sn_t], cs_t.v(F32, GS), cosT[:, gs])
                        P.load("sp", [cs_t, sn_t], sn_t.v(F32, GS), sinT[:, gs])

                        def fm(chunk, ncols=128):
                            b = P.bank()
                            for k in range(8):
                                P.mm(b, b.ap[0:ncols, 0:GS], win_t, win_v[:, k, chunk * 128: chunk * 128 + ncols], xn[k], xnv[k], k == 0, k == 7)
                            return b

                        for c in range(2):
                            b = fm(c)
                            P.copy("dve", cqf[c], cqf[c].v(F32, GS), b, b.ap[:, 0:GS])
                        b = fm(2)
                        P.copy("dve", ckvf, ckvf.v(F32, GS), b, b.ap[:, 0:GS])
                        rs2 = P.bank()
                        norm_stats(cqf, [t.v(F32, GS) for t in cqf], 2, 256, [(actt[c], actv[c]) for c in range(2)], rs2, L["tmp"], W=GS)
                        for c in range(2):
                            P.stt(cqn[c], cqn[c].v(BF, GS), cqf[c], cqf[c].v(F32, GS), gcol(GE + c), rs2, rs2.ap[:, 0:GS], ALU.mult, ALU.mult, extra_reads=[gains])
                        rs3 = P.bank()
                        norm_stats([ckvf], [ckvf.v(F32, GS)], 1, 128, [(actt[2], actv[2])], rs3, L["tmp"], W=GS)
                        P.stt(ckvn, ckvn.v(BF, GS), ckvf, ckvf.v(F32, GS), gcol(GE + 2), rs3, rs3.ap[:, 0:GS], ALU.mult, ALU.mult, extra_reads=[gains])
                        yield

                        def rope(b_a, b_s, out_t, out_ap):
                            P.tt("dve", rt[0], rt[0].v(F32, GS), b_a, b_a.ap[:, 0:GS], cs_t, cs_t.v(F32, GS), ALU.mult)
                            P.tt("dve", rt[1], rt[1].v(F32, GS), b_s, b_s.ap[:, 0:GS], sn_t, sn_t.v(F32, GS), ALU.mult)
                            P.tt("pool", out_t, out_ap, rt[0], rt[0].v(F32, GS), rt[1], rt[1].v(F32, GS), ALU.add)

                        b3 = fm(3)
                        b4 = fm(4)
                        s_ = nstg()
                        rope(b3, b4, s_, s_.v(BF, GS))
                        P.store("sp", [s_], krT[:, gs], s_.v(BF, GS))
                        for c0, dst, sc in ((5, gqT, 64.0 ** -0.5), (7, gkT, 1.0)):
                            s_ = nstg()
                            sv_ = s_.v(F32, 2, GS)
                            for c in range(2):
                                b = fm(c0 + c)
                                P.act(s_, sv_[:, c, :], b, b.ap[:, 0:GS], AF.Identity, scale=sc)
                            P.store("sp", [s_], dst.rearrange("(c p) s -> p c s", p=128)[:, :, gs], sv_)
                        yield
                        s_ = nstg()
                        sv_ = s_.v(BF, 4, GS)
                        for c in range(4):
                            b = fm(9 + c)
                            P.act(s_, sv_[:, c, :], b, b.ap[:, 0:GS], AF.Silu)
                        P.store("sp", [s_], sgrT.rearrange("(c p) s -> p c s", p=128)[:, :, gs], sv_)
                        b = fm(13, 16)
                        P.copy("dve", gat, gat.v(BF, GS)[0:16, :], b, b.ap[0:16, 0:GS])
                        P.store("sp", [gat], gaT[:, gs], gat.v(BF, GS)[0:32, :])
                        yield
                        sk_ = nstg()
                        svv_ = nstg()
                        for tt_ in range(TT):
                            b1 = P.bank()
                            b2 = P.bank()
                            for k in range(8):
                                P.mm(b1, b1.ap[:, 0:256], xn[k], xnv[k][:, tt_ * 128:(tt_ + 1) * 128], win_t, win_v[:, k, 1680:1936], k == 0, k == 7)
                            for k in range(8):
                                P.mm(b2, b2.ap[:, :], xn[k], xnv[k][:, tt_ * 128:(tt_ + 1) * 128], win_t, win_v[:, k, 1936:2448], k == 0, k == 7)
                            P.copy("act", sk_, sk_.v(F32, TT, 256)[:, tt_, :], b1, b1.ap[:, 0:256])
                            P.copy("dve", svv_, svv_.v(BF, TT, 512)[:, tt_, :], b2, b2.ap[:, :])
                        P.store("sp", [sk_], gkt[gs, :].rearrange("(t p) c -> p t c", p=128), sk_.v(F32, TT, 256))
                        P.store("sp", [svv_], gv[gs, :].rearrange("(t p) c -> p t c", p=128), svv_.v(BF, TT, 512))
                        yield

                        def uq(chunk):
                            b = P.bank()
                            for k in range(2):
                                P.mm(b, b.ap[:, 0:GS], wuq_t, wuq_v[:, k, chunk * 128:(chunk + 1) * 128], cqn[k], cqn[k].v(BF, GS), k == 0, k == 1)
                            return b
                        s_ = nstg()
                        sv_ = s_.v(BF, 4, GS)
                        for h in range(4):
                            b = uq(h)
                            P.copy("act", s_, sv_[:, h, :], b, b.ap[:, 0:GS])
                        P.store("sp", [s_], qnT.rearrange("(c p) s -> p c s", p=128)[:, :, gs], sv_)
                        s_ = nstg()
                        sv_ = s_.v(BF, 2, GS)
                        for pr in range(2):
                            ba = uq(4 + pr)
                            bs = uq(6 + pr)
                            rope(ba, bs, s_, sv_[:, pr, :])
                        P.store("sp", [s_], qrT.rearrange("(c p) s -> p c s", p=128)[:, :, gs], sv_)
                        s_ = nstg()
                        sv_ = s_.v(BF, 4, GS)
                        for h in range(4):
                            b = P.bank()
                            P.mm(b, b.ap[:, 0:GS], wukv_t, wukv_v[:, h * 128:(h + 1) * 128], ckvn, ckvn.v(BF, GS), True, True)
                            P.copy("act", s_, sv_[:, h, :], b, b.ap[:, 0:GS])
                        P.store("sp", [s_], knT.rearrange("(c p) s -> p c s", p=128)[:, :, gs], sv_)
                        s_ = nstg()
                        sv_ = s_.v(BF, TT, 512)
                        for tt_ in range(TT):
                            b = P.bank()
                            P.mm(b, b.ap[:, :], ckvn, ckvn.v(BF, GS)[:, tt_ * 128:(tt_ + 1) * 128], wukv_t, wukv_v[:, 512:1024], True, True)
                            P.copy("dve", s_, sv_[:, tt_, :], b, b.ap[:, :])
                        P.store("sp", [s_], vm[gs, :].rearrange("(t p) c -> p t c", p=128), sv_)
                        yield

            run_lanes(body(lanes[0], range(0, NGS, 2)), body(lanes[1], range(1, NGS, 2)))
            P.banks = saved_banks
            P.barrier()

        def phase_B(l):
            P.top = base_top
            even = (l % 2 == 0)
            j = l // 2
            last = (l == NL - 1)
            lanes = [alloc_lane("a"), alloc_lane("b")]
            W = load_w("wgu", wgu_in[2 * l + 1], 8, 2 * DFF) + load_w("wdn", wdn_in[2 * l + 1], 11, D)
            wo_t, wo_v = load_w("wo", (woute_in if even else wouto_in)[j], 8, D)
            wg_t, wg_v = load_w("pleg", pleg_in[l], 8, D)
            wp_t, wp_v = load_w("plep", plep_in[l], 2, D)
            for li, L in enumerate(lanes):
                L["rsb"] = P.banks[li]
                L["ot"] = [P.alloc("ot%d_%d" % (li, c), GS * 2) for c in range(8)]
                P.new_dg(L["ot"])
                L["pin"] = P.alloc("pin%d" % li, TT * 256 * 4)
                P.new_dg([L["pin"]])
                L["pT"] = [P.alloc("pT%d_%d" % (li, c), GS * 2) for c in range(2)]
                L["sig"] = [P.alloc("sig%d_%d" % (li, c), GS * 4) for c in range(2)]
                L["pt2"] = [P.alloc("pt2%d_%d" % (li, c), GS * 4) for c in range(2)]
                if last:
                    L["yt"] = [P.alloc("yt%d_%d" % (li, c), 4096) for c in range(2)]
                    for t in L["yt"]:
                        P.new_dg([t])
            saved_banks = P.banks
            P.banks = saved_banks[2:]
            P.bank_i = 0
            ocd = oT.rearrange("(c p) s -> p c s", p=128)

            def body(L, groups):
                xct, xcv, xn, xnv, actt, actv = L["xct"], L["xcv"], L["xn"], L["xnv"], L["actt"], L["actv"]
                rsb, ot, pin, pT, sig, pt2 = L["rsb"], L["ot"], L["pin"], L["pT"], L["sig"], L["pt2"]
                for g in groups:
                    gs = slice(g * GS, (g + 1) * GS)
                    P.load("sp", xct, P.span(xct, F32, 8, GS), xcd[:, :, gs])
                    P.load("sp", ot, P.span(ot, BF, 8, GS), ocd[:, :, gs])
                    P.load("sp", [pin], pin.v(F32, TT, 256), p_in[l, gs, :].rearrange("(t p) c -> p t c", p=128))
                    for m in range(8):
                        b = P.bank()
                        for k in range(8):
                            P.mm(b, b.ap[:, 0:GS], wo_t, wo_v[:, k, m * 128:(m + 1) * 128], ot[k], ot[k].v(BF, GS), k == 0, k == 7)
                        P.tt("dve", xct[m], xcv[m], b, b.ap[:, 0:GS], xct[m], xcv[m], ALU.add)
                    yield
                    yield from ffn_gen(L, W, gcol_layer(l, 2))
                    norm_xn(L, gcol_layer(l, 3))
                    for c in range(2):
                        b = P.bank()
                        for tt_ in range(TT):
                            P.op("pe", lambda e, b=b, tt_=tt_, c=c: e.transpose(b.ap[:, tt_ * 128:(tt_ + 1) * 128], pin.v(F32, TT, 256)[:, tt_, c * 128:(c + 1) * 128], ident),
                                 reads=[pin, cst], writes=[b])
                        P.copy("act", pT[c], pT[c].v(BF, GS), b, b.ap[:, 0:GS])
                    yield
                    for m in range(8):
                        bg = P.bank()
                        bp = P.bank()
                        for k in range(8):
                            P.mm(bg, bg.ap[:, 0:GS], wg_t, wg_v[:, k, m * 128:(m + 1) * 128], xn[k], xnv[k], k == 0, k == 7)
                        for k in range(2):
                            P.mm(bp, bp.ap[:, 0:GS], wp_t, wp_v[:, k, m * 128:(m + 1) * 128], pT[k], pT[k].v(BF, GS), k == 0, k == 1)
                        s_ = sig[m % 2]
                        t2 = pt2[m % 2]
                        P.act(s_, s_.v(F32, GS), bg, bg.ap[:, 0:GS], AF.Sigmoid)
                        P.tt("dve", t2, t2.v(F32, GS), s_, s_.v(F32, GS), bp, bp.ap[:, 0:GS], ALU.mult)
                        P.tt("pool", xct[m], xcv[m], xct[m], xcv[m], t2, t2.v(F32, GS), ALU.add)
                    if not last:
                        P.store("sp", xct, xcd[:, :, gs], P.span(xct, F32, 8, GS))
                        yield
                    else:
                        yield
                        norm_stats(xct, xcv, 8, D, [(actt[c], actv[c]) for c in range(8)], rsb, L["tmp"], W=GS)
                        for c in range(8):
                            P.stt(xct[c], xcv[c], xct[c], xcv[c], gcol(GC_FIN + c), rsb, rsb.ap[:, 0:GS], ALU.mult, ALU.mult, extra_reads=[gains])
                        yield
                        for tt_ in range(TT):
                            y_ = L["yt"][tt_ % 2]
                            yv = y_.v(F32, 1024)
                            for half in range(2):
                                b = P.bank()
                                for c4 in range(4):
                                    c = half * 4 + c4
                                    P.op("pe", lambda e, b=b, c4=c4, c=c, tt_=tt_: e.transpose(b.ap[:, c4 * 128:(c4 + 1) * 128], xcv[c][:, tt_ * 128:(tt_ + 1) * 128], ident),
                                         reads=[xct[c], cst], writes=[b])
                                P.copy("act" if half else "dve", y_, yv[:, half * 512:(half + 1) * 512], b, b.ap[:, :])
                            P.store("sp", [y_], y_out[g * GS + tt_ * 128: g * GS + (tt_ + 1) * 128, :], yv)
                        yield

            run_lanes(body(lanes[0], range(0, NGS, 2)), body(lanes[1], range(1, NGS, 2)))
            P.banks = saved_banks
            P.barrier()

        def phase_SB():
            P.top = base_top
            KT = [P.alloc("KT%d" % i, S * 2) for i in range(2)]
            VV = [P.alloc("VV%d" % i, S * 2) for i in range(2)]
            for t in KT + VV:
                P.new_dg([t])
            qt = [P.alloc("qt%d" % i, 1024) for i in range(3)]
            for t in qt:
                P.new_dg([t])
            et = [P.alloc("et%d" % i, 2048) for i in range(3)]
            lpt = [P.alloc("lp%d" % i, 1024) for i in range(3)]
            At = [P.alloc("At%d" % i, 1024) for i in range(3)]
            lp2 = [P.alloc("lp2%d" % i, 1024) for i in range(2)]
            ost = [P.alloc("ost%d" % i, 1024) for i in range(2)]
            for t in ost:
                P.new_dg([t])
            Bb = [P.banks[0], P.banks[1], P.banks[2], P.banks[7]]
            Db = P.banks[3:5]
            Ob = P.banks[5:7]
            items = []
            gi = 0
            for h in range(8):
                for g in range(NG):
                    n = 4 * g + 4
                    for ci, c in enumerate(range(n - 1, -1, -1)):
                        r = c - 4 * g
                        q0 = r * 128 if r >= 0 else 0
                        items.append(dict(h=h, g=g, c=c, q0=q0, diag=(r >= 0), first=(ci == 0), last=(c == 0), gi=gi))
                    gi += 1
            state = {}

            def s1(i, it):
                h, g, c, q0 = it["h"], it["g"], it["c"], it["q0"]
                if it["first"]:
                    def ld_kv(hh):
                        P.load("sp", [KT[hh % 2]], KT[hh % 2].v(BF, S), skT[hh * 128:(hh + 1) * 128, :])
                        P.load("sp", [VV[hh % 2]], VV[hh % 2].v(BF, NT, 128), sv[:, hh * 128:(hh + 1) * 128].rearrange("(n p) d -> p n d", p=128))

                    def ld_q(gg):
                        h_, g_ = gg // NG, gg % NG
                        P.load("sp", [qt[gg % 3]], qt[gg % 3].v(BF, 512), sqT[h_ * 128:(h_ + 1) * 128, g_ * 512:(g_ + 1) * 512])
                    if it["gi"] == 0:
                        ld_kv(0)
                        ld_q(0)
                    if it["gi"] + 1 < 8 * NG:
                        ld_q(it["gi"] + 1)
                if g == 0 and c == 0 and h + 1 < 8:
                    P.load("sp", [KT[(h + 1) % 2]], KT[(h + 1) % 2].v(BF, S), skT[(h + 1) * 128:(h + 2) * 128, :])
                    P.load("sp", [VV[(h + 1) % 2]], VV[(h + 1) % 2].v(BF, NT, 128), sv[:, (h + 1) * 128:(h + 2) * 128].rearrange("(n p) d -> p n d", p=128))
                kt = KT[h % 2]
                q_ = qt[it["gi"] % 3]
                B = Bb[i % 4]
                P.mm(B, B.ap[:, q0:512], kt, kt.v(BF, S)[:, c * 128:(c + 1) * 128], q_, q_.v(BF, 512)[:, q0:512], True, False)
                e_ = et[i % 3]
                P.act(e_, e_.v(F32, 512)[:, q0:512], B, B.ap[:, q0:512], AF.Exp)

            def s1b(i, it):
                q0 = it["q0"]
                e_ = et[i % 3]
                l_ = lpt[i % 3]
                P.act(l_, l_.v(BF, 512)[:, q0:512], e_, e_.v(F32, 512)[:, q0:512], AF.Ln, bias=1.0)
                if it["diag"]:
                    P.tt("pool", l_, l_.v(BF, 512)[:, q0:q0 + 128], l_, l_.v(BF, 512)[:, q0:q0 + 128], cbf, cbf_b[:, B_LT:B_LT + 128], ALU.mult)

            def s2(i, it):
                h, g, c, q0 = it["h"], it["g"], it["c"], it["q0"]
                B = Bb[i % 4]
                l_ = lpt[i % 3]
                Dk = Db[it["gi"] % 2]
                if it["first"]:
                    P.mm(B, B.ap[:, q0:512], cbf, cbf_b[:, B_NGE:B_NGE + 128], l_, l_.v(BF, 512)[:, q0:512], False, True)
                    P.mm(Dk, Dk.ap[:, :], cbf, cbf_b[:, B_NEG:B_NEG + 128], cbf, cbf_b[:, B_ZERO:B_ZERO + 512], True, False)
                else:
                    l2 = lp2[i % 2]
                    P.stt(l2, l2.v(BF, 512)[:, q0:512], Dk, Dk.ap[:, q0:512], cst_f[:, C_E127:C_E127 + 1], l_, l_.v(BF, 512)[:, q0:512],
                          ALU.mult, ALU.add, extra_reads=[cst])
                    P.mm(B, B.ap[:, q0:512], cbf, cbf_b[:, B_NGE:B_NGE + 128], l2, l2.v(BF, 512)[:, q0:512], False, True)
                if not it["last"]:
                    P.mm(Dk, Dk.ap[:, q0:512], cbf, cbf_b[:, B_NEG:B_NEG + 128], l_, l_.v(BF, 512)[:, q0:512], False, False)
                a_ = At[i % 3]
                P.act(a_, a_.v(BF, 512)[:, q0:512], B, B.ap[:, q0:512], AF.Exp)
                if it["diag"]:
                    P.tt("pool", a_, a_.v(BF, 512)[:, q0:q0 + 128], a_, a_.v(BF, 512)[:, q0:q0 + 128], cbf, cbf_b[:, B_LT:B_LT + 128], ALU.mult)

            def s3(i, it):
                h, g, c, q0 = it["h"], it["g"], it["c"], it["q0"]
                a_ = At[i % 3]
                vt = VV[h % 2]
                O = Ob[it["gi"] % 2]
                if it["first"]:
                    P.mm(O, O.ap[:, :], cbf, cbf_b[:, B_NEG:B_NEG + 128], cbf, cbf_b[:, B_ZERO:B_ZERO + 512], True, False)
                P.mm(O, O.ap[:, q0:512], vt, vt.v(BF, NT, 128)[:, c, :], a_, a_.v(BF, 512)[:, q0:512], False, it["last"])
                if it["last"]:
                    o_ = ost[it["gi"] % 2]
                    P.copy("dve", o_, o_.v(BF, 512), O, O.ap[:, :])
                    P.store("sp", [o_], oT[h * 128:(h + 1) * 128, g * 512:(g + 1) * 512], o_.v(BF, 512))

            n = len(items)
            for t in range(n + 3):
                if t < n:
                    s1(t, items[t])
                if 0 <= t - 1 < n:
                    s1b(t - 1, items[t - 1])
                if 0 <= t - 2 < n:
                    s2(t - 2, items[t - 2])
                if 0 <= t - 3 < n:
                    s3(t - 3, items[t - 3])
            P.barrier()

        def phase_MLA(j):
            P.top = base_top
            KN = [P.alloc("KN%d" % i, S * 2) for i in range(2)]
            VV = [P.alloc("VM%d" % i, S * 2) for i in range(2)]
            KR = P.alloc("KR", S * 2)
            for t in KN + VV + [KR]:
                P.new_dg([t])
            qn = [P.alloc("qn%d" % i, 1024) for i in range(3)]
            qr = [P.alloc("qr%d" % i, 1024) for i in range(3)]
            for i in range(3):
                P.new_dg([qn[i], qr[i]])
            Pt = [P.alloc("Pt%d" % i, 1024) for i in range(3)]
            rec = P.alloc("rec", 2048)
            ost = [P.alloc("ost%d" % i, 1024) for i in range(2)]
            for t in ost:
                P.new_dg([t])
            Bb = P.banks[0:2]
            Db = P.banks[2:4]
            Ob = P.banks[4:6]
            saved_banks = P.banks
            P.banks = saved_banks[6:8]
            P.bank_i = 0
            gla = gla_gen(j)
            scale = float((128 + 64) ** -0.5)
            items = []
            gi = 0
            for h in range(4):
                for g in range(NG):
                    n = 4 * g + 4
                    for c in range(n):
                        r = c - 4 * g
                        q0 = r * 128 if r >= 0 else 0
                        items.append(dict(h=h, g=g, c=c, q0=q0, diag=(r >= 0), first=(c == 0), last=(c == n - 1), gi=gi))
                    gi += 1
            P.load("sp", [KR], KR.v(BF, S), krT[:, :])

            def s1(i, it):
                h, g, c, q0 = it["h"], it["g"], it["c"], it["q0"]
                hb = 64 * (h % 2)
                if it["first"]:
                    def ld_kv(hh):
                        P.load("sp", [KN[hh % 2]], KN[hh % 2].v(BF, S), knT[hh * 128:(hh + 1) * 128, :])
                        P.load("sp", [VV[hh % 2]], VV[hh % 2].v(BF, NT, 128), vm[:, hh * 128:(hh + 1) * 128].rearrange("(n p) d -> p n d", p=128))

                    def ld_q(gg):
                        h_, g_ = gg // NG, gg % NG
                        k_ = gg % 3
                        P.load("sp", [qn[k_], qr[k_]], qn[k_].v(BF, 512), qnT[h_ * 128:(h_ + 1) * 128, g_ * 512:(g_ + 1) * 512])
                        P.load("sp", [qn[k_], qr[k_]], qr[k_].v(BF, 512), qrT[(h_ // 2) * 128:(h_ // 2 + 1) * 128, g_ * 512:(g_ + 1) * 512])
                    if it["gi"] == 0:
                        ld_kv(0)
                        ld_q(0)
                    if it["gi"] + 1 < 4 * NG:
                        ld_q(it["gi"] + 1)
                if g == 0 and c == 2 and h + 1 < 4:
                    P.load("sp", [KN[(h + 1) % 2]], KN[(h + 1) % 2].v(BF, S), knT[(h + 1) * 128:(h + 2) * 128, :])
                    P.load("sp", [VV[(h + 1) % 2]], VV[(h + 1) % 2].v(BF, NT, 128), vm[:, (h + 1) * 128:(h + 2) * 128].rearrange("(n p) d -> p n d", p=128))
                k3 = it["gi"] % 3
                kt = KN[h % 2]
                B = Bb[i % 2]
                P.mm(B, B.ap[:, q0:512], kt, kt.v(BF, S)[:, c * 128:(c + 1) * 128], qn[k3], qn[k3].v(BF, 512)[:, q0:512], True, False)
                P.mm(B, B.ap[:, q0:512], KR, KR.v(BF, S)[hb:hb + 64, c * 128:(c + 1) * 128], qr[k3], qr[k3].v(BF, 512)[hb:hb + 64, q0:512], False, True)
                p_ = Pt[i % 3]
                P.act(p_, p_.v(BF, 512)[:, q0:512], B, B.ap[:, q0:512], AF.Exp, scale=scale)
                if it["diag"]:
                    P.tt("pool", p_, p_.v(BF, 512)[:, q0:q0 + 128], p_, p_.v(BF, 512)[:, q0:q0 + 128], cbf, cbf_b[:, B_LE:B_LE + 128], ALU.mult)

            def s2(i, it):
                h, g, c, q0 = it["h"], it["g"], it["c"], it["q0"]
                p_ = Pt[i % 3]
                vt = VV[h % 2]
                O = Ob[it["gi"] % 2]
                Dn = Db[it["gi"] % 2]
                P.mm(O, O.ap[:, q0:512], vt, vt.v(BF, NT, 128)[:, c, :], p_, p_.v(BF, 512)[:, q0:512], it["first"], it["last"])
                P.mm(Dn, Dn.ap[:, q0:512], cbf, cbf_b[:, B_ONE:B_ONE + 128], p_, p_.v(BF, 512)[:, q0:512], it["first"], it["last"])
                if it["last"]:
                    P.op("dve", lambda e: e.reciprocal(out=rec.v(F32, 512), in_=Dn.ap[:, :]), reads=[Dn], writes=[rec])
                    o_ = ost[it["gi"] % 2]
                    P.tt("dve", o_, o_.v(BF, 512), O, O.ap[:, :], rec, rec.v(F32, 512), ALU.mult)
                    P.store("sp", [o_], oT[h * 128:(h + 1) * 128, g * 512:(g + 1) * 512], o_.v(BF, 512))

            n = len(items)
            for t in range(n + 1):
                if t < n:
                    s1(t, items[t])
                if 0 <= t - 1 < n:
                    s2(t - 1, items[t - 1])
                next(gla, None)
            for _ in gla:
                pass
            P.banks = saved_banks
            P.barrier()

        def gla_gen(j):
            GE = GC_EV + 7 * j
            wg2 = P.alloc("wg2", 512)
            P.new_dg([wg2])
            P.load("pool", [wg2], wg2.v(BF, 256)[0:32, :], wg2_in[j])
            NB = 2
            ga = [P.alloc("ga%d" % i, 256) for i in range(NB)]
            gq = [P.alloc("gq%d" % i, 1024) for i in range(NB)]
            gk = [P.alloc("gk%d" % i, 1024) for i in range(NB)]
            gkt_ = [P.alloc("gkt%d" % i, 1024) for i in range(NB)]
            gv_ = [P.alloc("gv%d" % i, 1024) for i in range(NB)]
            sgr = [P.alloc("sgr%d" % i, 1024) for i in range(NB)]
            for i in range(NB):
                P.new_dg([ga[i], gq[i], gk[i], gkt_[i], gv_[i], sgr[i]])
            e1 = P.alloc("e1", 1024)
            lsp = P.alloc("lsp", 1024)
            ebt = P.alloc("ebt", 1024)
            enbt = P.alloc("enbt", 1024)
            edt = P.alloc("edt", 1024)
            qtl = P.alloc("qtl", 512)
            ktl = P.alloc("ktl", 512)
            kb = P.alloc("kb", 512)
            attm = P.alloc("attm", 1024)
            Sst = [P.alloc("Sst%d" % i, 1024) for i in range(2)]
            Sz = [P.alloc("Sz%d" % i, 512) for i in range(2)]
            of = P.alloc("of", 2048)
            osq = P.alloc("osq", 1024)
            ntmp = P.alloc("gtmp", 2048)
            on = P.alloc("on", 2048)
            ost = [P.alloc("gost%d" % i, 1024) for i in range(2)]
            for t in ost:
                P.new_dg([t])
            for i in range(2):
                P.memset("dve", Sst[i], Sst[i].v(F32, 256), 0.0)
                P.memset("dve", Sz[i], Sz[i].v(BF, 256), 0.0)
            U_f = cst_f[:, C_U:C_U + 128]
            U2_f = cst_f[:, C_U2:C_U2 + 128]
            bm_f = cst_f[:, C_BM:C_BM + 256]

            def loads(n):
                i = n % NB
                sl = slice(n * 128, (n + 1) * 128)
                grp_t = [ga[i], gq[i], gk[i], gkt_[i], gv_[i], sgr[i]]
                P.load("sp", grp_t, ga[i].v(BF, 128)[0:32, :], gaT[:, sl])
                P.load("sp", grp_t, gq[i].v(F32, 2, 128), gqT.rearrange("(c p) s -> p c s", p=128)[:, :, sl])
                P.load("sp", grp_t, gk[i].v(F32, 2, 128), gkT.rearrange("(c p) s -> p c s", p=128)[:, :, sl])
                P.load("sp", grp_t, gkt_[i].v(F32, 256), gkt[sl, :])
                P.load("sp", grp_t, gv_[i].v(BF, 512), gv[sl, :])
                P.load("sp", grp_t, sgr[i].v(BF, 4, 128), sgrT.rearrange("(c p) s -> p c s", p=128)[:, :, sl])

            loads(0)
            for n in range(NT):
                i = n % NB
                if n + 1 < NT:
                    loads(n + 1)
                bG = P.bank()
                P.mm(bG, bG.ap[:, 0:256], ga[i], ga[i].v(BF, 128)[0:32, :], wg2, wg2.v(BF, 256)[0:32, :], True, True)
                yield
                P.act(e1, e1.v(F32, 256), bG, bG.ap[:, 0:256], AF.Exp, scale=-1.0)
                yield
                P.act(lsp, lsp.v(F32, 256), e1, e1.v(F32, 256), AF.Ln, bias=1.0)
                yield
                bD = P.bank()
                P.mm(bD, bD.ap[:, 0:256], cst, U2_f, lsp, lsp.v(F32, 256), True, True)
                bT = P.bank()
                for c in range(2):
                    P.mm(bT, bT.ap[:, c * 128:(c + 1) * 128], lsp, lsp.v(F32, 256)[:, c * 128:(c + 1) * 128], cst, U_f, True, True)
                yield
                P.act(ebt, ebt.v(F32, 256), bT, bT.ap[:, 0:256], AF.Exp)
                P.act(enbt, enbt.v(F32, 256), bT, bT.ap[:, 0:256], AF.Exp, scale=-1.0)
                P.act(edt, edt.v(F32, 256), bD, bD.ap[:, 0:256], AF.Exp)
                yield
                P.tt("dve", qtl, qtl.v(BF, 256), gq[i], gq[i].v(F32, 256), ebt, ebt.v(F32, 256), ALU.mult)
                P.tt("dve", ktl, ktl.v(BF, 256), gk[i], gk[i].v(F32, 256), enbt, enbt.v(F32, 256), ALU.mult)
                P.tt("dve", kb, kb.v(BF, 256), gkt_[i], gkt_[i].v(F32, 256), edt, edt.v(F32, 256), ALU.mult)
                yield
                ba = P.bank()
                bb = P.bank()
                for h in range(4):
                    pr, hb = h // 2, 64 * (h % 2)
                    bk = ba if h % 2 == 0 else bb
                    P.mm(bk, bk.ap[:, pr * 128:(pr + 1) * 128], ktl, ktl.v(BF, 2, 128)[hb:hb + 64, pr, :], qtl, qtl.v(BF, 2, 128)[hb:hb + 64, pr, :], True, True)
                yield
                for h in range(4):
                    pr = h // 2
                    bk = ba if h % 2 == 0 else bb
                    P.tt("dve", attm, attm.v(BF, 4, 128)[:, h, :], bk, bk.ap[:, pr * 128:(pr + 1) * 128], cbf, cbf_b[:, B_LE:B_LE + 128], ALU.mult)
                yield
                bO = P.bank()
                for h in range(4):
                    pr = h // 2
                    P.mm(bO, bO.ap[:, h * 128:(h + 1) * 128], gv_[i], gv_[i].v(BF, 512)[:, h * 128:(h + 1) * 128], attm, attm.v(BF, 4, 128)[:, h, :], True, False)
                    P.mm(bO, bO.ap[:, h * 128:(h + 1) * 128], Sz[pr], Sz[pr].v(BF, 256)[:, (h % 2) * 128:(h % 2 + 1) * 128], qtl, qtl.v(BF, 2, 128)[:, pr, :], False, True)
                P.copy("act", of, of.v(F32, 512), bO, bO.ap[:, :])
                yield
                for pr in range(2):
                    bS = P.bank()
                    P.mm(bS, bS.ap[:, 0:256], kb, kb.v(BF, 256)[:, pr * 128:(pr + 1) * 128], gv_[i], gv_[i].v(BF, 512)[:, pr * 256:(pr + 1) * 256], True, True)
                    P.stt(Sst[pr], Sst[pr].v(F32, 256), Sst[pr], Sst[pr].v(F32, 256), ebt.v(F32, 2, 128)[:, pr, 127:128], bS, bS.ap[:, 0:256], ALU.mult, ALU.add, extra_reads=[ebt])
                    P.tt("dve", Sz[pr], Sz[pr].v(BF, 256), Sst[pr], Sst[pr].v(F32, 256), cst, bm_f, ALU.mult)
                yield
                brs = P.bank()
                norm_stats([of], [of.v(F32, 512)], 1, 128, [(osq, osq.v(BF, 512))], brs, (ntmp, ntmp.v(F32, 512)))
                yield
                o_ = ost[n % 2]
                for h in range(4):
                    P.stt(on, on.v(F32, 4, 128)[:, h, :], of, of.v(F32, 4, 128)[:, h, :], gcol(GE + 3 + h), brs, brs.ap[:, h * 128:(h + 1) * 128], ALU.mult, ALU.mult, extra_reads=[gains])
                P.tt("dve", o_, o_.v(BF, 512), on, on.v(F32, 512), sgr[i], sgr[i].v(BF, 512), ALU.mult)
                P.store("sp", [o_], oT.rearrange("(c p) s -> p c s", p=128)[:, 4:8, n * 128:(n + 1) * 128], o_.v(BF, 4, 128))
                yield

        seq = []
        seq.append(("PRO", prologue))
        for l in range(NL):
            seq.append(("A%d" % l, lambda l=l: phase_A(l)))
            if l % 2 == 0:
                seq.append(("GLA%d" % l, lambda l=l: phase_MLA(l // 2)))
            else:
                seq.append(("SB%d" % l, phase_SB))
            seq.append(("B%d" % l, lambda l=l: phase_B(l)))
        for name, fn in seq:
            fn()
            P.recycle()
            if stop is not None and name == stop:
                break
        P.barrier()
        P.emit()
    return nc


def kc_layout(w):
    K, N = w.shape
    return np.ascontiguousarray(w.reshape(K // 128, 128, N).transpose(1, 0, 2))


def col_layout(v):
    return np.ascontiguousarray(v.reshape(-1, 128).T)


def prep_shared(inp, NL):
    NE = (NL + 1) // 2
    NO = NL // 2
    f = lambda a: np.asarray(a, np.float32)
    sh = {}
    sh["consts"] = make_consts()
    gl = []
    for l in range(NL):
        for nm in ("ffn1_norm", "mix_norm", "ffn2_norm", "ple_norm"):
            gl.append(col_layout(f(inp[nm][l])))
    gl.append(col_layout(f(inp["final_norm"])))
    for j in range(NE):
        gl.append(col_layout(f(inp["ev_q_norm"][j])))
        gl.append(col_layout(f(inp["ev_kv_norm"][j])))
        gl.append(np.ascontiguousarray(f(inp["ev_gla_norm"][j]).T))
    sh["gains"] = np.ascontiguousarray(np.concatenate(gl, axis=1))
    wgu, wdn = [], []
    for l in range(NL):
        for a, b in (("ffn1_w_gu", "ffn1_w_down"), ("ffn2_w_gu", "ffn2_w_down")):
            wgu.append(kc_layout(f(inp[a][l])))
            wdn.append(kc_layout(f(inp[b][l])))
    sh["wgu"] = np.stack(wgu)
    sh["wdn"] = np.stack(wdn)
    sh["pleg"] = np.stack([kc_layout(f(inp["ple_w_gate"][l])) for l in range(NL)])
    sh["plep"] = np.stack([kc_layout(f(inp["ple_w_proj"][l])) for l in range(NL)])
    win, wuq, wukv, wg2, woute = [], [], [], [], []
    for j in range(NE):
        w = f(inp["ev_w_in"][j])
        cq, ckv, kr = w[:, 0:256], w[:, 256:384], w[:, 384:448]
        gq, gk, gvv, ga, gr = w[:, 448:704], w[:, 704:960], w[:, 960:1472], w[:, 1472:1488], w[:, 1488:2000]
        krs = np.concatenate([kr[:, 32:64], kr[:, 0:32]], axis=1)
        ext = np.concatenate([cq, ckv, kr, kr, krs, krs, gq, gk, gr, ga, gk, gvv], axis=1)
        assert ext.shape[1] == 2448
        win.append(kc_layout(ext))
        u = f(inp["ev_w_uq"][j]).reshape(256, 4, 192)
        nope = [u[:, h, 0:128] for h in range(4)]
        rope = [u[:, h, 128:192] for h in range(4)]
        rsw = [np.concatenate([r[:, 32:64], r[:, 0:32]], axis=1) for r in rope]
        wuq.append(kc_layout(np.concatenate(nope + rope + rsw, axis=1)))
        kv = f(inp["ev_w_ukv"][j]).reshape(128, 4, 256)
        wukv.append(np.ascontiguousarray(np.concatenate([kv[:, h, 0:128] for h in range(4)] + [kv[:, h, 128:256] for h in range(4)], axis=1)))
        g2 = np.zeros((32, 256), np.float32)
        g2[0:16] = f(inp["ev_w_gate2"][j])
        g2[16] = f(inp["ev_b_gate"][j])
        wg2.append(g2)
        woute.append(kc_layout(f(inp["ev_w_out"][j])))
    sh["win"] = np.stack(win)
    sh["wuq"] = np.stack(wuq)
    sh["wukv"] = np.stack(wukv)
    sh["wg2"] = np.stack(wg2)
    sh["woute"] = np.stack(woute)
    if NO > 0:
        sh["wqkv"] = np.stack([kc_layout(f(inp["od_w_qkv"][j])) for j in range(NO)])
        sh["wouto"] = np.stack([kc_layout(f(inp["od_w_out"][j])) for j in range(NO)])
    return sh


def run(inp, S, NL, dbg=False, stop=None, ncores=NCORES):
    nc = build(S, NL, dbg=dbg, stop=stop)
    sh = prep_shared(inp, NL)
    x = np.asarray(inp["x"], np.float32)
    p = np.asarray(inp["p"], np.float32)
    pos = np.asarray(inp["positions"], np.int32)
    in_maps = []
    for c in range(ncores):
        m = dict(sh)
        m["x"] = np.ascontiguousarray(x[c, :S])
        m["p"] = np.ascontiguousarray(p[:NL, c, :S])
        m["pos"] = np.ascontiguousarray(pos[c:c + 1, :S])
        in_maps.append(m)
    res = run_bass_kernel_spmd(nc, in_maps, core_ids=list(range(ncores)))
    return res.results


def kernel(**inputs):
    x = np.asarray(inputs["x"])
    B, S, _ = x.shape
    NL = int(np.asarray(inputs["ffn1_norm"]).shape[0])
    res = run(inputs, S, NL)
    return np.stack([np.asarray(r["y"], np.float32) for r in res], axis=0)
```

```python
from contextlib import ExitStack
import numpy as np
import ml_dtypes
import concourse.bass as bass
import concourse.mybir as mybir
from concourse.bass_utils import run_bass_kernel_spmd

F32 = mybir.dt.float32
BF = mybir.dt.bfloat16
I32 = mybir.dt.int32
AF = mybir.ActivationFunctionType
ALU = mybir.AluOpType

D = 1024
DFF = 1408
PD = 256
EPS = 1e-6
NCORES = 8
ENGS = ["pe", "act", "dve", "pool", "sp"]


class DmaGroup:
    def __init__(self, sem):
        self.sem = sem
        self.cnt = 0


class Tile:
    def __init__(self, ap, name):
        self.ap = ap
        self.name = name
        self.w = None
        self.r = {}
        self.dg = None

    def v(self, dt, *shape):
        n = 1
        for s in shape:
            n *= s
        nb = n * (4 if dt in (F32, I32) else 2) // 2
        a = self.ap[:, 0:nb]
        if dt != BF:
            a = a.bitcast(dt)
        if len(shape) == 2:
            return a.rearrange("p (c n) -> p c n", c=shape[0])
        if len(shape) == 3:
            return a.rearrange("p (a c n) -> p a c n", a=shape[0], c=shape[1])
        return a


class Prog:
    def __init__(self, nc, stack):
        self.nc = nc
        self.stack = stack
        self.ops = {e: [] for e in ENGS}
        self.seen = {e: {} for e in ENGS}
        self.pending = {e: [] for e in ENGS}
        self.esem = {e: stack.enter_context(nc.semaphore("sem_" + e)) for e in ENGS}
        self.dgs = []
        self.nsem = 0
        self.top = 0
        self.big = None
        self.banks = []
        self.bank_i = 0
        self._start = {}
        self.free_dgs = []
        self.phase_dgs = []

    def init_mem(self, kb=206):
        nc = self.nc
        self.big = self.stack.enter_context(nc.sbuf_tensor("big", [128, kb * 512], BF))
        self.cap = kb * 512
        self.psall = self.stack.enter_context(nc.psum_tensor("psall", [128, 4096], F32))
        for i in range(8):
            self.banks.append(Tile(self.psall[:, i * 512:(i + 1) * 512], "bank%d" % i))

    def alloc(self, name, nbytes):
        n = (nbytes + 63) // 64 * 32
        assert self.top + n <= self.cap, ("SBUF overflow", name, self.top, n, self.cap)
        t = Tile(self.big[:, self.top:self.top + n], name)
        self._start[id(t)] = self.top
        self.top += n
        return t

    def span(self, tiles, dt, *shape):
        n = sum(t.ap.shape[1] for t in tiles)
        st = self._start[id(tiles[0])]
        a = self.big[:, st:st + n]
        if dt != BF:
            a = a.bitcast(dt)
        if len(shape) == 2:
            return a.rearrange("p (c n) -> p c n", c=shape[0])
        return a

    def bank(self):
        b = self.banks[self.bank_i % len(self.banks)]
        self.bank_i += 1
        return b

    def new_dg(self, tiles):
        if self.free_dgs:
            dg = self.free_dgs.pop()
        else:
            sem = self.stack.enter_context(self.nc.semaphore("dsem%d" % self.nsem))
            self.nsem += 1
            dg = DmaGroup(sem)
            self.dgs.append(dg)
        self.phase_dgs.append(dg)
        for t in tiles:
            t.dg = dg
        return dg

    def recycle(self):
        self.free_dgs.extend(self.phase_dgs)
        self.phase_dgs = []

    def _add(self, eng, fn, deps, dma=None):
        waits = []
        deps = list(deps) + self.pending[eng]
        self.pending[eng] = []
        for d in deps:
            if d is None:
                continue
            if d[0] == "e":
                _, e2, i2 = d
                if e2 == eng and eng in ("pe", "sp"):
                    continue
                if self.seen[eng].get(e2, -1) >= i2:
                    continue
                self.seen[eng][e2] = i2
                waits.append(d)
            else:
                _, dg, cnt = d
                if self.seen[eng].get(id(dg), -1) >= cnt:
                    continue
                self.seen[eng][id(dg)] = cnt
                waits.append(d)
        idx = len(self.ops[eng])
        self.ops[eng].append({"fn": fn, "waits": waits, "inc": False, "dma": dma})
        return ("e", eng, idx)

    def op(self, eng, fn, reads=(), writes=()):
        deps = []
        for t in reads:
            deps.append(t.w)
        for t in writes:
            deps.append(t.w)
            deps.extend(t.r.values())
        me = self._add(eng, fn, deps)
        for t in reads:
            t.r[eng] = me
        for t in writes:
            t.w = me
            t.r = {}
        return me

    def load(self, q, tiles, out_ap, in_ap, **kw):
        dg = tiles[0].dg
        assert dg is not None and all(t.dg is dg for t in tiles)
        deps = []
        for t in tiles:
            if not (t.w is not None and t.w[0] == "d" and t.w[1] is dg):
                deps.append(t.w)
            deps.extend(t.r.values())
        dg.cnt += 16
        self._add(q, lambda e: e.dma_start(out=out_ap, in_=in_ap, **kw), deps, dma=dg)
        me = ("d", dg, dg.cnt)
        for t in tiles:
            t.w = me
            t.r = {}

    def store(self, q, tiles, out_ap, in_ap, **kw):
        dg = tiles[0].dg
        assert dg is not None and all(t.dg is dg for t in tiles)
        deps = [t.w for t in tiles]
        dg.cnt += 16
        self._add(q, lambda e: e.dma_start(out=out_ap, in_=in_ap, **kw), deps, dma=dg)
        me = ("d", dg, dg.cnt)
        for t in tiles:
            t.r["dma"] = me

    def barrier(self):
        deps = [("d", dg, dg.cnt) for dg in self.dgs if dg.cnt > 0]
        for e in ENGS:
            if e != "sp":
                for i in range(len(self.ops[e]) - 1, -1, -1):
                    if self.ops[e][i]["dma"] is None:
                        deps.append(("e", e, i))
                        break
        me = self._add("sp", lambda e: e.nop(), deps)
        for e in ENGS:
            if e != "sp":
                self.pending[e].append(me)

    def emit(self):
        nc = self.nc
        for e in ENGS:
            for o in self.ops[e]:
                for d in o["waits"]:
                    if d[0] == "e":
                        self.ops[d[1]][d[2]]["inc"] = True
        for e in ENGS:
            c = 0
            for o in self.ops[e]:
                if o["inc"]:
                    assert o["dma"] is None
                    c += 1
                o["cnt"] = c

        def replay(name, eng):
            sem = self.esem[name]
            for o in self.ops[name]:
                for d in o["waits"]:
                    if d[0] == "e":
                        eng.wait_ge(self.esem[d[1]], self.ops[d[1]][d[2]]["cnt"])
                    else:
                        eng.wait_ge(d[1].sem, d[2])
                ins = o["fn"](eng)
                if o["dma"] is not None:
                    ins.then_inc(o["dma"].sem, 16)
                elif o["inc"]:
                    ins.then_inc(sem, 1)

        with nc.Block() as block:
            @block.tensor
            def _(e):
                replay("pe", e)

            @block.scalar
            def _(e):
                replay("act", e)

            @block.vector
            def _(e):
                replay("dve", e)

            @block.gpsimd
            def _(e):
                replay("pool", e)

            @block.sync
            def _(e):
                replay("sp", e)

    def mm(self, out_t, out_ap, lt, l_ap, rt, r_ap, start, stop):
        self.op("pe", lambda e: e.matmul(out_ap, l_ap, r_ap, start=start, stop=stop), reads=[lt, rt], writes=[out_t])

    def act(self, out_t, out_ap, in_t, in_ap, func, scale=None, bias=None, extra_reads=()):
        kw = {}
        if scale is not None:
            kw["scale"] = scale
        if bias is not None:
            kw["bias"] = bias
        self.op("act", lambda e: e.activation(out=out_ap, in_=in_ap, func=func, **kw),
                reads=[in_t] + list(extra_reads), writes=[out_t])

    def tt(self, eng, out_t, out_ap, a_t, a_ap, b_t, b_ap, op):
        self.op(eng, lambda e: e.tensor_tensor(out=out_ap, in0=a_ap, in1=b_ap, op=op), reads=[a_t, b_t], writes=[out_t])

    def stt(self, out_t, out_ap, a_t, a_ap, scalar, b_t, b_ap, op0, op1, extra_reads=()):
        self.op("dve", lambda e: e.scalar_tensor_tensor(out=out_ap, in0=a_ap, scalar=scalar, in1=b_ap, op0=op0, op1=op1),
                reads=[a_t, b_t] + list(extra_reads), writes=[out_t])

    def ts(self, eng, out_t, out_ap, a_t, a_ap, s1, op0, s2=None, op1=None, extra_reads=()):
        if op1 is None:
            f = lambda e: e.tensor_scalar(out=out_ap, in0=a_ap, scalar1=s1, scalar2=None, op0=op0)
        else:
            f = lambda e: e.tensor_scalar(out=out_ap, in0=a_ap, scalar1=s1, scalar2=s2, op0=op0, op1=op1)
        self.op(eng, f, reads=[a_t] + list(extra_reads), writes=[out_t])

    def copy(self, eng, out_t, out_ap, in_t, in_ap):
        if eng == "act":
            self.op("act", lambda e: e.copy(out=out_ap, in_=in_ap), reads=[in_t], writes=[out_t])
        else:
            self.op(eng, lambda e: e.tensor_copy(out=out_ap, in_=in_ap), reads=[in_t], writes=[out_t])

    def memset(self, eng, t, ap, val):
        self.op(eng, lambda e: e.memset(ap, val), writes=[t])


C_ID, C_LE, C_LT, C_NGE, C_U, C_U2, C_BM, C_IF, C_SG, C_E127, CW = 0, 128, 256, 384, 512, 640, 768, 1024, 1025, 1026, 1027


def make_consts():
    c = np.zeros((128, CW), np.float32)
    k = np.arange(128)[:, None]
    q = np.arange(128)[None, :]
    c[:, C_ID:C_ID + 128] = (k == q)
    c[:, C_LE:C_LE + 128] = (k <= q)
    c[:, C_LT:C_LT + 128] = (k < q)
    c[:, C_NGE:C_NGE + 128] = -1.0 * (k >= q)
    c[:, C_U:C_U + 128] = (-1.0 / 16.0) * (k <= q)
    c[:, C_U2:C_U2 + 128] = (-1.0 / 16.0) * (k > q)
    bm = np.zeros((128, 256), np.float32)
    bm[:64, :128] = 1.0
    bm[64:, 128:] = 1.0
    c[:, C_BM:C_BM + 256] = bm
    inv_freq = (1.0 / (10000.0 ** (np.arange(32, dtype=np.float32) / 32.0))).astype(np.float32)
    c[:, C_IF] = np.tile(inv_freq, 4) / (2.0 * np.pi)
    c[:, C_SG] = np.tile(np.concatenate([-np.ones(32), np.ones(32)]), 2)
    c[127, C_E127] = -1.0
    return c


def gcol_layer(l, which):
    return l * 32 + which * 8


def build(S, NL, dbg=False, stop=None):
    NG = S // 512
    NT = S // 128
    NE = (NL + 1) // 2
    NO = NL // 2
    GC_FIN = NL * 32
    GC_EV = GC_FIN + 8
    GW = GC_EV + 7 * NE
    nc = bass.Bass("TRN2", target_bir_lowering=False)

    def din(name, shape, dt=F32):
        return nc.dram_tensor(name, list(shape), dt, kind="ExternalInput").ap()

    def dscr(name, shape, dt):
        return nc.dram_tensor(name, list(shape), dt, kind=("ExternalOutput" if dbg else "Internal")).ap()

    x_in = din("x", [S, D])
    p_in = din("p", [NL, S, PD])
    pos_in = din("pos", [1, S], I32)
    consts_in = din("consts", [128, CW])
    gains_in = din("gains", [128, GW])
    wgu_in = din("wgu", [NL * 2, 128, 8, 2 * DFF])
    wdn_in = din("wdn", [NL * 2, 128, 11, D])
    pleg_in = din("pleg", [NL, 128, 8, D])
    plep_in = din("plep", [NL, 128, 2, D])
    win_in = din("win", [NE, 128, 8, 2448])
    wuq_in = din("wuq", [NE, 128, 2, 1024])
    wukv_in = din("wukv", [NE, 128, 1024])
    wg2_in = din("wg2", [NE, 32, 256])
    woute_in = din("woute", [NE, 128, 8, D])
    if NO > 0:
        wqkv_in = din("wqkv", [NO, 128, 8, 3 * D])
        wouto_in = din("wouto", [NO, 128, 8, D])
    y_out = nc.dram_tensor("y", [S, D], F32, kind="ExternalOutput").ap()

    xT = dscr("xT", [D, S], F32)
    oT = dscr("oT", [D, S], BF)
    cosT = dscr("cosT", [128, S], F32)
    sinT = dscr("sinT", [128, S], F32)
    qnT = dscr("qnT", [512, S], BF)
    qrT = dscr("qrT", [256, S], BF)
    knT = dscr("knT", [512, S], BF)
    krT = dscr("krT", [128, S], BF)
    vm = dscr("vm", [S, 512], BF)
    gqT = dscr("gqT", [256, S], F32)
    gkT = dscr("gkT", [256, S], F32)
    gkt = dscr("gkt", [S, 256], F32)
    gv = dscr("gv", [S, 512], BF)
    gaT = dscr("gaT", [32, S], BF)
    sgrT = dscr("sgrT", [512, S], BF)
    sqT = dscr("sqT", [D, S], BF)
    skT = dscr("skT", [D, S], BF)
    sv = dscr("sv", [S, D], BF)

    stack = ExitStack()
    with stack:
        P = Prog(nc, stack)
        P.init_mem()

        cst = P.alloc("consts", CW * 4)
        gains = P.alloc("gains", GW * 4)
        cbf = P.alloc("cbf", 1280 * 2)
        P.new_dg([cst])
        P.new_dg([gains])
        cst_f = cst.v(F32, CW)
        gains_f = gains.v(F32, GW)
        cbf_b = cbf.v(BF, 1280)
        B_LE, B_LT, B_NGE, B_ONE, B_NEG, B_ZERO = 0, 128, 256, 384, 512, 640
        P.load("sp", [cst], cst_f, consts_in[:, :])
        P.load("sp", [gains], gains_f, gains_in[:, :])
        P.copy("dve", cbf, cbf_b[:, 0:384], cst, cst_f[:, C_LE:C_LE + 384])
        P.memset("dve", cbf, cbf_b[:, B_ONE:B_ONE + 128], 1.0)
        P.memset("dve", cbf, cbf_b[:, B_NEG:B_NEG + 128], -1.0)
        P.memset("dve", cbf, cbf_b[:, B_ZERO:B_ZERO + 512], 0.0)
        ident = cst_f[:, C_ID:C_ID + 128]
        base_top = P.top
        P.phase_dgs = []

        def gcol(c):
            return gains_f[:, c:c + 1]

        def grp(ap2d, g):
            return ap2d[:, g * 512:(g + 1) * 512]

        def prologue():
            P.top = base_top
            CH = min(S, 2048)
            pi_t = P.alloc("posi", CH * 4)
            t_u = P.alloc("u", CH * 4)
            t_t = P.alloc("t", CH * 4)
            t_k = P.alloc("k", CH * 4)
            t_f = P.alloc("f", CH * 4)
            t_m = P.alloc("m", CH * 4)
            t_o = P.alloc("o", CH * 4)
            P.new_dg([pi_t])
            P.new_dg([t_o])
            for c0 in range(0, S, CH):
                P.load("sp", [pi_t], pi_t.v(I32, CH), pos_in[0:1, c0:c0 + CH].partition_broadcast(128)[:, 0, :])
                P.copy("dve", t_u, t_u.v(F32, CH), pi_t, pi_t.v(I32, CH))
                P.ts("dve", t_u, t_u.v(F32, CH), t_u, t_u.v(F32, CH), cst_f[:, C_IF:C_IF + 1], ALU.mult, extra_reads=[cst])
                for shift, dst, signed in ((0.0, sinT, True), (0.25, cosT, False)):
                    P.ts("dve", t_t, t_t.v(F32, CH), t_u, t_u.v(F32, CH), shift, ALU.add)
                    P.copy("dve", t_k, t_k.v(I32, CH), t_t, t_t.v(F32, CH))
                    P.copy("dve", t_f, t_f.v(F32, CH), t_k, t_k.v(I32, CH))
                    P.tt("dve", t_f, t_f.v(F32, CH), t_t, t_t.v(F32, CH), t_f, t_f.v(F32, CH), ALU.subtract)
                    P.ts("dve", t_m, t_m.v(F32, CH), t_f, t_f.v(F32, CH), 0.5, ALU.is_gt)
                    P.tt("dve", t_f, t_f.v(F32, CH), t_f, t_f.v(F32, CH), t_m, t_m.v(F32, CH), ALU.subtract)
                    P.ts("dve", t_m, t_m.v(F32, CH), t_f, t_f.v(F32, CH), -0.5, ALU.is_lt)
                    P.tt("dve", t_f, t_f.v(F32, CH), t_f, t_f.v(F32, CH), t_m, t_m.v(F32, CH), ALU.add)
                    P.act(t_o, t_o.v(F32, CH), t_f, t_f.v(F32, CH), AF.Sin, scale=2.0 * np.pi * (1.0 - 1e-6))
                    if signed:
                        P.ts("dve", t_o, t_o.v(F32, CH), t_o, t_o.v(F32, CH), cst_f[:, C_SG:C_SG + 1], ALU.mult, extra_reads=[cst])
                    P.store("sp", [t_o], dst[:, c0:c0 + CH], t_o.v(F32, CH))
            P.barrier()

        def load_w(name, dram_ap, kc, cols):
            t = P.alloc(name, kc * cols * 2)
            P.new_dg([t])
            v = t.v(BF, kc, cols)
            nsp = 1
            while cols // nsp > 2048 or cols % nsp:
                nsp += 1
            w = cols // nsp
            for k in range(kc):
                for s_ in range(nsp):
                    P.load("pool", [t], v[:, k, s_ * w:(s_ + 1) * w], dram_ap[:, k, s_ * w:(s_ + 1) * w])
            return t, v

        def norm_stats(xc, xcv, nchunk, nfeat, sq, rs_bank, tmp, W=512):
            for c in range(nchunk):
                P.act(sq[c][0], sq[c][1], xc[c], xcv[c], AF.Square)
            ssum = P.bank()
            for c in range(nchunk):
                P.mm(ssum, ssum.ap[:, 0:W], cbf, cbf_b[:, B_ONE:B_ONE + 128], sq[c][0], sq[c][1], c == 0, c == nchunk - 1)
            P.act(tmp[0], tmp[1], ssum, ssum.ap[:, 0:W], AF.Ln, scale=1.0 / nfeat, bias=EPS)
            P.act(rs_bank, rs_bank.ap[:, 0:W], tmp[0], tmp[1], AF.Exp, scale=-0.5)

        GS = 256
        NGS = S // GS
        TT = GS // 128

        def alloc_lane(tag):
            L = {}
            L["xct"] = [P.alloc("xc%s%d" % (tag, c), GS * 4) for c in range(8)]
            P.new_dg(L["xct"])
            L["xcv"] = [t.v(F32, GS) for t in L["xct"]]
            L["xn"] = [P.alloc("xn%s%d" % (tag, c), GS * 2) for c in range(8)]
            L["xnv"] = [t.v(BF, GS) for t in L["xn"]]
            L["actt"] = [P.alloc("act%s%d" % (tag, c), GS * 2) for c in range(11)]
            L["actv"] = [t.v(BF, GS) for t in L["actt"]]
            L["sg"] = [P.alloc("sg%s%d" % (tag, c), GS * 4) for c in range(2)]
            t = P.alloc("ntmp" + tag, GS * 4)
            L["tmp"] = (t, t.v(F32, GS))
            return L

        def norm_xn(L, gc):
            xct, xcv, xn, xnv, actt, actv, rsb = L["xct"], L["xcv"], L["xn"], L["xnv"], L["actt"], L["actv"], L["rsb"]
            norm_stats(xct, xcv, 8, D, [(actt[c], actv[c]) for c in range(8)], rsb, L["tmp"], W=GS)
            for c in range(8):
                P.stt(xn[c], xnv[c], xct[c], xcv[c], gcol(gc + c), rsb, rsb.ap[:, 0:GS], ALU.mult, ALU.mult, extra_reads=[gains])

        def ffn_gen(L, W, gc):
            xct, xcv, xn, xnv, actt, actv, sg = L["xct"], L["xcv"], L["xn"], L["xnv"], L["actt"], L["actv"], L["sg"]
            wgu_t, wgu_v, wdn_t, wdn_v = W
            norm_xn(L, gc)
            yield
            for f in range(11):
                pg = P.bank()
                pu = P.bank()
                for k in range(8):
                    P.mm(pg, pg.ap[:, 0:GS], wgu_t, wgu_v[:, k, f * 128:(f + 1) * 128], xn[k], xnv[k], k == 0, k == 7)
                for k in range(8):
                    P.mm(pu, pu.ap[:, 0:GS], wgu_t, wgu_v[:, k, DFF + f * 128:DFF + (f + 1) * 128], xn[k], xnv[k], k == 0, k == 7)
                s = sg[f % 2]
                P.act(s, s.v(F32, GS), pg, pg.ap[:, 0:GS], AF.Silu)
                P.tt("dve", actt[f], actv[f], s, s.v(F32, GS), pu, pu.ap[:, 0:GS], ALU.mult)
            yield
            for m in range(8):
                pd = P.bank()
                for f in range(11):
                    P.mm(pd, pd.ap[:, 0:GS], wdn_t, wdn_v[:, f, m * 128:(m + 1) * 128], actt[f], actv[f], f == 0, f == 10)
                P.stt(xct[m], xcv[m], pd, pd.ap[:, 0:GS], 0.5, xct[m], xcv[m], ALU.mult, ALU.add)
            yield

        def run_lanes(g0, g1):
            a0 = next(g0, "END") != "END"
            a1 = True
            while a0 or a1:
                if a1:
                    a1 = next(g1, "END") != "END"
                if a0:
                    a0 = next(g0, "END") != "END"

        xcd = xT.rearrange("(c p) s -> p c s", p=128)

        def phase_A(l):
            P.top = base_top
            even = (l % 2 == 0)
            j = l // 2
            lanes = [alloc_lane("a"), alloc_lane("b")]
            W = load_w("wgu", wgu_in[2 * l], 8, 2 * DFF) + load_w("wdn", wdn_in[2 * l], 11, D)
            if even:
                win_t, win_v = load_w("win", win_in[j], 8, 2448)
                wuq_t, wuq_v = load_w("wuq", wuq_in[j], 2, 1024)
                wukv_t = P.alloc("wukv", 2048)
                P.new_dg([wukv_t])
                wukv_v = wukv_t.v(BF, 1024)
                P.load("pool", [wukv_t], wukv_v, wukv_in[j])
            else:
                wqkv_t, wqkv_v = load_w("wqkv", wqkv_in[j], 8, 3 * D)
            for li, L in enumerate(lanes):
                L["rsb"] = P.banks[li]
                L["stg"] = [P.alloc("stg%d_%d" % (li, i), 4096) for i in range(3)]
                for s_ in L["stg"]:
                    P.new_dg([s_])
                L["stg_i"] = 0
                if even:
                    L["cqf"] = [P.alloc("cqf%d_%d" % (li, c), GS * 4) for c in range(2)]
                    L["cqn"] = [P.alloc("cqn%d_%d" % (li, c), GS * 2) for c in range(2)]
                    L["ckvf"] = P.alloc("ckvf%d" % li, GS * 4)
                    L["ckvn"] = P.alloc("ckvn%d" % li, GS * 2)
                    L["cs"] = P.alloc("cos%d" % li, GS * 4)
                    L["sn"] = P.alloc("sin%d" % li, GS * 4)
                    P.new_dg([L["cs"], L["sn"]])
                    L["rt"] = [P.alloc("rt%d_%d" % (li, i), GS * 4) for i in range(2)]
                    L["gat"] = P.alloc("gat%d" % li, GS * 2)
                    P.new_dg([L["gat"]])
                    P.memset("dve", L["gat"], L["gat"].v(BF, GS)[0:32, :], 1.0)
            saved_banks = P.banks
            P.banks = saved_banks[2:]
            P.bank_i = 0

            def body(L, groups):
                xct, xcv, xn, xnv, actt, actv = L["xct"], L["xcv"], L["xn"], L["xnv"], L["actt"], L["actv"]
                rsb = L["rsb"]

                def nstg():
                    s_ = L["stg"][L["stg_i"] % 3]
                    L["stg_i"] += 1
                    return s_

                for g in groups:
                    gs = slice(g * GS, (g + 1) * GS)
                    if l == 0:
                        xts = []
                        for tt_ in range(TT):
                            s_ = nstg()
                            P.load("sp", [s_], s_.v(F32, 1024), x_in[g * GS + tt_ * 128: g * GS + (tt_ + 1) * 128, :])
                            xts.append(s_)
                        for c in range(8):
                            b = P.bank()
                            for tt_ in range(TT):
                                s_ = xts[tt_]
                                P.op("pe", lambda e, b=b, tt_=tt_, s_=s_, c=c: e.transpose(b.ap[:, tt_ * 128:(tt_ + 1) * 128], s_.v(F32, 1024)[:, c * 128:(c + 1) * 128], ident),
                                     reads=[s_, cst], writes=[b])
                            P.copy("act" if c % 2 else "dve", xct[c], xcv[c], b, b.ap[:, 0:GS])
                    else:
                        P.load("sp", xct, P.span(xct, F32, 8, GS), xcd[:, :, gs])
                    yield from ffn_gen(L, W, gcol_layer(l, 0))
                    P.store("sp", xct, xcd[:, :, gs], P.span(xct, F32, 8, GS))
                    norm_xn(L, gcol_layer(l, 1))
                    yield
                    if not even:
                        for kind, dst, sc in ((0, sqT, 128.0 ** -0.5), (1, skT, 1.0)):
                            for hh in range(0, 8, 4):
                                s_ = nstg()
                                sv_ = s_.v(BF, 4, GS)
                                for h4 in range(4):
                                    h = hh + h4
                                    b = P.bank()
                                    for k in range(8):
                                        P.mm(b, b.ap[:, 0:GS], wqkv_t, wqkv_v[:, k, kind * D + h * 128: kind * D + (h + 1) * 128], xn[k], xnv[k], k == 0, k == 7)
                                    P.act(s_, sv_[:, h4, :], b, b.ap[:, 0:GS], AF.Identity, scale=sc)
                                P.store("sp", [s_], dst.rearrange("(c p) s -> p c s", p=128)[:, hh:hh + 4, gs], sv_)
                            yield
                        for tt_ in range(TT):
                            s_ = nstg()
                            sv_ = s_.v(BF, 1024)
                            for half in range(2):
                                b = P.bank()
                                for k in range(8):
                                    P.mm(b, b.ap[:, :], xn[k], xnv[k][:, tt_ * 128:(tt_ + 1) * 128], wqkv_t, wqkv_v[:, k, 2 * D + half * 512: 2 * D + (half + 1) * 512], k == 0, k == 7)
                                P.copy("dve" if half else "act", s_, sv_[:, half * 512:(half + 1) * 512], b, b.ap[:, :])
                            P.store("sp", [s_], sv[g * GS + tt_ * 128: g * GS + (tt_ + 1) * 128, :], sv_)
                        yield
                    else:
                        GE = GC_EV + 7 * j
                        cqf, cqn, ckvf, ckvn, cs_t, sn_t, rt, gat = L["cqf"], L["cqn"], L["ckvf"], L["ckvn"], L["cs"], L["sn"], L["rt"], L["gat"]
                        P.load("sp", [cs_t, sn_t], cs_t.v(F32, GS), cosT[:, gs])
                        P.load("sp", [cs_t, sn_t], sn_t.v(F32, GS), sinT[:, gs])

                        def fm(chunk, ncols=128):
                            b = P.bank()
                            for k in range(8):
                                P.mm(b, b.ap[0:ncols, 0:GS], win_t, win_v[:, k, chunk * 128: chunk * 128 + ncols], xn[k], xnv[k], k == 0, k == 7)
                            return b

                        for c in range(2):
                            b = fm(c)
                            P.copy("dve", cqf[c], cqf[c].v(F32, GS), b, b.ap[:, 0:GS])
                        b = fm(2)
                        P.copy("dve", ckvf, ckvf.v(F32, GS), b, b.ap[:, 0:GS])
                        rs2 = P.bank()
                        norm_stats(cqf, [t.v(F32, GS) for t in cqf], 2, 256, [(actt[c], actv[c]) for c in range(2)], rs2, L["tmp"], W=GS)
                        for c in range(2):
                            P.stt(cqn[c], cqn[c].v(BF, GS), cqf[c], cqf[c].v(F32, GS), gcol(GE + c), rs2, rs2.ap[:, 0:GS], ALU.mult, ALU.mult, extra_reads=[gains])
                        rs3 = P.bank()
                        norm_stats([ckvf], [ckvf.v(F32, GS)], 1, 128, [(actt[2], actv[2])], rs3, L["tmp"], W=GS)
                        P.stt(ckvn, ckvn.v(BF, GS), ckvf, ckvf.v(F32, GS), gcol(GE + 2), rs3, rs3.ap[:, 0:GS], ALU.mult, ALU.mult, extra_reads=[gains])
                        yield

                        def rope(b_a, b_s, out_t, out_ap):
                            P.tt("dve", rt[0], rt[0].v(F32, GS), b_a, b_a.ap[:, 0:GS], cs_t, cs_t.v(F32, GS), ALU.mult)
                            P.tt("dve", rt[1], rt[1].v(F32, GS), b_s, b_s.ap[:, 0:GS], sn_t, sn_t.v(F32, GS), ALU.mult)
                            P.tt("pool", out_t, out_ap, rt[0], rt[0].v(F32, GS), rt[1], rt[1].v(F32, GS), ALU.add)

                        b3 = fm(3)
                        b4 = fm(4)
                        s_ = nstg()
                        rope(b3, b4, s_, s_.v(BF, GS))
                        P.store("sp", [s_], krT[:, gs], s_.v(BF, GS))
                        for c0, dst, sc in ((5, gqT, 64.0 ** -0.5), (7, gkT, 1.0)):
                            s_ = nstg()
                            sv_ = s_.v(F32, 2, GS)
                            for c in range(2):
                                b = fm(c0 + c)
                                P.act(s_, sv_[:, c, :], b, b.ap[:, 0:GS], AF.Identity, scale=sc)
                            P.store("sp", [s_], dst.rearrange("(c p) s -> p c s", p=128)[:, :, gs], sv_)
                        yield
                        s_ = nstg()
                        sv_ = s_.v(BF, 4, GS)
                        for c in range(4):
                            b = fm(9 + c)
                            P.act(s_, sv_[:, c, :], b, b.ap[:, 0:GS], AF.Silu)
                        P.store("sp", [s_], sgrT.rearrange("(c p) s -> p c s", p=128)[:, :, gs], sv_)
                        b = fm(13, 16)
                        P.copy("dve", gat, gat.v(BF, GS)[0:16, :], b, b.ap[0:16, 0:GS])
                        P.store("sp", [gat], gaT[:, gs], gat.v(BF, GS)[0:32, :])
                        yield
                        sk_ = nstg()
                        svv_ = nstg()
                        for tt_ in range(TT):
                            b1 = P.bank()
                            b2 = P.bank()
                            for k in range(8):
                                P.mm(b1, b1.ap[:, 0:256], xn[k], xnv[k][:, tt_ * 128:(tt_ + 1) * 128], win_t, win_v[:, k, 1680:1936], k == 0, k == 7)
                            for k in range(8):
                                P.mm(b2, b2.ap[:, :], xn[k], xnv[k][:, tt_ * 128:(tt_ + 1) * 128], win_t, win_v[:, k, 1936:2448], k == 0, k == 7)
                            P.copy("act", sk_, sk_.v(F32, TT, 256)[:, tt_, :], b1, b1.ap[:, 0:256])
                            P.copy("dve", svv_, svv_.v(BF, TT, 512)[:, tt_, :], b2, b2.ap[:, :])
                        P.store("sp", [sk_], gkt[gs, :].rearrange("(t p) c -> p t c", p=128), sk_.v(F32, TT, 256))
                        P.store("sp", [svv_], gv[gs, :].rearrange("(t p) c -> p t c", p=128), svv_.v(BF, TT, 512))
                        yield

                        def uq(chunk):
                            b = P.bank()
                            for k in range(2):
                                P.mm(b, b.ap[:, 0:GS], wuq_t, wuq_v[:, k, chunk * 128:(chunk + 1) * 128], cqn[k], cqn[k].v(BF, GS), k == 0, k == 1)
                            return b
                        s_ = nstg()
                        sv_ = s_.v(BF, 4, GS)
                        for h in range(4):
                            b = uq(h)
                            P.copy("act", s_, sv_[:, h, :], b, b.ap[:, 0:GS])
                        P.store("sp", [s_], qnT.rearrange("(c p) s -> p c s", p=128)[:, :, gs], sv_)
                        s_ = nstg()
                        sv_ = s_.v(BF, 2, GS)
                        for pr in range(2):
                            ba = uq(4 + pr)
                            bs = uq(6 + pr)
                            rope(ba, bs, s_, sv_[:, pr, :])
                        P.store("sp", [s_], qrT.rearrange("(c p) s -> p c s", p=128)[:, :, gs], sv_)
                        s_ = nstg()
                        sv_ = s_.v(BF, 4, GS)
                        for h in range(4):
                            b = P.bank()
                            P.mm(b, b.ap[:, 0:GS], wukv_t, wukv_v[:, h * 128:(h + 1) * 128], ckvn, ckvn.v(BF, GS), True, True)
                            P.copy("act", s_, sv_[:, h, :], b, b.ap[:, 0:GS])
                        P.store("sp", [s_], knT.rearrange("(c p) s -> p c s", p=128)[:, :, gs], sv_)
                        s_ = nstg()
                        sv_ = s_.v(BF, TT, 512)
                        for tt_ in range(TT):
                            b = P.bank()
                            P.mm(b, b.ap[:, :], ckvn, ckvn.v(BF, GS)[:, tt_ * 128:(tt_ + 1) * 128], wukv_t, wukv_v[:, 512:1024], True, True)
                            P.copy("dve", s_, sv_[:, tt_, :], b, b.ap[:, :])
                        P.store("sp", [s_], vm[gs, :].rearrange("(t p) c -> p t c", p=128), sv_)
                        yield

            run_lanes(body(lanes[0], range(0, NGS, 2)), body(lanes[1], range(1, NGS, 2)))
            P.banks = saved_banks
            P.barrier()

        def phase_B(l):
            P.top = base_top
            even = (l % 2 == 0)
            j = l // 2
            last = (l == NL - 1)
            lanes = [alloc_lane("a"), alloc_lane("b")]
            W = load_w("wgu", wgu_in[2 * l + 1], 8, 2 * DFF) + load_w("wdn", wdn_in[2 * l + 1], 11, D)
            wo_t, wo_v = load_w("wo", (woute_in if even else wouto_in)[j], 8, D)
            wg_t, wg_v = load_w("pleg", pleg_in[l], 8, D)
            wp_t, wp_v = load_w("plep", plep_in[l], 2, D)
            for li, L in enumerate(lanes):
                L["rsb"] = P.banks[li]
                L["ot"] = [P.alloc("ot%d_%d" % (li, c), GS * 2) for c in range(8)]
                P.new_dg(L["ot"])
                L["pin"] = P.alloc("pin%d" % li, TT * 256 * 4)
                P.new_dg([L["pin"]])
                L["pT"] = [P.alloc("pT%d_%d" % (li, c), GS * 2) for c in range(2)]
                L["sig"] = [P.alloc("sig%d_%d" % (li, c), GS * 4) for c in range(2)]
                L["pt2"] = [P.alloc("pt2%d_%d" % (li, c), GS * 4) for c in range(2)]
                if last:
                    L["yt"] = [P.alloc("yt%d_%d" % (li, c), 4096) for c in range(2)]
                    for t in L["yt"]:
                        P.new_dg([t])
            saved_banks = P.banks
            P.banks = saved_banks[2:]
            P.bank_i = 0
            ocd = oT.rearrange("(c p) s -> p c s", p=128)

            def body(L, groups):
                xct, xcv, xn, xnv, actt, actv = L["xct"], L["xcv"], L["xn"], L["xnv"], L["actt"], L["actv"]
                rsb, ot, pin, pT, sig, pt2 = L["rsb"], L["ot"], L["pin"], L["pT"], L["sig"], L["pt2"]
                for g in groups:
                    gs = slice(g * GS, (g + 1) * GS)
                    P.load("sp", xct, P.span(xct, F32, 8, GS), xcd[:, :, gs])
                    P.load("sp", ot, P.span(ot, BF, 8, GS), ocd[:, :, gs])
                    P.load("sp", [pin], pin.v(F32, TT, 256), p_in[l, gs, :].rearrange("(t p) c -> p t c", p=128))
                    for m in range(8):
                        b = P.bank()
                        for k in range(8):
                            P.mm(b, b.ap[:, 0:GS], wo_t, wo_v[:, k, m * 128:(m + 1) * 128], ot[k], ot[k].v(BF, GS), k == 0, k == 7)
                        P.tt("dve", xct[m], xcv[m], b, b.ap[:, 0:GS], xct[m], xcv[m], ALU.add)
                    yield
                    yield from ffn_gen(L, W, gcol_layer(l, 2))
                    norm_xn(L, gcol_layer(l, 3))
                    for c in range(2):
                        b = P.bank()
                        for tt_ in range(TT):
                            P.op("pe", lambda e, b=b, tt_=tt_, c=c: e.transpose(b.ap[:, tt_ * 128:(tt_ + 1) * 128], pin.v(F32, TT, 256)[:, tt_, c * 128:(c + 1) * 128], ident),
                                 reads=[pin, cst], writes=[b])
                        P.copy("act", pT[c], pT[c].v(BF, GS), b, b.ap[:, 0:GS])
                    yield
                    for m in range(8):
                        bg = P.bank()
                        bp = P.bank()
                        for k in range(8):
                            P.mm(bg, bg.ap[:, 0:GS], wg_t, wg_v[:, k, m * 128:(m + 1) * 128], xn[k], xnv[k], k == 0, k == 7)
                        for k in range(2):
                            P.mm(bp, bp.ap[:, 0:GS], wp_t, wp_v[:, k, m * 128:(m + 1) * 128], pT[k], pT[k].v(BF, GS), k == 0, k == 1)
                        s_ = sig[m % 2]
                        t2 = pt2[m % 2]
                        P.act(s_, s_.v(F32, GS), bg, bg.ap[:, 0:GS], AF.Sigmoid)
                        P.tt("dve", t2, t2.v(F32, GS), s_, s_.v(F32, GS), bp, bp.ap[:, 0:GS], ALU.mult)
                        P.tt("pool", xct[m], xcv[m], xct[m], xcv[m], t2, t2.v(F32, GS), ALU.add)
                    if not last:
                        P.store("sp", xct, xcd[:, :, gs], P.span(xct, F32, 8, GS))
                        yield
                    else:
                        yield
                        norm_stats(xct, xcv, 8, D, [(actt[c], actv[c]) for c in range(8)], rsb, L["tmp"], W=GS)
                        for c in range(8):
                            P.stt(xct[c], xcv[c], xct[c], xcv[c], gcol(GC_FIN + c), rsb, rsb.ap[:, 0:GS], ALU.mult, ALU.mult, extra_reads=[gains])
                        yield
                        for tt_ in range(TT):
                            y_ = L["yt"][tt_ % 2]
                            yv = y_.v(F32, 1024)
                            for half in range(2):
                                b = P.bank()
                                for c4 in range(4):
                                    c = half * 4 + c4
                                    P.op("pe", lambda e, b=b, c4=c4, c=c, tt_=tt_: e.transpose(b.ap[:, c4 * 128:(c4 + 1) * 128], xcv[c][:, tt_ * 128:(tt_ + 1) * 128], ident),
                                         reads=[xct[c], cst], writes=[b])
                                P.copy("act" if half else "dve", y_, yv[:, half * 512:(half + 1) * 512], b, b.ap[:, :])
                            P.store("sp", [y_], y_out[g * GS + tt_ * 128: g * GS + (tt_ + 1) * 128, :], yv)
                        yield

            run_lanes(body(lanes[0], range(0, NGS, 2)), body(lanes[1], range(1, NGS, 2)))
            P.banks = saved_banks
            P.barrier()

        def phase_SB():
            P.top = base_top
            KT = [P.alloc("KT%d" % i, S * 2) for i in range(2)]
            VV = [P.alloc("VV%d" % i, S * 2) for i in range(2)]
            for t in KT + VV:
                P.new_dg([t])
            qt = [P.alloc("qt%d" % i, 1024) for i in range(3)]
            for t in qt:
                P.new_dg([t])
            et = [P.alloc("et%d" % i, 4096) for i in range(2)]
            lpt = [P.alloc("lp%d" % i, 2048) for i in range(3)]
            lp2 = [P.alloc("lp2%d" % i, 2048) for i in range(2)]
            At = [P.alloc("At%d" % i, 2048) for i in range(3)]
            ost = [P.alloc("ost%d" % i, 1024) for i in range(2)]
            for t in ost:
                P.new_dg([t])
            Dk = P.banks[6]
            O = P.banks[7]
            items = []
            gi = 0
            for h in range(8):
                for g in range(NG):
                    n = 4 * g + 4
                    cs = list(range(n - 1, -1, -1))
                    for pi in range(n // 2):
                        hv = []
                        for c in (cs[2 * pi], cs[2 * pi + 1]):
                            r = c - 4 * g
                            hv.append((c, r * 128 if r >= 0 else 0, r >= 0))
                        items.append(dict(h=h, g=g, hv=hv, full=(not hv[0][2] and not hv[1][2]), first=(pi == 0), last=(pi == n // 2 - 1), pi=pi, gi=gi))
                    gi += 1

            def bp(i):
                k = (i % 3) * 2
                return P.banks[k], P.banks[k + 1], P.psall[:, k * 512:(k + 2) * 512]

            def f_z(i, it):
                h, g = it["h"], it["g"]
                if it["first"]:
                    def ld_q(gg):
                        h_, g_ = gg // NG, gg % NG
                        P.load("sp", [qt[gg % 3]], qt[gg % 3].v(BF, 512), sqT[h_ * 128:(h_ + 1) * 128, g_ * 512:(g_ + 1) * 512])
                    if it["gi"] == 0:
                        P.load("sp", [KT[0]], KT[0].v(BF, S), skT[0:128, :])
                        P.load("sp", [VV[0]], VV[0].v(BF, NT, 128), sv[:, 0:128].rearrange("(n p) d -> p n d", p=128))
                        ld_q(0)
                    if it["gi"] + 1 < 8 * NG:
                        ld_q(it["gi"] + 1)
                if g == 1 and it["pi"] == 2 and h + 1 < 8:
                    P.load("sp", [KT[(h + 1) % 2]], KT[(h + 1) % 2].v(BF, S), skT[(h + 1) * 128:(h + 2) * 128, :])
                    P.load("sp", [VV[(h + 1) % 2]], VV[(h + 1) % 2].v(BF, NT, 128), sv[:, (h + 1) * 128:(h + 2) * 128].rearrange("(n p) d -> p n d", p=128))
                kt = KT[h % 2]
                q_ = qt[it["gi"] % 3]
                b0, b1, bw = bp(i)
                for hf, bk in ((0, b0), (1, b1)):
                    c, q0, _ = it["hv"][hf]
                    P.mm(bk, bk.ap[:, q0:512], kt, kt.v(BF, S)[:, c * 128:(c + 1) * 128], q_, q_.v(BF, 512)[:, q0:512], True, False)

            def f_exp(i, it):
                b0, b1, bw = bp(i)
                e_ = et[i % 2]
                if it["full"]:
                    P.op("act", lambda e: e.activation(out=e_.v(F32, 1024), in_=bw, func=AF.Exp), reads=[b0, b1], writes=[e_])
                else:
                    for hf, bk in ((0, b0), (1, b1)):
                        c, q0, _ = it["hv"][hf]
                        P.act(e_, e_.v(F32, 2, 512)[:, hf, q0:512], bk, bk.ap[:, q0:512], AF.Exp)

            def f_ln(i, it):
                e_ = et[i % 2]
                l_ = lpt[i % 3]
                if it["full"]:
                    P.act(l_, l_.v(BF, 1024), e_, e_.v(F32, 1024), AF.Ln, bias=1.0)
                else:
                    for hf in range(2):
                        c, q0, dg_ = it["hv"][hf]
                        P.act(l_, l_.v(BF, 2, 512)[:, hf, q0:512], e_, e_.v(F32, 2, 512)[:, hf, q0:512], AF.Ln, bias=1.0)
                        if dg_:
                            P.tt("pool", l_, l_.v(BF, 2, 512)[:, hf, q0:q0 + 128], l_, l_.v(BF, 2, 512)[:, hf, q0:q0 + 128], cbf, cbf_b[:, B_LT:B_LT + 128], ALU.mult)

            def f_stt(i, it):
                if it["first"]:
                    return
                l_ = lpt[i % 3]
                lv = l_.v(BF, 2, 512)
                l2 = lp2[i % 2]
                l2v = l2.v(BF, 2, 512)
                for hf in range(2):
                    q0 = it["hv"][hf][1]
                    P.stt(l2, l2v[:, hf, q0:512], Dk, Dk.ap[:, q0:512], cst_f[:, C_E127:C_E127 + 1], l_, lv[:, hf, q0:512],
                          ALU.mult, ALU.add, extra_reads=[cst])

            def f_dtri(i, it):
                b0, b1, bw = bp(i)
                l_ = lpt[i % 3]
                lv = l_.v(BF, 2, 512)
                (c0, q00, _), (c1, q01, _) = it["hv"]
                if it["first"]:
                    P.mm(Dk, Dk.ap[:, :], cbf, cbf_b[:, B_NEG:B_NEG + 128], cbf, cbf_b[:, B_ZERO:B_ZERO + 512], True, False)
                    src, sv_ = l_, lv
                else:
                    src, sv_ = lp2[i % 2], lp2[i % 2].v(BF, 2, 512)
                if not it["last"]:
                    P.mm(Dk, Dk.ap[:, q00:512], cbf, cbf_b[:, B_NEG:B_NEG + 128], l_, lv[:, 0, q00:512], False, False)
                    P.mm(Dk, Dk.ap[:, q01:512], cbf, cbf_b[:, B_NEG:B_NEG + 128], l_, lv[:, 1, q01:512], False, False)
                P.mm(b0, b0.ap[:, q00:512], cbf, cbf_b[:, B_NGE:B_NGE + 128], src, sv_[:, 0, q00:512], False, True)
                P.mm(b1, b1.ap[:, q01:512], cbf, cbf_b[:, B_NGE:B_NGE + 128], src, sv_[:, 1, q01:512], False, False)
                P.mm(b1, b1.ap[:, q00:512], cbf, cbf_b[:, B_NEG:B_NEG + 128], l_, lv[:, 0, q00:512], False, True)

            def f_expA(i, it):
                b0, b1, bw = bp(i)
                a_ = At[i % 3]
                av = a_.v(BF, 2, 512)
                if it["full"]:
                    P.op("act", lambda e: e.activation(out=a_.v(BF, 1024), in_=bw, func=AF.Exp), reads=[b0, b1], writes=[a_])
                else:
                    for hf, bk in ((0, b0), (1, b1)):
                        c, q0, dg_ = it["hv"][hf]
                        P.act(a_, av[:, hf, q0:512], bk, bk.ap[:, q0:512], AF.Exp)
                        if dg_:
                            P.tt("pool", a_, av[:, hf, q0:q0 + 128], a_, av[:, hf, q0:q0 + 128], cbf, cbf_b[:, B_LT:B_LT + 128], ALU.mult)

            def f_pv(i, it):
                h, g = it["h"], it["g"]
                a_ = At[i % 3]
                av = a_.v(BF, 2, 512)
                vt = VV[h % 2]
                if it["first"]:
                    P.mm(O, O.ap[:, :], cbf, cbf_b[:, B_NEG:B_NEG + 128], cbf, cbf_b[:, B_ZERO:B_ZERO + 512], True, False)
                for hf in range(2):
                    c, q0, _ = it["hv"][hf]
                    P.mm(O, O.ap[:, q0:512], vt, vt.v(BF, NT, 128)[:, c, :], a_, av[:, hf, q0:512], False, it["last"] and hf == 1)
                if it["last"]:
                    o_ = ost[it["gi"] % 2]
                    P.copy("dve", o_, o_.v(BF, 512), O, O.ap[:, :])
                    P.store("sp", [o_], oT[h * 128:(h + 1) * 128, g * 512:(g + 1) * 512], o_.v(BF, 512))

            n = len(items)

            def at(k, f):
                if 0 <= k < n:
                    f(k, items[k])

            for t in range(n + 4):
                at(t - 3, f_expA)
                at(t - 4, f_pv)
                at(t - 2, f_stt)
                at(t - 1, f_ln)
                at(t, f_z)
                at(t - 2, f_dtri)
                at(t, f_exp)
            P.barrier()

        def phase_MLA(j):
            P.top = base_top
            KN = [P.alloc("KN%d" % i, S * 2) for i in range(2)]
            VV = [P.alloc("VM%d" % i, S * 2) for i in range(2)]
            KR = P.alloc("KR", S * 2)
            for t in KN + VV + [KR]:
                P.new_dg([t])
            qn = [P.alloc("qn%d" % i, 1024) for i in range(3)]
            qr = [P.alloc("qr%d" % i, 1024) for i in range(3)]
            for i in range(3):
                P.new_dg([qn[i], qr[i]])
            Pt = [P.alloc("Pt%d" % i, 1024) for i in range(3)]
            rec = P.alloc("rec", 2048)
            ost = [P.alloc("ost%d" % i, 1024) for i in range(2)]
            for t in ost:
                P.new_dg([t])
            Bb = P.banks[0:2]
            Db = P.banks[2:4]
            Ob = P.banks[4:6]
            saved_banks = P.banks
            P.banks = saved_banks[6:8]
            P.bank_i = 0
            gla = gla_gen(j)
            scale = float((128 + 64) ** -0.5)
            items = []
            gi = 0
            for h in range(4):
                for g in range(NG):
                    n = 4 * g + 4
                    for c in range(n):
                        r = c - 4 * g
                        q0 = r * 128 if r >= 0 else 0
                        items.append(dict(h=h, g=g, c=c, q0=q0, diag=(r >= 0), first=(c == 0), last=(c == n - 1), gi=gi))
                    gi += 1
            P.load("sp", [KR], KR.v(BF, S), krT[:, :])

            def s1(i, it):
                h, g, c, q0 = it["h"], it["g"], it["c"], it["q0"]
                hb = 64 * (h % 2)
                if it["first"]:
                    def ld_kv(hh):
                        P.load("sp", [KN[hh % 2]], KN[hh % 2].v(BF, S), knT[hh * 128:(hh + 1) * 128, :])
                        P.load("sp", [VV[hh % 2]], VV[hh % 2].v(BF, NT, 128), vm[:, hh * 128:(hh + 1) * 128].rearrange("(n p) d -> p n d", p=128))

                    def ld_q(gg):
                        h_, g_ = gg // NG, gg % NG
                        k_ = gg % 3
                        P.load("sp", [qn[k_], qr[k_]], qn[k_].v(BF, 512), qnT[h_ * 128:(h_ + 1) * 128, g_ * 512:(g_ + 1) * 512])
                        P.load("sp", [qn[k_], qr[k_]], qr[k_].v(BF, 512), qrT[(h_ // 2) * 128:(h_ // 2 + 1) * 128, g_ * 512:(g_ + 1) * 512])
                    if it["gi"] == 0:
                        ld_kv(0)
                        ld_q(0)
                    if it["gi"] + 1 < 4 * NG:
                        ld_q(it["gi"] + 1)
                if g == 0 and c == 2 and h + 1 < 4:
                    P.load("sp", [KN[(h + 1) % 2]], KN[(h + 1) % 2].v(BF, S), knT[(h + 1) * 128:(h + 2) * 128, :])
                    P.load("sp", [VV[(h + 1) % 2]], VV[(h + 1) % 2].v(BF, NT, 128), vm[:, (h + 1) * 128:(h + 2) * 128].rearrange("(n p) d -> p n d", p=128))
                k3 = it["gi"] % 3
                kt = KN[h % 2]
                B = Bb[i % 2]
                P.mm(B, B.ap[:, q0:512], kt, kt.v(BF, S)[:, c * 128:(c + 1) * 128], qn[k3], qn[k3].v(BF, 512)[:, q0:512], True, False)
                P.mm(B, B.ap[:, q0:512], KR, KR.v(BF, S)[hb:hb + 64, c * 128:(c + 1) * 128], qr[k3], qr[k3].v(BF, 512)[hb:hb + 64, q0:512], False, True)
                p_ = Pt[i % 3]
                P.act(p_, p_.v(BF, 512)[:, q0:512], B, B.ap[:, q0:512], AF.Exp, scale=scale)
                if it["diag"]:
                    P.tt("pool", p_, p_.v(BF, 512)[:, q0:q0 + 128], p_, p_.v(BF, 512)[:, q0:q0 + 128], cbf, cbf_b[:, B_LE:B_LE + 128], ALU.mult)

            def s2(i, it):
                h, g, c, q0 = it["h"], it["g"], it["c"], it["q0"]
                p_ = Pt[i % 3]
                vt = VV[h % 2]
                O = Ob[it["gi"] % 2]
                Dn = Db[it["gi"] % 2]
                P.mm(O, O.ap[:, q0:512], vt, vt.v(BF, NT, 128)[:, c, :], p_, p_.v(BF, 512)[:, q0:512], it["first"], it["last"])
                P.mm(Dn, Dn.ap[:, q0:512], cbf, cbf_b[:, B_ONE:B_ONE + 128], p_, p_.v(BF, 512)[:, q0:512], it["first"], it["last"])
                if it["last"]:
                    P.op("dve", lambda e: e.reciprocal(out=rec.v(F32, 512), in_=Dn.ap[:, :]), reads=[Dn], writes=[rec])
                    o_ = ost[it["gi"] % 2]
                    P.tt("dve", o_, o_.v(BF, 512), O, O.ap[:, :], rec, rec.v(F32, 512), ALU.mult)
                    P.store("sp", [o_], oT[h * 128:(h + 1) * 128, g * 512:(g + 1) * 512], o_.v(BF, 512))

            n = len(items)
            for t in range(n + 1):
                if t < n:
                    s1(t, items[t])
                if 0 <= t - 1 < n:
                    s2(t - 1, items[t - 1])
                next(gla, None)
            for _ in gla:
                pass
            P.banks = saved_banks
            P.barrier()

        def gla_gen(j):
            GE = GC_EV + 7 * j
            wg2 = P.alloc("wg2", 512)
            P.new_dg([wg2])
            P.load("pool", [wg2], wg2.v(BF, 256)[0:32, :], wg2_in[j])
            NB = 2
            ga = [P.alloc("ga%d" % i, 256) for i in range(NB)]
            gq = [P.alloc("gq%d" % i, 1024) for i in range(NB)]
            gk = [P.alloc("gk%d" % i, 1024) for i in range(NB)]
            gkt_ = [P.alloc("gkt%d" % i, 1024) for i in range(NB)]
            gv_ = [P.alloc("gv%d" % i, 1024) for i in range(NB)]
            sgr = [P.alloc("sgr%d" % i, 1024) for i in range(NB)]
            for i in range(NB):
                P.new_dg([ga[i], gq[i], gk[i], gkt_[i], gv_[i], sgr[i]])
            e1 = P.alloc("e1", 1024)
            lsp = P.alloc("lsp", 1024)
            ebt = P.alloc("ebt", 1024)
            enbt = P.alloc("enbt", 1024)
            edt = P.alloc("edt", 1024)
            qtl = P.alloc("qtl", 512)
            ktl = P.alloc("ktl", 512)
            kb = P.alloc("kb", 512)
            attm = P.alloc("attm", 1024)
            Sst = [P.alloc("Sst%d" % i, 1024) for i in range(2)]
            Sz = [P.alloc("Sz%d" % i, 512) for i in range(2)]
            of = P.alloc("of", 2048)
            osq = P.alloc("osq", 1024)
            ntmp = P.alloc("gtmp", 2048)
            on = P.alloc("on", 2048)
            ost = [P.alloc("gost%d" % i, 1024) for i in range(2)]
            for t in ost:
                P.new_dg([t])
            for i in range(2):
                P.memset("dve", Sst[i], Sst[i].v(F32, 256), 0.0)
                P.memset("dve", Sz[i], Sz[i].v(BF, 256), 0.0)
            U_f = cst_f[:, C_U:C_U + 128]
            U2_f = cst_f[:, C_U2:C_U2 + 128]
            bm_f = cst_f[:, C_BM:C_BM + 256]

            def loads(n):
                i = n % NB
                sl = slice(n * 128, (n + 1) * 128)
                grp_t = [ga[i], gq[i], gk[i], gkt_[i], gv_[i], sgr[i]]
                P.load("sp", grp_t, ga[i].v(BF, 128)[0:32, :], gaT[:, sl])
                P.load("sp", grp_t, gq[i].v(F32, 2, 128), gqT.rearrange("(c p) s -> p c s", p=128)[:, :, sl])
                P.load("sp", grp_t, gk[i].v(F32, 2, 128), gkT.rearrange("(c p) s -> p c s", p=128)[:, :, sl])
                P.load("sp", grp_t, gkt_[i].v(F32, 256), gkt[sl, :])
                P.load("sp", grp_t, gv_[i].v(BF, 512), gv[sl, :])
                P.load("sp", grp_t, sgr[i].v(BF, 4, 128), sgrT.rearrange("(c p) s -> p c s", p=128)[:, :, sl])

            loads(0)
            for n in range(NT):
                i = n % NB
                if n + 1 < NT:
                    loads(n + 1)
                bG = P.bank()
                P.mm(bG, bG.ap[:, 0:256], ga[i], ga[i].v(BF, 128)[0:32, :], wg2, wg2.v(BF, 256)[0:32, :], True, True)
                yield
                P.act(e1, e1.v(F32, 256), bG, bG.ap[:, 0:256], AF.Exp, scale=-1.0)
                yield
                P.act(lsp, lsp.v(F32, 256), e1, e1.v(F32, 256), AF.Ln, bias=1.0)
                yield
                bD = P.bank()
                P.mm(bD, bD.ap[:, 0:256], cst, U2_f, lsp, lsp.v(F32, 256), True, True)
                bT = P.bank()
                for c in range(2):
                    P.mm(bT, bT.ap[:, c * 128:(c + 1) * 128], lsp, lsp.v(F32, 256)[:, c * 128:(c + 1) * 128], cst, U_f, True, True)
                yield
                P.act(ebt, ebt.v(F32, 256), bT, bT.ap[:, 0:256], AF.Exp)
                P.act(enbt, enbt.v(F32, 256), bT, bT.ap[:, 0:256], AF.Exp, scale=-1.0)
                P.act(edt, edt.v(F32, 256), bD, bD.ap[:, 0:256], AF.Exp)
                yield
                P.tt("dve", qtl, qtl.v(BF, 256), gq[i], gq[i].v(F32, 256), ebt, ebt.v(F32, 256), ALU.mult)
                P.tt("dve", ktl, ktl.v(BF, 256), gk[i], gk[i].v(F32, 256), enbt, enbt.v(F32, 256), ALU.mult)
                P.tt("dve", kb, kb.v(BF, 256), gkt_[i], gkt_[i].v(F32, 256), edt, edt.v(F32, 256), ALU.mult)
                yield
                ba = P.bank()
                bb = P.bank()
                for h in range(4):
                    pr, hb = h // 2, 64 * (h % 2)
                    bk = ba if h % 2 == 0 else bb
                    P.mm(bk, bk.ap[:, pr * 128:(pr + 1) * 128], ktl, ktl.v(BF, 2, 128)[hb:hb + 64, pr, :], qtl, qtl.v(BF, 2, 128)[hb:hb + 64, pr, :], True, True)
                yield
                for h in range(4):
                    pr = h // 2
                    bk = ba if h % 2 == 0 else bb
                    P.tt("dve", attm, attm.v(BF, 4, 128)[:, h, :], bk, bk.ap[:, pr * 128:(pr + 1) * 128], cbf, cbf_b[:, B_LE:B_LE + 128], ALU.mult)
                yield
                bO = P.bank()
                for h in range(4):
                    pr = h // 2
                    P.mm(bO, bO.ap[:, h * 128:(h + 1) * 128], gv_[i], gv_[i].v(BF, 512)[:, h * 128:(h + 1) * 128], attm, attm.v(BF, 4, 128)[:, h, :], True, False)
                    P.mm(bO, bO.ap[:, h * 128:(h + 1) * 128], Sz[pr], Sz[pr].v(BF, 256)[:, (h % 2) * 128:(h % 2 + 1) * 128], qtl, qtl.v(BF, 2, 128)[:, pr, :], False, True)
                P.copy("act", of, of.v(F32, 512), bO, bO.ap[:, :])
                yield
                for pr in range(2):
                    bS = P.bank()
                    P.mm(bS, bS.ap[:, 0:256], kb, kb.v(BF, 256)[:, pr * 128:(pr + 1) * 128], gv_[i], gv_[i].v(BF, 512)[:, pr * 256:(pr + 1) * 256], True, True)
                    P.stt(Sst[pr], Sst[pr].v(F32, 256), Sst[pr], Sst[pr].v(F32, 256), ebt.v(F32, 2, 128)[:, pr, 127:128], bS, bS.ap[:, 0:256], ALU.mult, ALU.add, extra_reads=[ebt])
                    P.tt("dve", Sz[pr], Sz[pr].v(BF, 256), Sst[pr], Sst[pr].v(F32, 256), cst, bm_f, ALU.mult)
                yield
                brs = P.bank()
                norm_stats([of], [of.v(F32, 512)], 1, 128, [(osq, osq.v(BF, 512))], brs, (ntmp, ntmp.v(F32, 512)))
                yield
                o_ = ost[n % 2]
                for h in range(4):
                    P.stt(on, on.v(F32, 4, 128)[:, h, :], of, of.v(F32, 4, 128)[:, h, :], gcol(GE + 3 + h), brs, brs.ap[:, h * 128:(h + 1) * 128], ALU.mult, ALU.mult, extra_reads=[gains])
                P.tt("dve", o_, o_.v(BF, 512), on, on.v(F32, 512), sgr[i], sgr[i].v(BF, 512), ALU.mult)
                P.store("sp", [o_], oT.rearrange("(c p) s -> p c s", p=128)[:, 4:8, n * 128:(n + 1) * 128], o_.v(BF, 4, 128))
                yield

        seq = []
        seq.append(("PRO", prologue))
        for l in range(NL):
            seq.append(("A%d" % l, lambda l=l: phase_A(l)))
            if l % 2 == 0:
                seq.append(("GLA%d" % l, lambda l=l: phase_MLA(l // 2)))
            else:
                seq.append(("SB%d" % l, phase_SB))
            seq.append(("B%d" % l, lambda l=l: phase_B(l)))
        for name, fn in seq:
            fn()
            P.recycle()
            if stop is not None and name == stop:
                break
        P.barrier()
        P.emit()
    return nc


def kc_layout(w):
    K, N = w.shape
    return np.ascontiguousarray(w.reshape(K // 128, 128, N).transpose(1, 0, 2))


def col_layout(v):
    return np.ascontiguousarray(v.reshape(-1, 128).T)


def prep_shared(inp, NL):
    NE = (NL + 1) // 2
    NO = NL // 2
    f = lambda a: np.asarray(a, np.float32)
    sh = {}
    sh["consts"] = make_consts()
    gl = []
    for l in range(NL):
        for nm in ("ffn1_norm", "mix_norm", "ffn2_norm", "ple_norm"):
            gl.append(col_layout(f(inp[nm][l])))
    gl.append(col_layout(f(inp["final_norm"])))
    for j in range(NE):
        gl.append(col_layout(f(inp["ev_q_norm"][j])))
        gl.append(col_layout(f(inp["ev_kv_norm"][j])))
        gl.append(np.ascontiguousarray(f(inp["ev_gla_norm"][j]).T))
    sh["gains"] = np.ascontiguousarray(np.concatenate(gl, axis=1))
    wgu, wdn = [], []
    for l in range(NL):
        for a, b in (("ffn1_w_gu", "ffn1_w_down"), ("ffn2_w_gu", "ffn2_w_down")):
            wgu.append(kc_layout(f(inp[a][l])))
            wdn.append(kc_layout(f(inp[b][l])))
    sh["wgu"] = np.stack(wgu)
    sh["wdn"] = np.stack(wdn)
    sh["pleg"] = np.stack([kc_layout(f(inp["ple_w_gate"][l])) for l in range(NL)])
    sh["plep"] = np.stack([kc_layout(f(inp["ple_w_proj"][l])) for l in range(NL)])
    win, wuq, wukv, wg2, woute = [], [], [], [], []
    for j in range(NE):
        w = f(inp["ev_w_in"][j])
        cq, ckv, kr = w[:, 0:256], w[:, 256:384], w[:, 384:448]
        gq, gk, gvv, ga, gr = w[:, 448:704], w[:, 704:960], w[:, 960:1472], w[:, 1472:1488], w[:, 1488:2000]
        krs = np.concatenate([kr[:, 32:64], kr[:, 0:32]], axis=1)
        ext = np.concatenate([cq, ckv, kr, kr, krs, krs, gq, gk, gr, ga, gk, gvv], axis=1)
        assert ext.shape[1] == 2448
        win.append(kc_layout(ext))
        u = f(inp["ev_w_uq"][j]).reshape(256, 4, 192)
        nope = [u[:, h, 0:128] for h in range(4)]
        rope = [u[:, h, 128:192] for h in range(4)]
        rsw = [np.concatenate([r[:, 32:64], r[:, 0:32]], axis=1) for r in rope]
        wuq.append(kc_layout(np.concatenate(nope + rope + rsw, axis=1)))
        kv = f(inp["ev_w_ukv"][j]).reshape(128, 4, 256)
        wukv.append(np.ascontiguousarray(np.concatenate([kv[:, h, 0:128] for h in range(4)] + [kv[:, h, 128:256] for h in range(4)], axis=1)))
        g2 = np.zeros((32, 256), np.float32)
        g2[0:16] = f(inp["ev_w_gate2"][j])
        g2[16] = f(inp["ev_b_gate"][j])
        wg2.append(g2)
        woute.append(kc_layout(f(inp["ev_w_out"][j])))
    sh["win"] = np.stack(win)
    sh["wuq"] = np.stack(wuq)
    sh["wukv"] = np.stack(wukv)
    sh["wg2"] = np.stack(wg2)
    sh["woute"] = np.stack(woute)
    if NO > 0:
        sh["wqkv"] = np.stack([kc_layout(f(inp["od_w_qkv"][j])) for j in range(NO)])
        sh["wouto"] = np.stack([kc_layout(f(inp["od_w_out"][j])) for j in range(NO)])
    return sh


def run(inp, S, NL, dbg=False, stop=None, ncores=NCORES):
    nc = build(S, NL, dbg=dbg, stop=stop)
    sh = prep_shared(inp, NL)
    x = np.asarray(inp["x"], np.float32)
    p = np.asarray(inp["p"], np.float32)
    pos = np.asarray(inp["positions"], np.int32)
    in_maps = []
    for c in range(ncores):
        m = dict(sh)
        m["x"] = np.ascontiguousarray(x[c, :S])
        m["p"] = np.ascontiguousarray(p[:NL, c, :S])
        m["pos"] = np.ascontiguousarray(pos[c:c + 1, :S])
        in_maps.append(m)
    res = run_bass_kernel_spmd(nc, in_maps, core_ids=list(range(ncores)))
    return res.results


def kernel(**inputs):
    x = np.asarray(inputs["x"])
    B, S, _ = x.shape
    NL = int(np.asarray(inputs["ffn1_norm"]).shape[0])
    res = run(inputs, S, NL)
    return np.stack([np.asarray(r["y"], np.float32) for r in res], axis=0)
```
